# Optimizing a Trainium2 kernel written in Bass

```python
import math
import jax, jax.numpy as jnp
from jax import lax
import numpy as np

D_MODEL = 1024
BATCH = 32
SEQ = 256
DEPTH = 2
DEC_BATCH = 8
DEC_SEQ = 1024
PAST_LEN = 256

GRID_W = 64
HEAD_DIM = 64
BLK = 128
A_HEADS = 4
A_KV = 2
B_HEADS = 4
B_HALF = HEAD_DIM // 2
C_HEADS = 4
C_KV = 2
C_WINDOW = 128
D_HEADS = 4
NA_ROWS = 8
NA_COLS = 16
N_BRANCH = 4
BRANCH_W = 4 * HEAD_DIM
D_FF = -(-8 * D_MODEL // (3 * 256)) * 256
ROPE_THETA = 10000.0
NORM_EPS = 1e-6
SUBLN_EPS = 1e-5
NEG_INF = -1e30
IN_SPLITS = (A_HEADS * HEAD_DIM, A_KV * HEAD_DIM, A_KV * HEAD_DIM,
             B_HEADS * HEAD_DIM, B_HEADS * HEAD_DIM, B_HEADS * HEAD_DIM,
             C_HEADS * HEAD_DIM, C_KV * HEAD_DIM, C_KV * HEAD_DIM,
             D_HEADS * HEAD_DIM, D_HEADS * HEAD_DIM, D_HEADS * HEAD_DIM,
             N_BRANCH * D_MODEL)
IN_COLS = sum(IN_SPLITS)

kernel_name = 'hybrid_dit_prefix_step'


def rms_norm(x, g, eps=NORM_EPS):
    xf = x.astype(jnp.float32)
    y = xf * lax.rsqrt(jnp.mean(xf * xf, axis=-1, keepdims=True) + eps)
    return (y * g.astype(jnp.float32)).astype(x.dtype)


def axial_rope(x):
    t_len, d = x.shape[-2], x.shape[-1]
    quarter = d // 4
    inv = jnp.power(jnp.float32(ROPE_THETA), -jnp.arange(quarter, dtype=jnp.float32) / quarter)
    t = jnp.arange(t_len)
    ang_r = (t // GRID_W).astype(jnp.float32)[:, None] * inv
    ang_c = (t % GRID_W).astype(jnp.float32)[:, None] * inv

    def rot(xp, ang):
        cos = jnp.cos(ang).astype(x.dtype)
        sin = jnp.sin(ang).astype(x.dtype)
        x1, x2 = jnp.split(xp, 2, axis=-1)
        return jnp.concatenate([x1 * cos - x2 * sin, x1 * sin + x2 * cos], axis=-1)

    xr, xc = jnp.split(x, 2, axis=-1)
    return jnp.concatenate([rot(xr, ang_r), rot(xc, ang_c)], axis=-1)


def to_heads(x, n):
    b, t, _ = x.shape
    return x.reshape(b, t, n, -1).transpose(0, 2, 1, 3)


def from_heads(x):
    b, h, t, d = x.shape
    return x.transpose(0, 2, 1, 3).reshape(b, t, h * d)


def group_q(q, n_kv):
    b, h, t, d = q.shape
    return q.reshape(b, n_kv, h // n_kv, t, d)


def blocked_attention(q, k, v, sink=None):
    b, hk, g, t, d = q.shape
    nb = t // BLK
    scale = d ** -0.5
    qb = jnp.moveaxis(q.reshape(b, hk, g, nb, BLK, d), 3, 0)

    def one_block(qi):
        s = jnp.einsum('bkgqd,bksd->bkgqs', qi, k).astype(jnp.float32) * scale
        if sink is not None:
            s_sink = jnp.broadcast_to(sink.astype(jnp.float32)[None, :, :, None, None], s.shape[:-1] + (1,))
            s = jnp.concatenate([s, s_sink], axis=-1)
        w = jax.nn.softmax(s, axis=-1)
        if sink is not None:
            w = w[..., :-1]
        return jnp.einsum('bkgqs,bksd->bkgqd', w.astype(v.dtype), v)

    o = lax.map(one_block, qb)
    return jnp.moveaxis(o, 0, 3).reshape(b, hk * g, t, d)


def diff_attention(q, k, v, lam):
    b, h, t, dq = q.shape
    half = dq // 2
    scale = half ** -0.5
    nb = t // BLK
    k1, k2 = k[..., :half], k[..., half:]
    qb = jnp.moveaxis(q.reshape(b, h, nb, BLK, dq), 2, 0)

    def one_block(qi):
        s1 = jnp.einsum('bhqd,bhsd->bhqs', qi[..., :half], k1).astype(jnp.float32) * scale
        s2 = jnp.einsum('bhqd,bhsd->bhqs', qi[..., half:], k2).astype(jnp.float32) * scale
        p = jax.nn.softmax(s1, axis=-1) - lam * jax.nn.softmax(s2, axis=-1)
        return jnp.einsum('bhqs,bhsd->bhqd', p.astype(v.dtype), v)

    o = lax.map(one_block, qb)
    return jnp.moveaxis(o, 0, 2).reshape(b, h, t, v.shape[-1])


def banded_attention(q, k, v, k_ctx, v_ctx, sink):
    b, hk, g, t, d = q.shape
    nb = t // BLK
    scale = d ** -0.5

    def windows(x):
        xp = jnp.pad(x, ((0, 0), (0, 0), (BLK, BLK), (0, 0))).reshape(b, hk, nb + 2, BLK, d)
        return jnp.concatenate([xp[:, :, :nb], xp[:, :, 1:nb + 1], xp[:, :, 2:]], axis=3)

    kw, vw = windows(k), windows(v)
    qb = q.reshape(b, hk, g, nb, BLK, d)
    s_band = jnp.einsum('bkgnqd,bknsd->bkgnqs', qb, kw).astype(jnp.float32) * scale
    blk = jnp.arange(nb)[:, None]
    qpos = blk * BLK + jnp.arange(BLK)[None, :]
    kpos = (blk - 1) * BLK + jnp.arange(3 * BLK)[None, :]
    valid = ((jnp.abs(qpos[:, :, None] - kpos[:, None, :]) <= C_WINDOW)
             & (kpos[:, None, :] >= 0) & (kpos[:, None, :] < t))
    s_band = jnp.where(valid, s_band, NEG_INF)
    s_ctx = jnp.einsum('bkgnqd,bksd->bkgnqs', qb, k_ctx).astype(jnp.float32) * scale
    s_sink = jnp.broadcast_to(sink.astype(jnp.float32)[None, :, :, None, None, None], s_ctx.shape[:-1] + (1,))
    w = jax.nn.softmax(jnp.concatenate([s_band, s_ctx, s_sink], axis=-1), axis=-1).astype(v.dtype)
    nw = 3 * BLK
    n_ctx = k_ctx.shape[2]
    o = (jnp.einsum('bkgnqs,bknsd->bkgnqd', w[..., :nw], vw)
         + jnp.einsum('bkgnqs,bksd->bkgnqd', w[..., nw:nw + n_ctx], v_ctx))
    return o.reshape(b, hk * g, t, d)


def neighbourhood_attention(q, k, v, k_ctx, v_ctx, rpb):
    b, h, t, d = q.shape
    rows = t // GRID_W
    kr = min(NA_ROWS, rows)
    scale = d ** -0.5
    qg = q.reshape(b, h, rows, GRID_W, d)
    kg = k.reshape(b, h, rows, GRID_W, d)
    vg = v.reshape(b, h, rows, GRID_W, d)
    r = jnp.arange(rows)
    row_idx = jnp.clip(r - kr // 2, 0, rows - kr)[:, None] + jnp.arange(kr)[None, :]
    kn = kg[:, :, row_idx]
    vn = vg[:, :, row_idx]
    s_nb = jnp.einsum('bhrcd,bhrjwd->bhrcjw', qg, kn).astype(jnp.float32) * scale
    col = jnp.arange(GRID_W)
    start_c = jnp.clip(col - NA_COLS // 2, 0, GRID_W - NA_COLS)
    col_valid = (col[None, :] >= start_c[:, None]) & (col[None, :] < start_c[:, None] + NA_COLS)
    dr = row_idx - r[:, None] + (NA_ROWS - 1)
    dc = jnp.clip(col[None, :] - col[:, None], -(NA_COLS - 1), NA_COLS - 1) + (NA_COLS - 1)
    bias = rpb[:, dr[:, None, :, None], dc[None, :, None, :]].astype(jnp.float32)
    s_nb = jnp.where(col_valid[:, None, :], s_nb + bias[None], NEG_INF)
    nk = kr * GRID_W
    s_nb = s_nb.reshape(b, h, rows, GRID_W, nk)
    s_ctx = jnp.einsum('bhrcd,bhsd->bhrcs', qg, k_ctx).astype(jnp.float32) * scale
    w = jax.nn.softmax(jnp.concatenate([s_nb, s_ctx], axis=-1), axis=-1).astype(v.dtype)
    w_nb = w[..., :nk].reshape(b, h, rows, GRID_W, kr, GRID_W)
    o = (jnp.einsum('bhrcjw,bhrjwd->bhrcd', w_nb, vn)
         + jnp.einsum('bhrcs,bhsd->bhrcd', w[..., nk:], v_ctx))
    return o.reshape(b, h, t, d)


def halves_rope(x):
    return jnp.concatenate([axial_rope(x[..., :B_HALF]), axial_rope(x[..., B_HALF:])], axis=-1)


def mixers(h, ctx_kv, lam_init, w_in, g_q_a, g_k_a, lam_b, g_subln_b, sink_c, rpb_d, w_branch, w_out):
    b, t, _ = h.shape
    points = np.cumsum(IN_SPLITS)[:-1].tolist()
    aq, ak, av, bq, bk, bv, cq, ck, cv, dq, dk, dv, gates = jnp.split(h @ w_in, points, axis=-1)
    qa = rms_norm(to_heads(aq, A_HEADS), g_q_a)
    ka = rms_norm(to_heads(ak, A_KV), g_k_a)
    va = to_heads(av, A_KV)
    qb, kb, vb = to_heads(bq, B_HEADS), to_heads(bk, B_HEADS), to_heads(bv, B_HEADS)
    qc, kc, vc = to_heads(cq, C_HEADS), to_heads(ck, C_KV), to_heads(cv, C_KV)
    qd, kd, vd = to_heads(dq, D_HEADS), to_heads(dk, D_HEADS), to_heads(dv, D_HEADS)
    lp = lam_b.astype(jnp.float32)
    lam = jnp.exp(jnp.sum(lp[0] * lp[1])) - jnp.exp(jnp.sum(lp[2] * lp[3])) + lam_init
    sink = sink_c.reshape(C_KV, C_HEADS // C_KV)
    if ctx_kv is None:
        oa = blocked_attention(group_q(qa, A_KV), ka, va)
        ob = diff_attention(qb, kb, vb, lam)
        oc = blocked_attention(group_q(qc, C_KV), kc, vc, sink)
        od = blocked_attention(group_q(qd, D_HEADS), kd, vd)
        new_kv = (ka, va, kb, vb, kc, vc, kd, vd)
    else:
        ka_x, va_x, kb_x, vb_x, kc_x, vc_x, kd_x, vd_x = ctx_kv
        oa = blocked_attention(group_q(axial_rope(qa), A_KV),
                               jnp.concatenate([ka_x, axial_rope(ka)], axis=2),
                               jnp.concatenate([va_x, va], axis=2))
        ob = diff_attention(halves_rope(qb),
                            jnp.concatenate([kb_x, halves_rope(kb)], axis=2),
                            jnp.concatenate([vb_x, vb], axis=2), lam)
        oc = banded_attention(group_q(axial_rope(qc), C_KV), axial_rope(kc), vc, kc_x, vc_x, sink)
        od = neighbourhood_attention(qd, kd, vd, kd_x, vd_x, rpb_d)
        new_kv = None
    ob = rms_norm(ob, g_subln_b, SUBLN_EPS) * (1.0 - lam_init)
    branches = jnp.stack([from_heads(oa), from_heads(ob), from_heads(oc), from_heads(od)], axis=2)
    proj = jnp.einsum('btkc,kcd->btkd', branches, w_branch)
    gate = jax.nn.sigmoid(gates.reshape(b, t, N_BRANCH, D_MODEL))
    merged = jnp.sum(gate * proj, axis=2)
    return merged @ w_out, new_kv


def trunk_layer(x, cond, ctx_kv, lam_init, w_ada, b_ada, g_norm1, w_in, g_q_a, g_k_a, lam_b, g_subln_b,
                sink_c, rpb_d, w_branch, w_out, g_norm2, w_ffn_in, w_ffn_out):
    sh1, sc1, gt1, sh2, sc2, gt2 = jnp.split(jax.nn.silu(cond) @ w_ada + b_ada, 6, axis=-1)
    h = rms_norm(x, g_norm1) * (1 + sc1) + sh1
    mix, new_kv = mixers(h, ctx_kv, lam_init, w_in, g_q_a, g_k_a, lam_b, g_subln_b, sink_c, rpb_d, w_branch, w_out)
    x = x + gt1 * mix
    h = rms_norm(x, g_norm2) * (1 + sc2) + sh2
    a, u = jnp.split(h @ w_ffn_in, 2, axis=-1)
    x = x + gt2 * ((jax.nn.silu(a) * u) @ w_ffn_out)
    return x, new_kv


def setup_inputs(seed: int = 0) -> dict:
    key = jax.random.key(seed)
    ks = iter(jax.random.split(key, 32))

    def nrm(shape, scale=1.0):
        return jax.random.normal(next(ks), shape, jnp.float32) * scale

    def gain(shape):
        return 1.0 + nrm(shape, 0.05)

    d = D_MODEL
    return {
        'x_prompt': nrm((BATCH, SEQ, d)),
        'x_sample': nrm((DEC_BATCH, DEC_SEQ, d)),
        'cache_a_k': nrm((DEC_BATCH, DEPTH, A_KV, PAST_LEN, HEAD_DIM)),
        'cache_a_v': nrm((DEC_BATCH, DEPTH, A_KV, PAST_LEN, HEAD_DIM)),
        'cache_b_k': nrm((DEC_BATCH, DEPTH, B_HEADS, PAST_LEN, HEAD_DIM)),
        'cache_b_v': nrm((DEC_BATCH, DEPTH, B_HEADS, PAST_LEN, HEAD_DIM)),
        'cache_c_k': nrm((DEC_BATCH, DEPTH, C_KV, PAST_LEN, HEAD_DIM)),
        'cache_c_v': nrm((DEC_BATCH, DEPTH, C_KV, PAST_LEN, HEAD_DIM)),
        'cache_d_k': nrm((DEC_BATCH, DEPTH, D_HEADS, PAST_LEN, HEAD_DIM)),
        'cache_d_v': nrm((DEC_BATCH, DEPTH, D_HEADS, PAST_LEN, HEAD_DIM)),
        'c': nrm((DEC_BATCH, d)),
        'c_ctx': nrm((d,)),
        'w_ada': nrm((DEPTH, d, 6 * d), 0.5 * d ** -0.5),
        'b_ada': nrm((DEPTH, 6 * d), 0.02),
        'g_norm1': gain((DEPTH, d)),
        'w_in': nrm((DEPTH, d, IN_COLS), d ** -0.5),
        'g_q_a': gain((DEPTH, HEAD_DIM)),
        'g_k_a': gain((DEPTH, HEAD_DIM)),
        'lam_b': nrm((DEPTH, 4, B_HALF), 0.1),
        'g_subln_b': gain((DEPTH, 2 * B_HALF)),
        'sink_c': nrm((DEPTH, C_HEADS), 0.5),
        'rpb_d': nrm((DEPTH, D_HEADS, 2 * NA_ROWS - 1, 2 * NA_COLS - 1), 0.2),
        'w_branch': nrm((DEPTH, N_BRANCH, BRANCH_W, d), BRANCH_W ** -0.5),
        'w_out': nrm((DEPTH, d, d), d ** -0.5),
        'g_norm2': gain((DEPTH, d)),
        'w_ffn_in': nrm((DEPTH, d, 2 * D_FF), d ** -0.5),
        'w_ffn_out': nrm((DEPTH, D_FF, d), D_FF ** -0.5),
        'g_final': gain((d,)),
    }


def reference(x_prompt, x_sample, cache_a_k, cache_a_v, cache_b_k, cache_b_v, cache_c_k, cache_c_v,
              cache_d_k, cache_d_v, c, c_ctx, w_ada, b_ada, g_norm1, w_in, g_q_a, g_k_a, lam_b, g_subln_b,
              sink_c, rpb_d, w_branch, w_out, g_norm2, w_ffn_in, w_ffn_out, g_final):
    xp = x_prompt
    xs = x_sample
    cond_lat = c[:, None, :]
    ctx_layers = []
    for l in range(DEPTH):
        lam_init = 0.8 - 0.6 * math.exp(-0.3 * l)
        lw = (w_ada[l], b_ada[l], g_norm1[l], w_in[l], g_q_a[l], g_k_a[l], lam_b[l], g_subln_b[l],
              sink_c[l], rpb_d[l], w_branch[l], w_out[l], g_norm2[l], w_ffn_in[l], w_ffn_out[l])
        xp, kv = trunk_layer(xp, c_ctx, None, lam_init, *lw)
        ctx_layers.append(kv)
        cached = (cache_a_k[:, l], cache_a_v[:, l], cache_b_k[:, l], cache_b_v[:, l],
                  cache_c_k[:, l], cache_c_v[:, l], cache_d_k[:, l], cache_d_v[:, l])
        xs, _ = trunk_layer(xs, cond_lat, cached, lam_init, *lw)
    y_prompt = rms_norm(xp, g_final)
    y_sample = rms_norm(xs, g_final)
    new = [jnp.stack([kv[i] for kv in ctx_layers], axis=1) for i in range(8)]
    return (y_prompt, y_sample, new[0], new[1], new[2], new[3], new[4], new[5], new[6], new[7])
```

```python
import os
import math
import types
from contextlib import ExitStack

import numpy as np
import concourse.bass as bass
import concourse.mybir as mybir
from concourse.bass_utils import run_bass_kernel_spmd

F32 = mybir.dt.float32
BF16 = mybir.dt.bfloat16
AF = mybir.ActivationFunctionType
ALU = mybir.AluOpType
AX = mybir.AxisListType
AP = bass.AP

N_CORES = 8
D = 1024
NT = 1024
DEPTH = 2
HD = 64
DFF = 2816
IN_COLS = 6656
NEG = -30000.0
SAME_ENGINE_SYNC = 'raw'
NSTEP = int(os.environ.get('NSTEP', '9'))

MIX = {
    "a": dict(q=0, k=256, v=384, kvh=2, vh=2, idx=0),
    "b": dict(q=512, k=768, v=1024, kvh=4, vh=4, idx=1),
    "c": dict(q=1280, k=1536, v=1664, kvh=2, vh=2, idx=2),
    "d": dict(q=1792, k=2048, v=2304, kvh=4, vh=4, idx=3),
}
GATE0 = 2560


class Res:
    __slots__ = ("name", "writer", "readers", "overlaps", "dsem", "dcount", "excl")

    def __init__(self, name, excl=False):
        self.name = name
        self.excl = excl
        self.writer = None
        self.readers = []
        self.overlaps = []
        self.dsem = None
        self.dcount = 0


class Eng:
    def __init__(self, name, in_order_safe=False):
        self.name = name
        self.sem = "eng_" + name
        self.seq = 0
        self.waited = {}
        self.ops = []
        self.in_order_safe = in_order_safe


class Prog:
    def __init__(self):
        self.engs = {
            "pe": Eng("pe", in_order_safe=True),
            "act": Eng("act"),
            "dve": Eng("dve"),
            "pool": Eng("pool"),
            "sp": Eng("sp"),
        }
        self.sem_names = set(e.sem for e in self.engs.values())
        self.n_ops = 0
        self.log = None

    def res(self, name, excl=False):
        return Res(name, excl)

    def _hazards(self, reads, writes):
        hz = []
        for r in reads:
            if r.writer is not None:
                hz.append(r.writer + (True,))
            if r.excl:
                hz.extend(t + (False,) for t in r.readers)
            for o in r.overlaps:
                if o.writer is not None:
                    hz.append(o.writer + (True,))
        for w in writes:
            if w.writer is not None:
                hz.append(w.writer + (False,))
            hz.extend(t + (False,) for t in w.readers)
            for o in w.overlaps:
                if o.writer is not None:
                    hz.append(o.writer + (False,))
                hz.extend(t + (False,) for t in o.readers)
        return hz

    def _waits_for(self, eng, hz):
        need = {}
        if SAME_ENGINE_SYNC == 'raw' and not eng.in_order_safe:
            own = [(v, r) for (s_, v, r) in hz if s_ == eng.sem]
            if any(r and eng.waited.get(eng.sem, 0) < v for (v, r) in own):
                hz = [h_ for h_ in hz if h_[0] != eng.sem] + [(eng.sem, max(v for v, _ in own), True)]
        for (sem, val, is_raw) in hz:
            if sem == eng.sem:
                if eng.in_order_safe or not SAME_ENGINE_SYNC:
                    continue
                if SAME_ENGINE_SYNC == 'raw' and not is_raw:
                    continue
            if eng.waited.get(sem, 0) >= val:
                continue
            if need.get(sem, 0) < val:
                need[sem] = val
        for sem, val in need.items():
            eng.waited[sem] = val
        return list(need.items())

    def op(self, engname, fn, reads=(), writes=(), signal=True):
        fn = _snapshot(fn)
        eng = self.engs[engname]
        waits = self._waits_for(eng, self._hazards(reads, writes))
        if signal:
            eng.seq += 1
            tok = (eng.sem, eng.seq)
        else:
            tok = (eng.sem, eng.seq + 1)
        sem = eng.sem

        def emit(h, S, waits=waits, fn=fn, signal=signal, sem=sem):
            for (s, v) in waits:
                h.wait_ge(S[s], v)
            inst = fn(h)
            if signal:
                inst.then_inc(S[sem], 1)

        eng.ops.append(emit)
        if self.log is not None:
            self.log.append((engname, tok, [r.name for r in reads], [w.name for w in writes], list(waits)))
        for r in reads:
            r.readers.append(tok)
            if len(r.readers) > 64:
                r.readers = _compact(r.readers)
        for w in writes:
            w.writer = tok
            w.readers = []
        self.n_ops += 1

    def dma(self, queue, fn, reads=(), writes=(), semres=None):
        fn = _snapshot(fn)
        eng = self.engs[queue]
        waits = self._waits_for(eng, self._hazards(reads, writes))
        sr = semres if semres is not None else (writes[0] if writes else reads[0])
        if sr.dsem is None:
            sr.dsem = "dma%d" % len(self.sem_names)
            self.sem_names.add(sr.dsem)
        sr.dcount += 16
        tok = (sr.dsem, sr.dcount)
        dsem = sr.dsem

        def emit(h, S, waits=waits, fn=fn, dsem=dsem):
            for (s, v) in waits:
                h.wait_ge(S[s], v)
            fn(h).then_inc(S[dsem], 16)

        eng.ops.append(emit)
        for r in reads:
            r.readers.append(tok)
            if len(r.readers) > 64:
                r.readers = _compact(r.readers)
        for w in writes:
            w.writer = tok
            w.readers = []
        self.n_ops += 1

    def finish(self, final_res):
        eng = self.engs["sp"]
        hz = []
        for r in final_res:
            if r.writer is not None:
                hz.append(r.writer + (True,))
            hz.extend(t + (True,) for t in r.readers)
        waits = self._waits_for(eng, hz)

        def emit(h, S, waits=waits):
            for (s, v) in waits:
                h.wait_ge(S[s], v)

        eng.ops.append(emit)

    def emit(self, nc, stack):
        S = {}
        for name in sorted(self.sem_names):
            S[name] = stack.enter_context(nc.semaphore(name))
        block = stack.enter_context(nc.Block())
        engs = self.engs

        @block.tensor
        def _(h):
            for f in engs["pe"].ops:
                f(h, S)

        @block.scalar
        def _(h):
            for f in engs["act"].ops:
                f(h, S)

        @block.vector
        def _(h):
            for f in engs["dve"].ops:
                f(h, S)

        @block.gpsimd
        def _(h):
            for f in engs["pool"].ops:
                f(h, S)

        @block.sync
        def _(h):
            for f in engs["sp"].ops:
                f(h, S)


def _snapshot(fn):
    if fn.__closure__ is None:
        return fn
    cells = []
    for c in fn.__closure__:
        try:
            cells.append(types.CellType(c.cell_contents))
        except ValueError:
            cells.append(c)
    return types.FunctionType(fn.__code__, fn.__globals__, fn.__name__, fn.__defaults__, tuple(cells))


def _compact(toks):
    best = {}
    for s, v in toks:
        if best.get(s, 0) < v:
            best[s] = v
    return list(best.items())


def _rope_tables():
    theta = np.float32(10000.0)
    t = np.arange(NT)
    row = (t // 64).astype(np.float32)
    col = (t % 64).astype(np.float32)
    inv16 = np.power(theta, -np.arange(16, dtype=np.float32) / np.float32(16)).astype(np.float32)
    inv8 = np.power(theta, -np.arange(8, dtype=np.float32) / np.float32(8)).astype(np.float32)
    cA = np.zeros((128, NT), np.float32); sA = np.zeros((128, NT), np.float32)
    cB = np.zeros((128, NT), np.float32); sB = np.zeros((128, NT), np.float32)
    pA = np.zeros((128, 128), np.float32); pB = np.zeros((128, 128), np.float32)
    for p in range(128):
        d = p % 64
        e = d % 32
        pos = row if d < 32 else col
        first = e < 16
        ang = (pos * inv16[e % 16]).astype(np.float32)
        cA[p] = np.cos(ang)
        sA[p] = -np.sin(ang) if first else np.sin(ang)
        partner = p + 16 if first else p - 16
        pA[partner, p] = 1.0
        e32 = d % 32
        posb = row if e32 < 16 else col
        eb = e32 % 16
        firstb = eb < 8
        angb = (posb * inv8[eb % 8]).astype(np.float32)
        cB[p] = np.cos(angb)
        sB[p] = -np.sin(angb) if firstb else np.sin(angb)
        partnerb = p + 8 if firstb else p - 8
        pB[partnerb, p] = 1.0
    return cA, sA, cB, sB, pA, pB


def _consts():
    cA, sA, cB, sB, pA, pB = _rope_tables()
    ident = np.eye(128, dtype=np.float32)
    blockones = np.zeros((128, 128), np.float32)
    blockones[:64, :64] = 1.0
    blockones[64:, 64:] = 1.0
    kl = np.arange(128)[:, None]
    ql = np.arange(128)[None, :]
    triA = np.where(kl <= ql, 0.0, NEG).astype(np.float32)
    triB = np.where(kl >= ql, 0.0, NEG).astype(np.float32)
    sq = np.stack([ident, pA, pB, blockones, triA, triB]).astype(np.float32)
    kc = np.arange(64)[:, None]
    qc = np.arange(64)[None, :]
    start_c = np.clip(qc - 8, 0, 48)
    colvalid = (kc >= start_c) & (kc < start_c + 16)
    cm = np.where(colvalid, 0.0, NEG).astype(np.float32)
    cmask = np.zeros((128, 25, 64), np.float32)
    cmask[:64] = cm[:, None, :]
    cmask[64:] = cm[:, None, :]
    cmask = cmask.reshape(128, 1600)
    small = np.zeros((2, 256), np.float32)
    small[0, 0:64] = 1.0
    small[1, 64:128] = 1.0
    for krl in range(2):
        for kt in range(8):
            for qr in range(16):
                st_ = min(max(qr - 4, 0), 8)
                kr = 2 * kt + krl
                ok = (kr >= st_) and (kr < st_ + 8)
                small[krl, 128 + kt * 16 + qr] = 0.0 if ok else NEG
    cc = np.zeros((128, 8), np.float32)
    for p in range(128):
        cc[p, 0] = 1.0 if (p % 64) < 32 else 0.0
        cc[p, 1] = 0.0 if (p % 64) < 32 else 1.0
    cc[:, 2] = 1e-6
    cc[:, 3] = 1e-5
    ropes = np.stack([cA, sA, cB, sB]).astype(np.float32)
    return dict(c_ident=ident, c_sq=sq, c_cmask=cmask, c_small=small, c_cc=cc, c_ropes=ropes)


def _rpb_expand(rpb_d):
    L, H = rpb_d.shape[0], rpb_d.shape[1]
    kc = np.arange(64)[:, None]
    qc = np.arange(64)[None, :]
    dc = np.clip(kc - qc, -15, 15) + 15
    out = np.zeros((L, H, 2, 64, 25, 64), np.float32)
    for krl in range(2):
        for j in range(25):
            dr = j - 12 + krl
            if -7 <= dr <= 7:
                out[:, :, krl, :, j, :] = rpb_d[:, :, dr + 7][:, :, dc]
    return out.reshape(L, H, 128, 1600)


def build_program(groups=("s", "p"), n_layers=DEPTH, dbg=None, skip=(), ada_layers=None, oplog=None):
    nc = bass.Bass("TRN2", target_bir_lowering=False)
    P = Prog()
    P.log = oplog
    st = ExitStack()
    with st:
        def din(name, shape):
            return nc.dram_tensor(name, list(shape), F32, kind="ExternalInput").ap()

        def dout(name, shape):
            return nc.dram_tensor(name, list(shape), F32, kind="ExternalOutput").ap()

        def sb(name, shape, dt=F32):
            name = "s_" + name
            return st.enter_context(nc.sbuf_tensor(name, list(shape), dt))

        x_in = {"s": din("xs", [NT, D]), "p": din("xp", [NT, D])}
        cond_d = din("cond", [16, 128])
        cache = {}
        for m in "abcd":
            H = MIX[m]["kvh"]
            cache[m] = (din("ck_" + m, [DEPTH, H, 256, HD]), din("cv_" + m, [DEPTH, H, 256, HD]))
        w_ada = din("w_ada", [DEPTH, D, 6 * D])
        w_in = din("w_in", [DEPTH, D, IN_COLS])
        w_branch = din("w_branch", [DEPTH, 4 * 256, D])
        w_out = din("w_out", [DEPTH, D, D])
        w_ffn_in = din("w_ffn_in", [DEPTH, D, 2 * DFF])
        w_ffn_out = din("w_ffn_out", [DEPTH, DFF, D])
        vecs_d = din("vecs", [256, 128])
        lamb_d = din("lamb", [1, 256])
        rpbx_d = din("rpbx", [DEPTH, 4, 128, 1600])
        c_ident = din("c_ident", [128, 128])
        c_sq = din("c_sq", [6, 128, 128])
        c_cmask = din("c_cmask", [128, 1600])
        c_small = din("c_small", [2, 256])
        c_cc = din("c_cc", [128, 8])
        c_ropes = din("c_ropes", [4, 128, NT])

        y_out = {"s": dout("ys", [NT, D]), "p": dout("yp", [NT, D])}
        nk_out, nv_out = {}, {}
        for m in "abcd":
            H = MIX[m]["kvh"]
            nk_out[m] = dout("nk_" + m, [4, DEPTH, H, 256, HD])
            nv_out[m] = dout("nv_" + m, [4, DEPTH, H, 256, HD])
        out_res = [P.res("out_all")]
        OUT = out_res[0]
        dbg_out = {}
        if dbg:
            for name, shape in dbg.items():
                dbg_out[name] = dout("dbg_" + name, shape)

        xT = sb("xT", [128, 8, NT]); xT_r = [[P.res("xT%d_%d" % (k, tb)) for tb in range(2)] for k in range(8)]
        hT = sb("hT", [128, 8, NT], BF16); hT_r = [[P.res("hT%d_%d" % (k, tb)) for tb in range(2)] for k in range(8)]
        bufA = sb("bufA", [128, 8, NT], BF16); A_r = [[P.res("A%d_%d" % (k, tb)) for tb in range(2)] for k in range(8)]
        bufB = sb("bufB", [128, 19, 512], BF16); B_r = [P.res("B%d" % i) for i in range(19)]
        NSLOT = 4
        wslot = [sb("wslot%d" % i, [128, 8, 512], BF16) for i in range(NSLOT)]
        wslot_r = [P.res("wslot%d" % i) for i in range(NSLOT)]
        ropes = sb("ropes", [128, 4, NT]); ropes_r = P.res("ropes")
        cmask = sb("cmask", [128, 1600], BF16)
        SBh = [sb("SBh%d" % h, [128, 1600], BF16) for h in range(4)]; SBh_r = [P.res("SBh%d" % h) for h in range(4)]
        NE = 6
        Et = [sb("Et%d" % i, [128, 512], BF16) for i in range(NE)]; Et_r = [P.res("Et%d" % i) for i in range(NE)]
        NTMP = 10
        tmp = [sb("tmp%d" % i, [128, 512]) for i in range(NTMP)]; tmp_r = [P.res("tmp%d" % i) for i in range(NTMP)]
        xstage = [sb("xstage%d" % i, [128, D]) for i in range(2)]; xstage_r = [P.res("xstage%d" % i) for i in range(2)]
        NKST = 4
        kst = [sb("kst%d" % i, [128, 512]) for i in range(NKST)]; kst_r = [P.res("kst%d" % i) for i in range(NKST)]
        NVST = 4
        vst = [sb("vst%d" % i, [128, 256]) for i in range(NVST)]; vst_r = [P.res("vst%d" % i) for i in range(NVST)]
        cst = sb("cst", [128, 2, 4, 64]); cst_r = P.res("cst")
        ident = sb("ident", [128, 128])
        sqb = sb("sqb", [128, 6, 128], BF16)
        ones_b = sb("ones_b", [128, 128], BF16)
        small = sb("small", [2, 256], BF16)
        cc = sb("cc", [128, 8])
        consts_r = P.res("consts")
        vec_in = sb("vec_in", [128, 2, 128])
        vecT = sb("vecT", [128, 256]); vecT_r = P.res("vecT")
        cond_sb = sb("cond_sb", [16, 128]); cond_r = P.res("cond_sb")
        csil = sb("csil", [16, 128]); csil_r = P.res("csil")
        scT = sb("scT", [128, 8, 2], BF16); scT_r = P.res("scT")
        mod = sb("mod", [128, DEPTH, 48, 2]); mod_r = P.res("mod")
        dm = sb("dm", [128, DEPTH, 2, 2, 8]); dm_r = P.res("dm")
        lamb = sb("lamb", [128, 256]); lamb_r = P.res("lamb")
        lsm = sb("lsm", [128, 16]); lsm_r = P.res("lsm")
        gfin_b = sb("gfin_b", [128, D]); gfin_r = P.res("gfin_b")
        ss_col = sb("ss_col", [128, 8]); ss_col_r = P.res("ss_col")

        banks = [st.enter_context(nc.psum_tensor("bank%d" % i, [128, 512], F32)) for i in range(8)]
        bank_r = [P.res("bank%d" % i, excl=True) for i in range(8)]

        rr = {"all": 0, "S": 0, "E": 0, "T": 0, "slot": 0, "xs": 0, "k": 0, "v": 0, "eng": 0}

        def nbank(pool="all"):
            if pool == "all":
                i = rr["all"] % 8
                rr["all"] += 1
            else:
                i = rr["S"] % 6
                rr["S"] += 1
            return i

        def nE():
            i = rr["E"] % NE
            rr["E"] += 1
            return i

        def nT():
            i = rr["T"] % NTMP
            rr["T"] += 1
            return i

        held_slots = set()

        def nslot():
            while True:
                i = rr["slot"] % NSLOT
                rr["slot"] += 1
                if i not in held_slots:
                    return i

        def alt():
            rr["eng"] += 1
            return "act" if rr["eng"] % 2 else "dve"

        def col(t, c):
            return t[:, c:c + 1]

        def load_w(src_ap, slot, cols=slice(0, 512), kk=slice(0, 8)):
            P.dma("pool", lambda h: h.dma_start(out=wslot[slot][:, kk, cols], in_=src_ap),
                  writes=[wslot_r[slot]])

        def wv(t3, l):
            return t3[l].rearrange("(k p) n -> p k n", p=128)

        def dump(name, src_ap, reads, q="sp"):
            if name in dbg_out and name not in dumped:
                dumped.add(name)
                P.dma(q, lambda h: h.dma_start(out=dbg_out[name], in_=src_ap), reads=reads, semres=OUT)
        dumped = set()

        P.dma("sp", lambda h: h.dma_start(out=ident[:], in_=c_ident), writes=[consts_r])
        P.dma("sp", lambda h: h.dma_start(out=cc[:], in_=c_cc), writes=[consts_r])
        P.dma("sp", lambda h: h.dma_start(out=vec_in[:], in_=vecs_d.rearrange("(a p) n -> p a n", p=128)), writes=[consts_r])
        P.dma("sp", lambda h: h.dma_start(out=cond_sb[:], in_=cond_d), writes=[cond_r])
        P.dma("sp", lambda h: h.dma_start(out=lamb[:], in_=AP(lamb_d.tensor, 0, [[0, 128], [1, 256]])), writes=[lamb_r])
        P.dma("sp", lambda h: h.dma_start(out=gfin_b[:], in_=AP(vecs_d.tensor, 128 * 128, [[0, 128], [1, D]])), writes=[gfin_r])
        constsp_r = P.res("constsp")
        P.dma("pool", lambda h: h.dma_start(out=sqb[:], in_=c_sq.rearrange("a p n -> p a n")), writes=[constsp_r])
        P.dma("pool", lambda h: h.dma_start(out=small[:], in_=c_small), writes=[constsp_r])
        for i in range(4):
            P.dma("pool", lambda h, i=i: h.dma_start(out=cmask[:, i * 400:(i + 1) * 400], in_=c_cmask[:, i * 400:(i + 1) * 400]),
                  writes=[constsp_r])
        if "s" in groups:
            P.dma("sp", lambda h: h.dma_start(out=ropes[:], in_=c_ropes.rearrange("a p n -> p a n")), writes=[ropes_r])
        ones_r = P.res("ones")
        P.op("dve", lambda h: h.memset(ones_b[:], 1.0), writes=[ones_r])
        identb = sqb[:, 0, :]
        permA = sqb[:, 1, :]
        permB = sqb[:, 2, :]
        blockones = sqb[:, 3, :]
        triA = sqb[:, 4, :]
        triB = sqb[:, 5, :]
        oh2 = small[0:2, 0:128]
        CR = [consts_r, constsp_r, ones_r]

        b = nbank()
        for a in range(2):
            P.op("pe", lambda h, a=a, b=b: h.transpose(banks[b][:, a * 128:(a + 1) * 128], vec_in[:, a, :], ident[:]),
                 reads=CR, writes=[bank_r[b]], signal=(a == 1))
        P.op("dve", lambda h, b=b: h.tensor_copy(out=vecT[:], in_=banks[b][:, 0:256]), reads=[bank_r[b]], writes=[vecT_r])
        V_BADA, V_G1, V_G2, V_GF, V_GQA, V_GKA, V_GSUB, V_SINK = 0, 96, 112, 128, 136, 138, 140, 142

        P.op("act", lambda h: h.activation(out=csil[:], in_=cond_sb[:], func=AF.Silu), reads=[cond_r], writes=[csil_r])
        b = nbank()
        P.op("pe", lambda h, b=b: h.transpose(banks[b][:, 0:16], csil[0:16, :], ident[0:16, 0:16]),
             reads=CR + [csil_r], writes=[bank_r[b]])
        P.op("dve", lambda h, b=b: h.tensor_copy(out=scT[:].rearrange("p k c -> p c k"), in_=banks[b][:, 0:16].rearrange("p (c k) -> p c k", c=2)),
             reads=[bank_r[b]], writes=[scT_r])

        ada_inflight = []

        def ada_issue(n):
            for _ in range(n):
                if not ada_pending or len(held_slots) >= NSLOT - 2 + (2 if not ada_started[0] else 0):
                    break
                l, cb = ada_pending.pop(0)
                s = nslot()
                held_slots.add(s)
                load_w(wv(w_ada, l)[:, :, cb * 512:(cb + 1) * 512], s)
                ada_inflight.append((l, cb, s))
            ada_started[0] = True

        def ada_compute():
            while ada_inflight:
                l, cb, s = ada_inflight.pop(0)
                ada_block(l, cb, s)
                held_slots.discard(s)

        def ada_block(l, cb, s):
            b = nbank()
            for ct in range(4):
                for k in range(8):
                    P.op("pe", lambda h, s=s, ct=ct, k=k, b=b: h.matmul(
                        banks[b][:, ct * 2:ct * 2 + 2], wslot[s][:, k, ct * 128:(ct + 1) * 128], scT[:, k, :],
                        start=(k == 0), stop=(k == 7)),
                        reads=[wslot_r[s], scT_r], writes=[bank_r[b]], signal=(k == 7 and ct == 3))
            P.op("dve", lambda h, l=l, b=b, cb=cb: h.tensor_tensor(
                out=mod[:, l, cb * 4:(cb + 1) * 4, :], in0=banks[b][:, 0:8].rearrange("p (j c) -> p j c", c=2),
                in1=AP(vecT, V_BADA + l * 48 + cb * 4, [[256, 128], [1, 4], [0, 2]]), op=ALU.add),
                reads=[bank_r[b], vecT_r], writes=[mod_r])
            if cb in (3, 9):
                which, j, gbase = (0, 1, V_G1) if cb == 3 else (1, 4, V_G2)
                for c in range(2):
                    P.op("dve", lambda h, l=l, c=c, which=which, j=j, gbase=gbase: h.scalar_tensor_tensor(
                        out=dm[:, l, c, which, :], in0=mod[:, l, j * 8:(j + 1) * 8, c], scalar=1.0,
                        in1=vecT[:, gbase + l * 8:gbase + (l + 1) * 8], op0=ALU.add, op1=ALU.mult),
                        reads=[mod_r, vecT_r], writes=[dm_r])

        ada_started = [False]
        ada_pending = [(l, cb) for l in range(n_layers if ada_layers is None else ada_layers) for cb in range(12)]

        def ada_some(n):
            ada_compute()
            ada_issue(n)

        def ada_flush():
            ada_compute()
            while ada_pending:
                ada_issue(2)
                ada_compute()


        if dbg:
            ada_flush()
        dump("mod", mod[:].rearrange("p l j c -> p (l j c)"), [mod_r])
        dump("vecT", vecT[:], [vecT_r])
        dump("dm", dm[:].rearrange("p l c w k -> p (l c w k)"), [dm_r])
        dump("scT", scT[:].rearrange("p k c -> p (k c)"), [scT_r], q="pool")

        def modcol(l, j, k, c):
            return mod[:, l, j * 8 + k, c:c + 1]

        for l in range(n_layers):
            lam_init = 0.8 - 0.6 * math.exp(-0.3 * l)
            for i in range(2):
                o = l * 128 + i * 64
                P.op("dve", lambda h, o=o, i=i: h.tensor_tensor(out=tmp[0][:, i * 32:(i + 1) * 32],
                                                              in0=lamb[:, o:o + 32], in1=lamb[:, o + 32:o + 64], op=ALU.mult),
                     reads=[lamb_r], writes=[tmp_r[0]])
                P.op("dve", lambda h, i=i: h.tensor_reduce(out=lsm[:, 8 + i:9 + i], in_=tmp[0][:, i * 32:(i + 1) * 32], axis=AX.X, op=ALU.add),
                     reads=[tmp_r[0]], writes=[lsm_r])
            P.op("act", lambda h: h.activation(out=lsm[:, 10:12], in_=lsm[:, 8:10], func=AF.Exp), reads=[lsm_r], writes=[lsm_r])
            P.op("dve", lambda h: h.tensor_tensor(out=lsm[:, 12:13], in0=lsm[:, 11:12], in1=lsm[:, 10:11], op=ALU.subtract),
                 reads=[lsm_r], writes=[lsm_r])
            P.op("dve", lambda h, l=l, li=lam_init: h.tensor_scalar(out=lsm[:, l:l + 1], in0=lsm[:, 12:13], scalar1=-li, scalar2=None, op0=ALU.add),
                 reads=[lsm_r], writes=[lsm_r])
            P.op("dve", lambda h, l=l, li=lam_init: h.tensor_scalar(out=lsm[:, 2 + l:3 + l], in0=vecT[:, V_GSUB + l:V_GSUB + l + 1],
                                                                   scalar1=(1.0 - li), scalar2=None, op0=ALU.mult),
                 reads=[vecT_r], writes=[lsm_r])
        P.op("act", lambda h: h.activation(out=lsm[:, 4:8], in_=vecT[:, V_SINK:V_SINK + 4], func=AF.Exp), reads=[vecT_r], writes=[lsm_r])

        def build_SB(l):
            for hh in range(4):
                for i in range(4):
                    P.dma("pool", lambda h, hh=hh, i=i: h.dma_start(out=SBh[hh][:, i * 400:(i + 1) * 400],
                                                                   in_=rpbx_d[l, hh][:, i * 400:(i + 1) * 400]), writes=[SBh_r[hh]])
            for hh in range(4):
                P.op("dve", lambda h, hh=hh: h.scalar_tensor_tensor(out=SBh[hh][:], in0=SBh[hh][:], scalar=8.0, in1=cmask[:],
                                                                  op0=ALU.mult, op1=ALU.add),
                     reads=[SBh_r[hh]] + CR, writes=[SBh_r[hh]])

        def rstd_from_bank(b, inv_n, eps_col):
            t1 = nT()
            out_t = nT()
            P.op("act", lambda h: h.activation(out=tmp[t1][:], in_=banks[b][:], func=AF.Ln, scale=inv_n, bias=col(cc, eps_col)),
                 reads=[bank_r[b]] + CR, writes=[tmp_r[t1]])
            P.op("act", lambda h: h.activation(out=tmp[out_t][:], in_=tmp[t1][:], func=AF.Exp, scale=-0.5),
                 reads=[tmp_r[t1]], writes=[tmp_r[out_t]])
            return out_t

        def norm_mod(l, c, which):
            jsh = 0 if which == 0 else 3
            for tb in range(2):
                ts = slice(tb * 512, (tb + 1) * 512)
                b = nbank()
                for k in range(8):
                    e = nE()
                    if k % 3 == 2:
                        P.op("act", lambda h, k=k, e=e: h.activation(out=Et[e][:], in_=xT[:, k, ts], func=AF.Square),
                             reads=[xT_r[k][tb]], writes=[Et_r[e]])
                    else:
                        P.op("pool" if k % 3 == 0 else "dve", lambda h, k=k, e=e: h.tensor_tensor(out=Et[e][:], in0=xT[:, k, ts], in1=xT[:, k, ts], op=ALU.mult),
                             reads=[xT_r[k][tb]], writes=[Et_r[e]])
                    P.op("pe", lambda h, k=k, e=e, b=b: h.matmul(banks[b][:], ones_b[:], Et[e][:], start=(k == 0), stop=(k == 7)),
                         reads=[Et_r[e]] + CR, writes=[bank_r[b]], signal=True)
                rs = rstd_from_bank(b, 1.0 / D, 2)
                for k in range(8):
                    t = nT()
                    P.op("dve", lambda h, k=k, t=t: h.tensor_tensor(out=tmp[t][:], in0=xT[:, k, ts], in1=tmp[rs][:], op=ALU.mult),
                         reads=[xT_r[k][tb], tmp_r[rs]], writes=[tmp_r[t]])
                    P.op("act", lambda h, k=k, t=t: h.activation(out=hT[:, k, ts], in_=tmp[t][:], func=AF.Identity,
                                                               scale=dm[:, l, c, which, k:k + 1], bias=modcol(l, jsh, k, c)),
                         reads=[tmp_r[t], dm_r, mod_r], writes=[hT_r[k][tb]])

        def QT(j, tb):
            return j * 2 + tb

        def KTtok(i, tb):
            return 4 + i * 2 + tb

        def KTctx_ap(i, p0, pn, c0, cn):
            return bufB[p0:p0 + pn, 12 + i // 2, (i % 2) * 256 + c0:(i % 2) * 256 + c0 + cn]

        def KTctx_r(i):
            return B_r[12 + i // 2]

        def V_ap(vt, c0, cn):
            return bufB[:, 14 + vt // 2, (vt % 2) * 256 + c0:(vt % 2) * 256 + c0 + cn]

        def V_r(vt):
            return B_r[14 + vt // 2]

        def post_qk(m, l, g, kind, j, tb, slot, s1, cbase):
            rope = (g == "s") and m in "abc"
            normed = (m == "a")
            nb, nt, ne, _ = qk_slot_plan(m, g)
            bl = [slot * nb + i for i in range(nb)]
            tl = [slot * nt + i for i in range(nt)]
            el = [slot * ne + i for i in range(ne)]
            b = bl.pop(0)
            b2 = bl.pop(0) if normed else None
            b3 = bl.pop(0) if rope else None
            b4 = bl.pop(0) if g == "p" else None
            t0 = tl.pop(0) if normed else None
            rs = tl.pop(0) if normed else None
            t1 = tl.pop(0) if (rope or normed) else None
            t2 = tl.pop(0) if rope else None
            t4 = tl.pop(0) if (g == "p" and not normed) else None
            e = el.pop(0) if normed else None
            e2 = el.pop(0) if rope else None
            for k in range(8):
                P.op("pe", lambda h, k=k: h.matmul(
                    banks[b][:], wslot[s1][:, k, cbase + j * 128:cbase + (j + 1) * 128], hT[:, k, tb * 512:(tb + 1) * 512],
                    start=(k == 0), stop=(k == 7)),
                    reads=[wslot_r[s1], hT_r[k][tb]], writes=[bank_r[b]], signal=(k == 7))
            yield
            ts = slice(tb * 512, (tb + 1) * 512)
            is_k = (kind == "k")
            need_f32 = is_k and g == "p"
            gcol = None
            if normed:
                gcol = col(vecT, (V_GKA if is_k else V_GQA) + l)
            if rope:
                ci, si, perm = (0, 1, permA) if m in "ac" else (2, 3, permB)
            if normed:
                P.op("act", lambda h: h.activation(out=Et[e][:], in_=banks[b][:], func=AF.Square), reads=[bank_r[b]], writes=[Et_r[e]])
            if rope:
                if normed:
                    P.op("act", lambda h: h.activation(out=Et[e2][:], in_=banks[b][:], func=AF.Copy, scale=gcol),
                         reads=[bank_r[b], vecT_r], writes=[Et_r[e2]])
                else:
                    P.op("act", lambda h: h.activation(out=Et[e2][:], in_=banks[b][:], func=AF.Copy),
                         reads=[bank_r[b]], writes=[Et_r[e2]])
            if normed or rope:
                yield
            if normed:
                P.op("pe", lambda h: h.matmul(banks[b2][:], blockones, Et[e][:], start=True, stop=True),
                     reads=[Et_r[e]] + CR, writes=[bank_r[b2]])
            if rope:
                P.op("pe", lambda h: h.matmul(banks[b3][:], perm, Et[e2][:], start=True, stop=True),
                     reads=[Et_r[e2]] + CR, writes=[bank_r[b3]])
                if normed:
                    P.op("dve", lambda h: h.scalar_tensor_tensor(out=tmp[t1][:], in0=banks[b][:], scalar=gcol, in1=ropes[:, ci, ts],
                                                               op0=ALU.mult, op1=ALU.mult),
                         reads=[bank_r[b], vecT_r, ropes_r], writes=[tmp_r[t1]])
                else:
                    P.op("dve", lambda h: h.tensor_tensor(out=tmp[t1][:], in0=banks[b][:], in1=ropes[:, ci, ts], op=ALU.mult),
                         reads=[bank_r[b], ropes_r], writes=[tmp_r[t1]])
            if normed or rope:
                yield
            if normed:
                P.op("act", lambda h: h.activation(out=tmp[t0][:], in_=banks[b2][:], func=AF.Ln, scale=1.0 / HD, bias=col(cc, 2)),
                     reads=[bank_r[b2]] + CR, writes=[tmp_r[t0]])
                P.op("act", lambda h: h.activation(out=tmp[rs][:], in_=tmp[t0][:], func=AF.Exp, scale=-0.5),
                     reads=[tmp_r[t0]], writes=[tmp_r[rs]])
            if rope:
                P.op("dve", lambda h: h.tensor_tensor(out=tmp[t2][:], in0=banks[b3][:], in1=ropes[:, si, ts], op=ALU.mult),
                     reads=[bank_r[b3], ropes_r], writes=[tmp_r[t2]])
                P.op("dve", lambda h: h.tensor_tensor(out=tmp[t1][:], in0=tmp[t1][:], in1=tmp[t2][:], op=ALU.add),
                     reads=[tmp_r[t1], tmp_r[t2]], writes=[tmp_r[t1]])
            if normed:
                yield
                if rope:
                    P.op("dve", lambda h: h.tensor_tensor(out=tmp[t1][:], in0=tmp[t1][:], in1=tmp[rs][:], op=ALU.mult),
                         reads=[tmp_r[t1], tmp_r[rs]], writes=[tmp_r[t1]])
                else:
                    P.op("dve", lambda h: h.scalar_tensor_tensor(out=tmp[t1][:], in0=banks[b][:], scalar=gcol, in1=tmp[rs][:],
                                                               op0=ALU.mult, op1=ALU.mult),
                         reads=[bank_r[b], vecT_r, tmp_r[rs]], writes=[tmp_r[t1]])
            if normed or rope:
                fin = (tmp[t1], tmp_r[t1])
            elif need_f32:
                P.op("act", lambda h: h.activation(out=tmp[t4][:], in_=banks[b][:], func=AF.Copy), reads=[bank_r[b]], writes=[tmp_r[t4]])
                fin = (tmp[t4], tmp_r[t4])
            else:
                fin = (banks[b], bank_r[b])
            yield
            ft, fr = fin
            if not is_k:
                d = QT(j, tb)
                eng = alt()
                if eng == "act":
                    P.op("act", lambda h: h.activation(out=bufB[:, d, :], in_=ft[:], func=AF.Copy), reads=[fr], writes=[B_r[d]])
                else:
                    P.op("dve", lambda h: h.tensor_copy(out=bufB[:, d, :], in_=ft[:]), reads=[fr], writes=[B_r[d]])
            elif m == "b":
                for mp in range(2):
                    d = KTtok(mp * 2 + j, tb)
                    eng = "act" if mp == 0 else "dve"
                    if eng == "act":
                        P.op("act", lambda h, mp=mp, d=d: h.activation(out=bufB[:, d, :], in_=ft[:], func=AF.Copy, scale=col(cc, mp)),
                             reads=[fr] + CR, writes=[B_r[d]])
                    else:
                        P.op("dve", lambda h, mp=mp, d=d: h.tensor_scalar(out=bufB[:, d, :], in0=ft[:], scalar1=col(cc, mp), scalar2=None,
                                                                        op0=ALU.mult),
                             reads=[fr] + CR, writes=[B_r[d]])
            else:
                d = KTtok(j, tb)
                eng = alt()
                if eng == "act":
                    P.op("act", lambda h: h.activation(out=bufB[:, d, :], in_=ft[:], func=AF.Copy), reads=[fr], writes=[B_r[d]])
                else:
                    P.op("dve", lambda h: h.tensor_copy(out=bufB[:, d, :], in_=ft[:]), reads=[fr], writes=[B_r[d]])
            if need_f32:
                dup = m in "ac"
                wd = 64 if dup else 128
                for i in range(4):
                    if dup:
                        P.op("pe", lambda h, i=i: h.transpose(banks[b4][:, i * 64:(i + 1) * 64], ft[0:64, i * 128:(i + 1) * 128], ident[0:64, 0:64]),
                             reads=[fr] + CR, writes=[bank_r[b4]], signal=(i == 3))
                    else:
                        P.op("pe", lambda h, i=i: h.transpose(banks[b4][:, i * 128:(i + 1) * 128], ft[:, i * 128:(i + 1) * 128], ident[:]),
                             reads=[fr] + CR, writes=[bank_r[b4]], signal=(i == 3))
                yield
                ks = rr["k"] % NKST
                rr["k"] += 1
                P.op("dve", lambda h: h.tensor_copy(out=kst[ks][:, 0:4 * wd], in_=banks[b4][:, 0:4 * wd]), reads=[bank_r[b4]], writes=[kst_r[ks]])
                for i in range(4):
                    tt = tb * 4 + i
                    bb, s0 = tt // 2, (tt % 2) * 128
                    if dup:
                        dst = nk_out[m][bb, l, j, s0:s0 + 128, :]
                        src = kst[ks][:, i * 64:(i + 1) * 64]
                    else:
                        dst = nk_out[m][bb, l, 2 * j:2 * j + 2, s0:s0 + 128, :].rearrange("h s d -> s h d")
                        src = kst[ks][:, i * 128:(i + 1) * 128].rearrange("p (h d) -> p h d", h=2)
                    P.dma("sp", lambda h, dst=dst, src=src: h.dma_start(out=dst, in_=src), reads=[kst_r[ks]], semres=kst_r[ks])

        def qk_slot_plan(m, g):
            rope = (g == "s") and m in "abc"
            normed = (m == "a")
            nb = 1 + (1 if normed else 0) + (1 if rope else 0) + (1 if g == "p" else 0)
            nt = (2 if normed else 0) + (2 if rope else (1 if normed else 0)) + (1 if (g == "p" and not normed) else 0)
            ne = (1 if normed else 0) + (1 if rope else 0)
            ns = min(4, 8 // nb)
            if nt:
                ns = min(ns, NTMP // nt)
            if ne:
                ns = min(ns, NE // ne)
            return nb, nt, ne, ns

        def run_pipelined(factories, depth=2):
            active = {}
            todo = list(factories)
            while todo or active:
                for slot in range(depth):
                    if slot not in active and todo:
                        active[slot] = todo.pop(0)(slot)
                for slot in sorted(active):
                    try:
                        next(active[slot])
                    except StopIteration:
                        del active[slot]

        def project_load(m, l):
            mx = MIX[m]
            dup = m in "ac"
            vw = mx["vh"] * 64
            wl = wv(w_in, l)
            s1 = nslot()
            load_w(wl[:, :, mx["q"]:mx["q"] + 256], s1, cols=slice(0, 256))
            if dup:
                for jj in range(2):
                    for r in range(2):
                        c0 = 256 + jj * 128 + r * 64
                        load_w(wl[:, :, mx["k"] + jj * 64:mx["k"] + (jj + 1) * 64], s1, cols=slice(c0, c0 + 64))
            else:
                load_w(wl[:, :, mx["k"]:mx["k"] + 256], s1, cols=slice(256, 512))
            s2 = nslot()
            load_w(wl[:, :, mx["v"]:mx["v"] + vw], s2, cols=slice(0, vw))
            return s1, s2

        def project(m, l, g, pre=None):
            mx = MIX[m]
            dup = m in "ac"
            vw = mx["vh"] * 64
            s1, s2 = pre if pre is not None else project_load(m, l)
            if g == "s":
                ck, cv = cache[m]
                H = mx["kvh"]
                for c in range(2):
                    P.dma("pool", lambda h, c=c: h.dma_start(
                        out=V_ap(c, 0, H * 64).rearrange("p (h d) -> p h d", h=H),
                        in_=cv[l, :, c * 128:(c + 1) * 128, :].rearrange("h s d -> s h d")), writes=[V_r(c)])
                for c in range(2):
                    if dup:
                        for hh in range(4):
                            P.dma("sp", lambda h, c=c, hh=hh: h.dma_start(out=cst[:, c, hh, :], in_=ck[l, hh // 2, c * 128:(c + 1) * 128, :]),
                                  writes=[cst_r])
                    else:
                        P.dma("sp", lambda h, c=c: h.dma_start(out=cst[:, c, :, :], in_=ck[l, :, c * 128:(c + 1) * 128, :].rearrange("h s d -> s h d")),
                              writes=[cst_r])
                for j in range(2):
                    b = nbank()
                    for c in range(2):
                        P.op("pe", lambda h, c=c, j=j, b=b: h.transpose(
                            banks[b][:, c * 128:(c + 1) * 128], cst[:, c, 2 * j:2 * j + 2, :].rearrange("p h d -> p (h d)"), ident[:]),
                            reads=[cst_r] + CR, writes=[bank_r[b]], signal=(c == 1))
                    if m == "b":
                        for mp in range(2):
                            i = mp * 2 + j
                            P.op("dve", lambda h, i=i, mp=mp, b=b: h.tensor_scalar(out=KTctx_ap(i, 0, 128, 0, 256), in0=banks[b][:, 0:256],
                                                                                 scalar1=col(cc, mp), scalar2=None, op0=ALU.mult),
                                 reads=[bank_r[b]] + CR, writes=[KTctx_r(i)])
                    else:
                        P.op("dve", lambda h, j=j, b=b: h.tensor_copy(out=KTctx_ap(j, 0, 128, 0, 256), in_=banks[b][:, 0:256]),
                             reads=[bank_r[b]], writes=[KTctx_r(j)])
            for tt in range(8):
                b = nbank()
                for k in range(8):
                    P.op("pe", lambda h, k=k, tt=tt, b=b: h.matmul(
                        banks[b][:, 0:vw], hT[:, k, tt * 128:(tt + 1) * 128], wslot[s2][:, k, 0:vw], start=(k == 0), stop=(k == 7)),
                        reads=[wslot_r[s2], hT_r[k][tt // 4]], writes=[bank_r[b]], signal=(k == 7))
                vt = 2 + tt
                P.op("dve", lambda h, vt=vt, b=b: h.tensor_copy(out=V_ap(vt, 0, vw), in_=banks[b][:, 0:vw]), reads=[bank_r[b]], writes=[V_r(vt)])
                if g == "p":
                    vs = rr["v"] % NVST
                    rr["v"] += 1
                    P.op("act", lambda h, vs=vs, b=b: h.activation(out=vst[vs][:, 0:vw], in_=banks[b][:, 0:vw], func=AF.Copy),
                         reads=[bank_r[b]], writes=[vst_r[vs]])
                    bb, s0 = tt // 2, (tt % 2) * 128
                    H = mx["vh"]
                    P.dma("sp", lambda h, vs=vs, bb=bb, s0=s0, H=H: h.dma_start(
                        out=nv_out[m][bb, l, :, s0:s0 + 128, :].rearrange("h s d -> s h d"),
                        in_=vst[vs][:, 0:H * 64].rearrange("p (h d) -> p h d", h=H)),
                        reads=[vst_r[vs]], semres=vst_r[vs])
            gens = []
            cnt_ = 0
            for kind, cbase in (("q", 0), ("k", 256)):
                for j in range(2):
                    for tb in range(2):
                        gens.append(lambda slot, kind=kind, j=j, tb=tb, cbase=cbase: post_qk(m, l, g, kind, j, tb, slot, s1, cbase))
            run_pipelined(gens, qk_slot_plan(m, g)[3])

        def recip_ap(d_ap, d_reads, nq, bias_ap, bias_reads):
            t1, t2 = nT(), nT()
            if nq <= 256:
                if bias_ap is None:
                    P.op("dve", lambda h: h.reciprocal(out=tmp[t2][:, 0:nq], in_=d_ap), reads=d_reads, writes=[tmp_r[t2]])
                else:
                    P.op("dve", lambda h: h.tensor_scalar(out=tmp[t1][:, 0:nq], in0=d_ap, scalar1=bias_ap, scalar2=None, op0=ALU.add),
                         reads=d_reads + bias_reads, writes=[tmp_r[t1]])
                    P.op("dve", lambda h: h.reciprocal(out=tmp[t2][:, 0:nq], in_=tmp[t1][:, 0:nq]), reads=[tmp_r[t1]], writes=[tmp_r[t2]])
                return t2
            if bias_ap is None:
                P.op("act", lambda h: h.activation(out=tmp[t1][:, 0:nq], in_=d_ap, func=AF.Ln), reads=d_reads, writes=[tmp_r[t1]])
            else:
                P.op("act", lambda h: h.activation(out=tmp[t1][:, 0:nq], in_=d_ap, func=AF.Ln, bias=bias_ap),
                     reads=d_reads + bias_reads, writes=[tmp_r[t1]])
            P.op("act", lambda h: h.activation(out=tmp[t2][:, 0:nq], in_=tmp[t1][:, 0:nq], func=AF.Exp, scale=-1.0),
                 reads=[tmp_r[t1]], writes=[tmp_r[t2]])
            return t2

        def run_blocks(blocks, scale, prompt):
            stream = [(bi, ui) for bi, blk in enumerate(blocks) for ui in range(len(blk["units"]))]
            for bi, blk in enumerate(blocks):
                if prompt:
                    blk["ob"] = blk["db"] = 6 + bi % 2
                    blk["oc"], blk["dcol"] = 0, 256
                else:
                    blk["ob"], blk["db"] = 6, 7
                    blk["oc"], blk["dcol"] = 0, 0
            sbk = {}
            PD = 2

            def QK(idx):
                bi, ui = stream[idx]
                blk = blocks[bi]
                U = blk["units"][ui]
                qtile, qc0 = blk["qtile"], blk["qc0"]
                lo, hi = U["lo"], U["hi"]
                pair = []
                for hh in range(2):
                    b = nbank("S")
                    pair.append(b)
                    kap, kr = U["k%d" % hh]
                    ex = U.get("extra%d" % hh, [])
                    P.op("pe", lambda h, b=b, kap=kap, hh=hh: h.matmul(
                        banks[b][:, lo:hi], kap, bufB[hh * 64:(hh + 1) * 64, qtile, qc0 + lo:qc0 + hi],
                        start=True, stop=(len(ex) == 0), skip_group_check=True),
                        reads=[kr, B_r[qtile]], writes=[bank_r[b]], signal=(len(ex) == 0))
                    for xi, (lt, rh, c0, c1, rds) in enumerate(ex):
                        last = xi == len(ex) - 1
                        P.op("pe", lambda h, b=b, lt=lt, rh=rh, c0=c0, c1=c1, last=last: h.matmul(
                            banks[b][:, c0:c1], lt, rh, start=False, stop=last, skip_group_check=True),
                            reads=rds, writes=[bank_r[b]], signal=last)
                sbk[idx] = pair

            for idx in range(min(PD, len(stream))):
                QK(idx)
            pending_fin = []
            for idx, (bi, ui) in enumerate(stream):
                if idx + PD < len(stream):
                    QK(idx + PD)
                blk = blocks[bi]
                U = blk["units"][ui]
                n = len(blk["units"])
                lo, hi = U["lo"], U["hi"]
                ob, db, oc, dcol = blk["ob"], blk["db"], blk["oc"], blk["dcol"]
                es = []
                for hh in range(2):
                    e = nE()
                    es.append(e)
                    b = sbk[idx][hh]
                    P.op("act", lambda h, e=e, b=b: h.activation(out=Et[e][:, lo:hi], in_=banks[b][:, lo:hi], func=AF.Exp, scale=scale),
                         reads=[bank_r[b]], writes=[Et_r[e]])
                del sbk[idx]
                while pending_fin:
                    pending_fin.pop(0)()
                for hh in range(2):
                    e = es[hh]
                    vap, vr = U["v%d" % hh]
                    P.op("pe", lambda h, e=e, vap=vap, hh=hh: h.matmul(
                        banks[ob][hh * 64:(hh + 1) * 64, oc + lo:oc + hi], vap, Et[e][:, lo:hi], start=(ui == 0), stop=(ui == n - 1),
                        skip_group_check=True),
                        reads=[vr, Et_r[es[0]], Et_r[es[1]], bank_r[db]], writes=[bank_r[ob]], signal=False)
                dstart = (ui == 0) and (db != ob)
                for hh in range(2):
                    e = es[hh]
                    P.op("pe", lambda h, e=e, hh=hh: h.matmul(
                        banks[db][hh * 64:(hh + 1) * 64, dcol + lo:dcol + hi], ones_b[:, 0:64], Et[e][:, lo:hi], start=dstart, stop=(ui == n - 1),
                        skip_group_check=True),
                        reads=[Et_r[e]] + CR, writes=[bank_r[db]], signal=(hh == 1))
                if ui == n - 1:
                    nq = blk["nq"]
                    rds = [bank_r[ob]] if ob == db else [bank_r[ob], bank_r[db]]
                    pending_fin.append(lambda blk=blk, ob=ob, db=db, oc=oc, dcol=dcol, nq=nq: blk["fin"](
                        banks[ob][:, oc:oc + nq], banks[db][:, dcol:dcol + nq], [bank_r[ob]], [bank_r[db]], db))
            while pending_fin:
                pending_fin.pop(0)()

        def attention(m, l, g):
            mx = MIX[m]
            mi = mx["idx"]
            dup = m in "ac"
            scale = (32.0 ** -0.5) if m == "b" else (64.0 ** -0.5)
            if g == "s":
                qblocks = [(tb, 0, 512, tb * 512) for tb in range(2)]
            else:
                qblocks = [(bb // 2, (bb % 2) * 256, 256, bb * 256) for bb in range(4)]
            blocks = []
            for j in range(2):
                for qi, (tb, qc0, nq, tok0) in enumerate(qblocks):
                    qtile = QT(j, tb)
                    held = {}
                    for mp in range(2 if m == "b" else 1):
                        ki = (mp * 2 + j) if m == "b" else j
                        units = []

                        def vpair(vt):
                            if dup:
                                a0 = (V_ap(vt, j * 64, 64), V_r(vt))
                                return a0, a0
                            return (V_ap(vt, (2 * j) * 64, 64), V_r(vt)), (V_ap(vt, (2 * j + 1) * 64, 64), V_r(vt))

                        def ktok(tt):
                            d = KTtok(ki, tt // 4)
                            c = (tt % 4) * 128
                            return (bufB[0:64, d, c:c + 128], B_r[d]), (bufB[64:128, d, c:c + 128], B_r[d])

                        if g == "s":
                            for c in range(2):
                                v0, v1 = vpair(c)
                                units.append(dict(k0=(KTctx_ap(ki, 0, 64, c * 128, 128), KTctx_r(ki)),
                                                  k1=(KTctx_ap(ki, 64, 64, c * 128, 128), KTctx_r(ki)), v0=v0, v1=v1, lo=0, hi=512))
                            if m in "ab":
                                for tt in range(8):
                                    k0, k1 = ktok(tt)
                                    v0, v1 = vpair(2 + tt)
                                    units.append(dict(k0=k0, k1=k1, v0=v0, v1=v1, lo=0, hi=512))
                            elif m == "c":
                                for tt in range(8):
                                    blo, bhi = max(tt - 1, 4 * tb), min(tt + 1, 4 * tb + 3)
                                    if blo > bhi:
                                        continue
                                    k0, k1 = ktok(tt)
                                    v0, v1 = vpair(2 + tt)
                                    lo, hi = (blo - 4 * tb) * 128, (bhi + 1 - 4 * tb) * 128
                                    ex = []
                                    if blo <= tt - 1 <= bhi:
                                        c0 = (tt - 1 - 4 * tb) * 128
                                        ex.append((identb, triA, c0, c0 + 128, CR))
                                    if blo <= tt + 1 <= bhi:
                                        c0 = (tt + 1 - 4 * tb) * 128
                                        ex.append((identb, triB, c0, c0 + 128, CR))
                                    units.append(dict(k0=k0, k1=k1, v0=v0, v1=v1, lo=lo, hi=hi, extra0=ex, extra1=ex))
                            else:
                                qt = tb
                                for kt in (range(0, 6) if qt == 0 else range(2, 8)):
                                    k0, k1 = ktok(kt)
                                    v0, v1 = vpair(2 + kt)
                                    j0 = 2 * kt - 8 * qt + 12
                                    rmask = AP(small, 128 + kt * 16 + qt * 8, [[256, 2], [1, 8], [0, 64]])
                                    exs = []
                                    for hh in range(2):
                                        hd = 2 * j + hh
                                        bias = AP(SBh[hd], j0 * 64, [[1600, 128], [-64, 8], [1, 64]])
                                        exs.append([(identb, bias, 0, 512, CR + [SBh_r[hd]]), (oh2, rmask, 0, 512, CR)])
                                    units.append(dict(k0=k0, k1=k1, v0=v0, v1=v1, lo=0, hi=512, extra0=exs[0], extra1=exs[1]))
                        else:
                            bb = qi
                            for tt in (2 * bb, 2 * bb + 1):
                                k0, k1 = ktok(tt)
                                v0, v1 = vpair(2 + tt)
                                units.append(dict(k0=k0, k1=k1, v0=v0, v1=v1, lo=0, hi=nq))
                        chunk = mi * 2 + j
                        dst = bufA[:, chunk, tok0:tok0 + nq]
                        dres = A_r[chunk][tok0 // 512]

                        def fin(o_ap, d_ap, o_rd, d_rd, dbank, mp=mp, j=j, nq=nq, dst=dst, dres=dres, held=held):
                            if m != "b":
                                if m == "c":
                                    rt = recip_ap(d_ap, d_rd, nq, col(lsm, 4 + l * 2 + j), [lsm_r])
                                else:
                                    rt = recip_ap(d_ap, d_rd, nq, None, [])
                                P.op("dve", lambda h: h.tensor_tensor(out=dst, in0=o_ap, in1=tmp[rt][:, 0:nq], op=ALU.mult),
                                     reads=o_rd + [tmp_r[rt]], writes=[dres])
                                return
                            rt = recip_ap(d_ap, d_rd, nq, None, [])
                            if mp == 0:
                                t3 = nT()
                                P.op("dve", lambda h: h.tensor_tensor(out=tmp[t3][:, 0:nq], in0=o_ap, in1=tmp[rt][:, 0:nq], op=ALU.mult),
                                     reads=o_rd + [tmp_r[rt]], writes=[tmp_r[t3]])
                                held["t3"] = t3
                                return
                            t3 = held["t3"]
                            t4 = nT()
                            P.op("dve", lambda h: h.scalar_tensor_tensor(
                                out=tmp[t4][:, 0:nq], in0=o_ap, scalar=col(lsm, l), in1=tmp[rt][:, 0:nq], op0=ALU.mult, op1=ALU.mult),
                                reads=o_rd + [tmp_r[rt], lsm_r], writes=[tmp_r[t4]])
                            P.op("dve", lambda h: h.tensor_tensor(out=tmp[t3][:, 0:nq], in0=tmp[t3][:, 0:nq], in1=tmp[t4][:, 0:nq], op=ALU.add),
                                 reads=[tmp_r[t3], tmp_r[t4]], writes=[tmp_r[t3]])
                            e = nE()
                            P.op("act", lambda h: h.activation(out=Et[e][:, 0:nq], in_=tmp[t3][:, 0:nq], func=AF.Square),
                                 reads=[tmp_r[t3]], writes=[Et_r[e]])
                            b2 = dbank
                            P.op("pe", lambda h: h.matmul(banks[b2][:, 0:nq], blockones, Et[e][:, 0:nq], start=True, stop=True),
                                 reads=[Et_r[e]] + CR, writes=[bank_r[b2]])
                            t5, t6 = nT(), nT()
                            P.op("act", lambda h: h.activation(out=tmp[t5][:, 0:nq], in_=banks[b2][:, 0:nq], func=AF.Ln,
                                                               scale=1.0 / HD, bias=col(cc, 3)),
                                 reads=[bank_r[b2]] + CR, writes=[tmp_r[t5]])
                            P.op("act", lambda h: h.activation(out=tmp[t6][:, 0:nq], in_=tmp[t5][:, 0:nq], func=AF.Exp, scale=-0.5),
                                 reads=[tmp_r[t5]], writes=[tmp_r[t6]])
                            P.op("dve", lambda h: h.scalar_tensor_tensor(
                                out=dst, in0=tmp[t3][:, 0:nq], scalar=col(lsm, 2 + l), in1=tmp[t6][:, 0:nq], op0=ALU.mult, op1=ALU.mult),
                                reads=[tmp_r[t3], tmp_r[t6], lsm_r], writes=[dres])

                        blocks.append(dict(qtile=qtile, qc0=qc0, nq=nq, units=units, fin=fin))
            run_blocks(blocks, scale, g == "p")

        def merge_wout(l, g, c):
            wl = wv(w_in, l)
            wb = w_branch[l].rearrange("(k p) n -> p k n", p=128)
            sbr = [0, 1]
            for q in range(2):
                load_w(wb[:, :, q * 512:(q + 1) * 512], sbr[q])
            for dc in range(8):
                sg_ = 2 + dc % 2
                for br in range(4):
                    c0 = GATE0 + br * 1024 + dc * 128
                    load_w(wl[:, :, c0:c0 + 128], sg_, cols=slice(br * 128, (br + 1) * 128))
                for tb in range(2):
                    ts = slice(tb * 512, (tb + 1) * 512)
                    acc = nT()
                    for br in range(4):
                        bg = nbank()
                        for k in range(8):
                            P.op("pe", lambda h, k=k, br=br, bg=bg: h.matmul(banks[bg][:], wslot[sg_][:, k, br * 128:(br + 1) * 128], hT[:, k, ts],
                                                                            start=(k == 0), stop=(k == 7)),
                                 reads=[wslot_r[sg_], hT_r[k][tb]], writes=[bank_r[bg]], signal=(k == 7))
                        bp = nbank()
                        for kc in range(2):
                            ch = br * 2 + kc
                            P.op("pe", lambda h, kc=kc, ch=ch, bp=bp: h.matmul(
                                banks[bp][:], wslot[sbr[dc // 4]][:, ch, (dc % 4) * 128:(dc % 4 + 1) * 128], bufA[:, ch, ts],
                                start=(kc == 0), stop=(kc == 1)),
                                reads=[wslot_r[sbr[dc // 4]], A_r[ch][tb]], writes=[bank_r[bp]], signal=(kc == 1))
                        tg = nT()
                        P.op("act", lambda h, tg=tg, bg=bg: h.activation(out=tmp[tg][:], in_=banks[bg][:], func=AF.Sigmoid),
                             reads=[bank_r[bg]], writes=[tmp_r[tg]])
                        if br == 0:
                            P.op("dve", lambda h, tg=tg, bp=bp, acc=acc: h.tensor_tensor(out=tmp[acc][:], in0=banks[bp][:], in1=tmp[tg][:], op=ALU.mult),
                                 reads=[bank_r[bp], tmp_r[tg]], writes=[tmp_r[acc]])
                        else:
                            P.op("dve", lambda h, tg=tg, bp=bp: h.tensor_tensor(out=tmp[tg][:], in0=banks[bp][:], in1=tmp[tg][:], op=ALU.mult),
                                 reads=[bank_r[bp], tmp_r[tg]], writes=[tmp_r[tg]])
                            if br < 3:
                                P.op("dve", lambda h, tg=tg, acc=acc: h.tensor_tensor(out=tmp[acc][:], in0=tmp[acc][:], in1=tmp[tg][:], op=ALU.add),
                                     reads=[tmp_r[acc], tmp_r[tg]], writes=[tmp_r[acc]])
                            else:
                                d = dc * 2 + tb
                                P.op("dve", lambda h, tg=tg, acc=acc, d=d: h.tensor_tensor(out=bufB[:, d, :], in0=tmp[acc][:], in1=tmp[tg][:], op=ALU.add),
                                     reads=[tmp_r[acc], tmp_r[tg]], writes=[B_r[d]])
            wo = wv(w_out, l)
            for q in range(2):
                s = nslot()
                load_w(wo[:, :, q * 512:(q + 1) * 512], s)
                for dcl in range(4):
                    dc = q * 4 + dcl
                    for tb in range(2):
                        ts = slice(tb * 512, (tb + 1) * 512)
                        b = nbank()
                        for k in range(8):
                            P.op("pe", lambda h, k=k, b=b, s=s, dcl=dcl, tb=tb: h.matmul(
                                banks[b][:], wslot[s][:, k, dcl * 128:(dcl + 1) * 128], bufB[:, k * 2 + tb, :], start=(k == 0), stop=(k == 7)),
                                reads=[wslot_r[s], B_r[k * 2 + tb]], writes=[bank_r[b]], signal=(k == 7))
                        P.op("dve", lambda h, b=b, dc=dc, ts=ts: h.scalar_tensor_tensor(
                            out=xT[:, dc, ts], in0=banks[b][:], scalar=modcol(l, 2, dc, c), in1=xT[:, dc, ts], op0=ALU.mult, op1=ALU.add),
                            reads=[bank_r[b], mod_r, xT_r[dc][tb]], writes=[xT_r[dc][tb]])

        def ffn_load(l, jb, w_):
            wi = wv(w_ffn_in, l)
            sa, su = nslot(), nslot()
            load_w(wi[:, :, jb * 512:jb * 512 + w_], sa, cols=slice(0, w_))
            load_w(wi[:, :, DFF + jb * 512:DFF + jb * 512 + w_], su, cols=slice(0, w_))
            return sa, su

        def ffn(l, g, c, pre=None):
            wi = wv(w_ffn_in, l)
            wo = w_ffn_out[l].rearrange("(f p) n -> p f n", p=128)
            groups_ = [(0, [(0, 512), (1, 512)]), (8, [(2, 512), (3, 512)]), (16, [(4, 512), (5, 256)])]
            for (f0, blocks) in groups_:
                nF = sum(w_ // 128 for _, w_ in blocks)
                for (jb, w_) in blocks:
                    if jb == 0 and pre is not None:
                        sa, su = pre
                    else:
                        sa, su = ffn_load(l, jb, w_)
                    for fl in range(w_ // 128):
                        fi = jb * 4 + fl - f0
                        for tb in range(2):
                            ts = slice(tb * 512, (tb + 1) * 512)
                            ba, bu = nbank(), nbank()
                            for (bk, sl) in ((ba, sa), (bu, su)):
                                for k in range(8):
                                    P.op("pe", lambda h, k=k, bk=bk, sl=sl, fl=fl, ts=ts: h.matmul(
                                        banks[bk][:], wslot[sl][:, k, fl * 128:(fl + 1) * 128], hT[:, k, ts], start=(k == 0), stop=(k == 7)),
                                        reads=[wslot_r[sl], hT_r[k][tb]], writes=[bank_r[bk]], signal=(k == 7))
                            t = nT()
                            P.op("act", lambda h, t=t, ba=ba: h.activation(out=tmp[t][:], in_=banks[ba][:], func=AF.Silu),
                                 reads=[bank_r[ba]], writes=[tmp_r[t]])
                            P.op("dve", lambda h, t=t, bu=bu, fi=fi, ts=ts: h.tensor_tensor(out=bufA[:, fi, ts], in0=banks[bu][:], in1=tmp[t][:], op=ALU.mult),
                                 reads=[bank_r[bu], tmp_r[t]], writes=[A_r[fi][tb]])
                for q in range(2):
                    s = nslot()
                    load_w(wo[:, f0:f0 + nF, q * 512:(q + 1) * 512], s, kk=slice(0, nF))
                    for dcl in range(4):
                        dc = q * 4 + dcl
                        for tb in range(2):
                            ts = slice(tb * 512, (tb + 1) * 512)
                            b = nbank()
                            for fi in range(nF):
                                P.op("pe", lambda h, fi=fi, b=b, s=s, dcl=dcl, ts=ts: h.matmul(
                                    banks[b][:], wslot[s][:, fi, dcl * 128:(dcl + 1) * 128], bufA[:, fi, ts], start=(fi == 0), stop=(fi == nF - 1)),
                                    reads=[wslot_r[s], A_r[fi][tb]], writes=[bank_r[b]], signal=(fi == nF - 1))
                            P.op("dve", lambda h, b=b, dc=dc, ts=ts: h.scalar_tensor_tensor(
                                out=xT[:, dc, ts], in0=banks[b][:], scalar=modcol(l, 5, dc, c), in1=xT[:, dc, ts], op0=ALU.mult, op1=ALU.add),
                                reads=[bank_r[b], mod_r, xT_r[dc][tb]], writes=[xT_r[dc][tb]])

        for g in groups:
            c = 0 if g == "s" else 1
            if not ada_started[0]:
                ada_issue(4)
            for tt in range(8 if "load" not in skip else 0):
                xs = rr["xs"] % 2
                rr["xs"] += 1
                P.dma("sp", lambda h, xs=xs, tt=tt: h.dma_start(out=xstage[xs][:], in_=x_in[g][tt * 128:(tt + 1) * 128, :]), writes=[xstage_r[xs]])
                for kh in range(2):
                    b = nbank()
                    for i in range(4):
                        k = kh * 4 + i
                        P.op("pe", lambda h, xs=xs, k=k, i=i, b=b: h.transpose(banks[b][:, i * 128:(i + 1) * 128], xstage[xs][:, k * 128:(k + 1) * 128], ident[:]),
                             reads=[xstage_r[xs]] + CR, writes=[bank_r[b]], signal=(i == 3))
                    eng = alt()
                    wr = [xT_r[kh * 4 + i][tt // 4] for i in range(4)]
                    dst = xT[:, kh * 4:(kh + 1) * 4, tt * 128:(tt + 1) * 128]
                    src = banks[b][:].rearrange("p (i t) -> p i t", i=4)
                    if eng == "act":
                        P.op("act", lambda h, dst=dst, src=src: h.activation(out=dst, in_=src, func=AF.Copy), reads=[bank_r[b]], writes=wr)
                    else:
                        P.op("dve", lambda h, dst=dst, src=src: h.tensor_copy(out=dst, in_=src), reads=[bank_r[b]], writes=wr)
            ada_compute()
            dump("xT", xT[:].rearrange("p k t -> p (k t)"), [r_ for rr_ in xT_r for r_ in rr_])
            for l in range(n_layers):
                pre_a = project_load("a", l) if "a" not in skip else None
                if "norm" not in skip:
                    norm_mod(l, c, 0)
                dump("hT", hT[:].rearrange("p k t -> p (k t)"), [r_ for rr_ in hT_r for r_ in rr_], q="pool")
                for m in "abcd":
                    if m in skip:
                        continue
                    project(m, l, g, pre_a if m == "a" else None)
                    ada_some(2 if m != "d" else 0)
                    if m == "a" and g == "s" and "d" not in skip:
                        build_SB(l)
                    attention(m, l, g)
                    ada_some(2 if m != "d" else 0)
                if l == 0:
                    while any(b_[0] == 0 for b_ in ada_pending):
                        ada_issue(2)
                        ada_compute()
                dump("brT", bufA[:].rearrange("p k t -> p (k t)"), [r_ for rr_ in A_r for r_ in rr_], q="pool")
                if "merge" not in skip:
                    merge_wout(l, g, c)
                dump("merged", bufB[:, 0:16, :].rearrange("p k t -> p (k t)"), B_r[0:16], q="pool")
                dump("x1", xT[:].rearrange("p k t -> p (k t)"), [r_ for rr_ in xT_r for r_ in rr_])
                pre_f = ffn_load(l, 0, 512) if "ffn" not in skip else None
                if "norm" not in skip:
                    norm_mod(l, c, 1)
                if "ffn" not in skip:
                    ffn(l, g, c, pre_f)
            for tt in range(8 if "final" not in skip else 0):
                xs = rr["xs"] % 2
                rr["xs"] += 1
                bs = []
                for kh in range(2):
                    b = nbank()
                    bs.append(b)
                    for i in range(4):
                        k = kh * 4 + i
                        P.op("pe", lambda h, k=k, i=i, b=b, tt=tt: h.transpose(banks[b][:, i * 128:(i + 1) * 128], xT[:, k, tt * 128:(tt + 1) * 128], ident[:]),
                             reads=[xT_r[k][tt // 4]] + CR, writes=[bank_r[b]], signal=(i == 3))
                    t = nT()
                    P.op("act", lambda h, b=b, t=t, kh=kh: h.activation(out=tmp[t][:], in_=banks[b][:], func=AF.Square, accum_out=ss_col[:, kh:kh + 1]),
                         reads=[bank_r[b]], writes=[tmp_r[t], ss_col_r])
                P.op("dve", lambda h: h.tensor_tensor(out=ss_col[:, 2:3], in0=ss_col[:, 0:1], in1=ss_col[:, 1:2], op=ALU.add),
                     reads=[ss_col_r], writes=[ss_col_r])
                P.op("act", lambda h: h.activation(out=ss_col[:, 3:4], in_=ss_col[:, 2:3], func=AF.Ln, scale=1.0 / D, bias=col(cc, 2)),
                     reads=[ss_col_r] + CR, writes=[ss_col_r])
                P.op("act", lambda h: h.activation(out=ss_col[:, 4:5], in_=ss_col[:, 3:4], func=AF.Exp, scale=-0.5), reads=[ss_col_r], writes=[ss_col_r])
                for kh in range(2):
                    b = bs[kh]
                    P.op("dve", lambda h, b=b, kh=kh, xs=xs: h.scalar_tensor_tensor(
                        out=xstage[xs][:, kh * 512:(kh + 1) * 512], in0=banks[b][:], scalar=ss_col[:, 4:5], in1=gfin_b[:, kh * 512:(kh + 1) * 512],
                        op0=ALU.mult, op1=ALU.mult),
                        reads=[bank_r[b], ss_col_r, gfin_r], writes=[xstage_r[xs]])
                P.dma("sp", lambda h, xs=xs, tt=tt: h.dma_start(out=y_out[g][tt * 128:(tt + 1) * 128, :], in_=xstage[xs][:]),
                      reads=[xstage_r[xs]], semres=xstage_r[xs])

        dump("lsm", lsm[:], [lsm_r])
        all_res = out_res + xstage_r + kst_r + vst_r
        P.finish(all_res)
        P.emit(nc, st)
    return nc


_NC_CACHE = {}


def _get_program():
    if "nc" not in _NC_CACHE:
        _NC_CACHE["nc"] = build_program()
    return _NC_CACHE["nc"]


def make_in_maps(x_prompt, x_sample, cache_a_k, cache_a_v, cache_b_k, cache_b_v, cache_c_k, cache_c_v,
                 cache_d_k, cache_d_v, c, c_ctx, w_ada, b_ada, g_norm1, w_in, g_q_a, g_k_a, lam_b, g_subln_b,
                 sink_c, rpb_d, w_branch, w_out, g_norm2, w_ffn_in, w_ffn_out, g_final):
    f = lambda a: np.ascontiguousarray(np.asarray(a, dtype=np.float32))
    consts = _consts()
    vecs = np.zeros((256, 128), np.float32)
    b_ada_, g1, g2, gf = f(b_ada), f(g_norm1), f(g_norm2), f(g_final)
    vecs[0:48] = b_ada_[0].reshape(48, 128)
    vecs[48:96] = b_ada_[1].reshape(48, 128)
    vecs[96:104] = g1[0].reshape(8, 128)
    vecs[104:112] = g1[1].reshape(8, 128)
    vecs[112:120] = g2[0].reshape(8, 128)
    vecs[120:128] = g2[1].reshape(8, 128)
    vecs[128:136] = gf.reshape(8, 128)
    gq, gk, gs_, sk = f(g_q_a), f(g_k_a), f(g_subln_b), f(sink_c)
    for l in range(DEPTH):
        vecs[136 + l] = np.tile(gq[l], 2)
        vecs[138 + l] = np.tile(gk[l], 2)
        vecs[140 + l] = np.tile(gs_[l], 2)
        for j in range(2):
            vecs[142 + l * 2 + j] = np.repeat(sk[l, 2 * j:2 * j + 2], 64)
    lamb = f(lam_b).reshape(1, 256)
    rpbx = _rpb_expand(f(rpb_d))
    shared = dict(
        w_ada=f(w_ada), w_in=f(w_in), w_branch=f(w_branch).reshape(DEPTH, 1024, D), w_out=f(w_out),
        w_ffn_in=f(w_ffn_in), w_ffn_out=f(w_ffn_out), vecs=vecs, lamb=lamb, rpbx=rpbx, **consts)
    xp, xs, c_, cctx = f(x_prompt), f(x_sample), f(c), f(c_ctx)
    caches = dict(a=(f(cache_a_k), f(cache_a_v)), b=(f(cache_b_k), f(cache_b_v)), c=(f(cache_c_k), f(cache_c_v)), d=(f(cache_d_k), f(cache_d_v)))
    in_maps = []
    for i in range(N_CORES):
        mp = dict(shared)
        mp["xs"] = xs[i]
        mp["xp"] = np.ascontiguousarray(xp[4 * i:4 * i + 4].reshape(NT, D))
        mp["cond"] = np.ascontiguousarray(np.stack([c_[i], cctx]).reshape(16, 128))
        for m in "abcd":
            mp["ck_" + m] = np.ascontiguousarray(caches[m][0][i])
            mp["cv_" + m] = np.ascontiguousarray(caches[m][1][i])
        in_maps.append(mp)
    return in_maps


def kernel(**inputs):
    nc = _get_program()
    in_maps = make_in_maps(**inputs)
    res = run_bass_kernel_spmd(nc, in_maps, core_ids=list(range(N_CORES)))
    R = res.results
    y_prompt = np.concatenate([r["yp"].reshape(4, 256, D) for r in R], axis=0).astype(np.float32)
    y_sample = np.stack([r["ys"] for r in R], axis=0).astype(np.float32)
    outs = [y_prompt, y_sample]
    for m in "abcd":
        outs.append(np.concatenate([r["nk_" + m] for r in R], axis=0).astype(np.float32))
        outs.append(np.concatenate([r["nv_" + m] for r in R], axis=0).astype(np.float32))
    return tuple(outs)
```

```python
import os
import math
import types
from contextlib import ExitStack

import numpy as np
import concourse.bass as bass
import concourse.mybir as mybir
from concourse.bass_utils import run_bass_kernel_spmd

F32 = mybir.dt.float32
BF16 = mybir.dt.bfloat16
AF = mybir.ActivationFunctionType
ALU = mybir.AluOpType
AX = mybir.AxisListType
AP = bass.AP

N_CORES = 8
D = 1024
NT = 1024
DEPTH = 2
HD = 64
DFF = 2816
IN_COLS = 6656
NEG = -30000.0
SAME_ENGINE_SYNC = 'raw'
NSTEP = int(os.environ.get('NSTEP', '9'))
USE_APPROX_RECIP = False

MIX = {
    "a": dict(q=0, k=256, v=384, kvh=2, vh=2, idx=0),
    "b": dict(q=512, k=768, v=1024, kvh=4, vh=4, idx=1),
    "c": dict(q=1280, k=1536, v=1664, kvh=2, vh=2, idx=2),
    "d": dict(q=1792, k=2048, v=2304, kvh=4, vh=4, idx=3),
}
GATE0 = 2560


class Res:
    __slots__ = ("name", "writer", "readers", "overlaps", "dsem", "dcount", "excl")

    def __init__(self, name, excl=False):
        self.name = name
        self.excl = excl
        self.writer = None
        self.readers = []
        self.overlaps = []
        self.dsem = None
        self.dcount = 0


class Eng:
    def __init__(self, name, in_order_safe=False):
        self.name = name
        self.sem = "eng_" + name
        self.seq = 0
        self.waited = {}
        self.ops = []
        self.in_order_safe = in_order_safe


class Prog:
    def __init__(self):
        self.engs = {
            "pe": Eng("pe", in_order_safe=True),
            "act": Eng("act"),
            "dve": Eng("dve"),
            "pool": Eng("pool"),
            "sp": Eng("sp"),
        }
        self.sem_names = set(e.sem for e in self.engs.values())
        self.n_ops = 0
        self.log = None

    def res(self, name, excl=False):
        return Res(name, excl)

    def _hazards(self, reads, writes):
        hz = []
        for r in reads:
            if r.writer is not None:
                hz.append(r.writer + (True,))
            if r.excl:
                hz.extend(t + (False,) for t in r.readers)
            for o in r.overlaps:
                if o.writer is not None:
                    hz.append(o.writer + (True,))
        for w in writes:
            if w.writer is not None:
                hz.append(w.writer + (False,))
            hz.extend(t + (False,) for t in w.readers)
            for o in w.overlaps:
                if o.writer is not None:
                    hz.append(o.writer + (False,))
                hz.extend(t + (False,) for t in o.readers)
        return hz

    def _waits_for(self, eng, hz):
        need = {}
        if SAME_ENGINE_SYNC == 'raw' and not eng.in_order_safe:
            own = [(v, r) for (s_, v, r) in hz if s_ == eng.sem]
            if any(r and eng.waited.get(eng.sem, 0) < v for (v, r) in own):
                hz = [h_ for h_ in hz if h_[0] != eng.sem] + [(eng.sem, max(v for v, _ in own), True)]
        for (sem, val, is_raw) in hz:
            if sem == eng.sem:
                if eng.in_order_safe or not SAME_ENGINE_SYNC:
                    continue
                if SAME_ENGINE_SYNC == 'raw' and not is_raw:
                    continue
            if eng.waited.get(sem, 0) >= val:
                continue
            if need.get(sem, 0) < val:
                need[sem] = val
        for sem, val in need.items():
            eng.waited[sem] = val
        return list(need.items())

    def op(self, engname, fn, reads=(), writes=(), signal=True):
        fn = _snapshot(fn)
        eng = self.engs[engname]
        waits = self._waits_for(eng, self._hazards(reads, writes))
        if signal:
            eng.seq += 1
            tok = (eng.sem, eng.seq)
        else:
            tok = (eng.sem, eng.seq + 1)
        sem = eng.sem

        def emit(h, S, waits=waits, fn=fn, signal=signal, sem=sem):
            for (s, v) in waits:
                h.wait_ge(S[s], v)
            inst = fn(h)
            if signal:
                inst.then_inc(S[sem], 1)

        eng.ops.append(emit)
        if self.log is not None:
            self.log.append((engname, tok, [r.name for r in reads], [w.name for w in writes], list(waits)))
        for r in reads:
            r.readers.append(tok)
            if len(r.readers) > 64:
                r.readers = _compact(r.readers)
        for w in writes:
            w.writer = tok
            w.readers = []
        self.n_ops += 1

    def dma(self, queue, fn, reads=(), writes=(), semres=None):
        fn = _snapshot(fn)
        eng = self.engs[queue]
        waits = self._waits_for(eng, self._hazards(reads, writes))
        sr = semres if semres is not None else (writes[0] if writes else reads[0])
        if sr.dsem is None:
            sr.dsem = "dma%d" % len(self.sem_names)
            self.sem_names.add(sr.dsem)
        sr.dcount += 16
        tok = (sr.dsem, sr.dcount)
        dsem = sr.dsem

        def emit(h, S, waits=waits, fn=fn, dsem=dsem):
            for (s, v) in waits:
                h.wait_ge(S[s], v)
            fn(h).then_inc(S[dsem], 16)

        eng.ops.append(emit)
        for r in reads:
            r.readers.append(tok)
            if len(r.readers) > 64:
                r.readers = _compact(r.readers)
        for w in writes:
            w.writer = tok
            w.readers = []
        self.n_ops += 1

    def finish(self, final_res):
        eng = self.engs["sp"]
        hz = []
        for r in final_res:
            if r.writer is not None:
                hz.append(r.writer + (True,))
            hz.extend(t + (True,) for t in r.readers)
        waits = self._waits_for(eng, hz)

        def emit(h, S, waits=waits):
            for (s, v) in waits:
                h.wait_ge(S[s], v)

        eng.ops.append(emit)

    def emit(self, nc, stack):
        S = {}
        for name in sorted(self.sem_names):
            S[name] = stack.enter_context(nc.semaphore(name))
        block = stack.enter_context(nc.Block())
        engs = self.engs

        @block.tensor
        def _(h):
            for f in engs["pe"].ops:
                f(h, S)

        @block.scalar
        def _(h):
            for f in engs["act"].ops:
                f(h, S)

        @block.vector
        def _(h):
            for f in engs["dve"].ops:
                f(h, S)

        @block.gpsimd
        def _(h):
            for f in engs["pool"].ops:
                f(h, S)

        @block.sync
        def _(h):
            for f in engs["sp"].ops:
                f(h, S)


def _snapshot(fn):
    if fn.__closure__ is None:
        return fn
    cells = []
    for c in fn.__closure__:
        try:
            cells.append(types.CellType(c.cell_contents))
        except ValueError:
            cells.append(c)
    return types.FunctionType(fn.__code__, fn.__globals__, fn.__name__, fn.__defaults__, tuple(cells))


def _compact(toks):
    best = {}
    for s, v in toks:
        if best.get(s, 0) < v:
            best[s] = v
    return list(best.items())


def _rope_tables():
    theta = np.float32(10000.0)
    t = np.arange(NT)
    row = (t // 64).astype(np.float32)
    col = (t % 64).astype(np.float32)
    inv16 = np.power(theta, -np.arange(16, dtype=np.float32) / np.float32(16)).astype(np.float32)
    inv8 = np.power(theta, -np.arange(8, dtype=np.float32) / np.float32(8)).astype(np.float32)
    cA = np.zeros((128, NT), np.float32); sA = np.zeros((128, NT), np.float32)
    cB = np.zeros((128, NT), np.float32); sB = np.zeros((128, NT), np.float32)
    pA = np.zeros((128, 128), np.float32); pB = np.zeros((128, 128), np.float32)
    for p in range(128):
        d = p % 64
        e = d % 32
        pos = row if d < 32 else col
        first = e < 16
        ang = (pos * inv16[e % 16]).astype(np.float32)
        cA[p] = np.cos(ang)
        sA[p] = -np.sin(ang) if first else np.sin(ang)
        partner = p + 16 if first else p - 16
        pA[partner, p] = 1.0
        e32 = d % 32
        posb = row if e32 < 16 else col
        eb = e32 % 16
        firstb = eb < 8
        angb = (posb * inv8[eb % 8]).astype(np.float32)
        cB[p] = np.cos(angb)
        sB[p] = -np.sin(angb) if firstb else np.sin(angb)
        partnerb = p + 8 if firstb else p - 8
        pB[partnerb, p] = 1.0
    return cA, sA, cB, sB, pA, pB


def _consts():
    cA, sA, cB, sB, pA, pB = _rope_tables()
    ident = np.eye(128, dtype=np.float32)
    blockones = np.zeros((128, 128), np.float32)
    blockones[:64, :64] = 1.0
    blockones[64:, 64:] = 1.0
    kl = np.arange(128)[:, None]
    ql = np.arange(128)[None, :]
    triA = np.where(kl <= ql, 0.0, NEG).astype(np.float32)
    triB = np.where(kl >= ql, 0.0, NEG).astype(np.float32)
    sq = np.stack([ident, pA, pB, blockones, triA, triB]).astype(np.float32)
    kc = np.arange(64)[:, None]
    qc = np.arange(64)[None, :]
    start_c = np.clip(qc - 8, 0, 48)
    colvalid = (kc >= start_c) & (kc < start_c + 16)
    cm = np.where(colvalid, 0.0, NEG).astype(np.float32)
    cmask = np.zeros((128, 25, 64), np.float32)
    cmask[:64] = cm[:, None, :]
    cmask[64:] = cm[:, None, :]
    cmask = cmask.reshape(128, 1600)
    small = np.zeros((2, 256), np.float32)
    small[0, 0:64] = 1.0
    small[1, 64:128] = 1.0
    for krl in range(2):
        for kt in range(8):
            for qr in range(16):
                st_ = min(max(qr - 4, 0), 8)
                kr = 2 * kt + krl
                ok = (kr >= st_) and (kr < st_ + 8)
                small[krl, 128 + kt * 16 + qr] = 0.0 if ok else NEG
    cc = np.zeros((128, 8), np.float32)
    for p in range(128):
        cc[p, 0] = 1.0 if (p % 64) < 32 else 0.0
        cc[p, 1] = 0.0 if (p % 64) < 32 else 1.0
    cc[:, 2] = 1e-6
    cc[:, 3] = 1e-5
    ropes = np.stack([cA, sA, cB, sB]).astype(np.float32)
    return dict(c_ident=ident, c_sq=sq, c_cmask=cmask, c_small=small, c_cc=cc, c_ropes=ropes)


def _rpb_expand(rpb_d):
    L, H = rpb_d.shape[0], rpb_d.shape[1]
    kc = np.arange(64)[:, None]
    qc = np.arange(64)[None, :]
    dc = np.clip(kc - qc, -15, 15) + 15
    out = np.zeros((L, H, 2, 64, 25, 64), np.float32)
    for krl in range(2):
        for j in range(25):
            dr = j - 12 + krl
            if -7 <= dr <= 7:
                out[:, :, krl, :, j, :] = rpb_d[:, :, dr + 7][:, :, dc]
    return out.reshape(L, H, 128, 1600)


def build_program(groups=("s", "p"), n_layers=DEPTH, dbg=None, skip=(), ada_layers=None, oplog=None):
    nc = bass.Bass("TRN2", target_bir_lowering=False)
    P = Prog()
    P.log = oplog
    st = ExitStack()
    with st:
        def din(name, shape):
            return nc.dram_tensor(name, list(shape), F32, kind="ExternalInput").ap()

        def dout(name, shape):
            return nc.dram_tensor(name, list(shape), F32, kind="ExternalOutput").ap()

        def sb(name, shape, dt=F32):
            name = "s_" + name
            return st.enter_context(nc.sbuf_tensor(name, list(shape), dt))

        x_in = {"s": din("xs", [NT, D]), "p": din("xp", [NT, D])}
        cond_d = din("cond", [16, 128])
        cache = {}
        for m in "abcd":
            H = MIX[m]["kvh"]
            cache[m] = (din("ck_" + m, [DEPTH, H, 256, HD]), din("cv_" + m, [DEPTH, H, 256, HD]))
        w_ada = din("w_ada", [DEPTH, D, 6 * D])
        w_in = din("w_in", [DEPTH, D, IN_COLS])
        w_branch = din("w_branch", [DEPTH, 4 * 256, D])
        w_out = din("w_out", [DEPTH, D, D])
        w_ffn_in = din("w_ffn_in", [DEPTH, D, 2 * DFF])
        w_ffn_out = din("w_ffn_out", [DEPTH, DFF, D])
        vecs_d = din("vecs", [256, 128])
        lamb_d = din("lamb", [1, 256])
        rpbx_d = din("rpbx", [DEPTH, 4, 128, 1600])
        c_ident = din("c_ident", [128, 128])
        c_sq = din("c_sq", [6, 128, 128])
        c_cmask = din("c_cmask", [128, 1600])
        c_small = din("c_small", [2, 256])
        c_cc = din("c_cc", [128, 8])
        c_ropes = din("c_ropes", [4, 128, NT])

        y_out = {"s": dout("ys", [NT, D]), "p": dout("yp", [NT, D])}
        nk_out, nv_out = {}, {}
        for m in "abcd":
            H = MIX[m]["kvh"]
            nk_out[m] = dout("nk_" + m, [4, DEPTH, H, 256, HD])
            nv_out[m] = dout("nv_" + m, [4, DEPTH, H, 256, HD])
        out_res = [P.res("out_all")]
        OUT = out_res[0]
        dbg_out = {}
        if dbg:
            for name, shape in dbg.items():
                dbg_out[name] = dout("dbg_" + name, shape)

        xT = sb("xT", [128, 8, NT]); xT_r = [[P.res("xT%d_%d" % (k, tb)) for tb in range(2)] for k in range(8)]
        hT = sb("hT", [128, 8, NT], BF16); hT_r = [[P.res("hT%d_%d" % (k, tb)) for tb in range(2)] for k in range(8)]
        bufA = sb("bufA", [128, 8, NT], BF16); A_r = [[P.res("A%d_%d" % (k, tb)) for tb in range(2)] for k in range(8)]
        bufB = sb("bufB", [128, 19, 512], BF16); B_r = [P.res("B%d" % i) for i in range(19)]
        NSLOT = 4
        wslot = [sb("wslot%d" % i, [128, 8, 512], BF16) for i in range(NSLOT)]
        wslot_r = [P.res("wslot%d" % i) for i in range(NSLOT)]
        ropes = sb("ropes", [128, 4, NT]); ropes_r = P.res("ropes")
        cmask = sb("cmask", [128, 1600], BF16)
        SBh = [sb("SBh%d" % h, [128, 1600], BF16) for h in range(4)]; SBh_r = [P.res("SBh%d" % h) for h in range(4)]
        NE = 6
        Et = [sb("Et%d" % i, [128, 512], BF16) for i in range(NE)]; Et_r = [P.res("Et%d" % i) for i in range(NE)]
        NTMP = 10
        tmp = [sb("tmp%d" % i, [128, 512]) for i in range(NTMP)]; tmp_r = [P.res("tmp%d" % i) for i in range(NTMP)]
        xstage = [sb("xstage%d" % i, [128, D]) for i in range(2)]; xstage_r = [P.res("xstage%d" % i) for i in range(2)]
        NKST = 4
        kst = [sb("kst%d" % i, [128, 512]) for i in range(NKST)]; kst_r = [P.res("kst%d" % i) for i in range(NKST)]
        NVST = 4
        vst = [sb("vst%d" % i, [128, 256]) for i in range(NVST)]; vst_r = [P.res("vst%d" % i) for i in range(NVST)]
        cst = sb("cst", [128, 2, 4, 64]); cst_r = P.res("cst")
        ident = sb("ident", [128, 128])
        sqb = sb("sqb", [128, 6, 128], BF16)
        ones_b = sb("ones_b", [128, 128], BF16)
        small = sb("small", [2, 256], BF16)
        cc = sb("cc", [128, 8])
        consts_r = P.res("consts")
        vec_in = sb("vec_in", [128, 2, 128])
        vecT = sb("vecT", [128, 256]); vecT_r = P.res("vecT")
        cond_sb = sb("cond_sb", [16, 128]); cond_r = P.res("cond_sb")
        csil = sb("csil", [16, 128]); csil_r = P.res("csil")
        scT = sb("scT", [128, 8, 2], BF16); scT_r = P.res("scT")
        mod = sb("mod", [128, DEPTH, 48, 2]); mod_r = P.res("mod")
        dm = sb("dm", [128, DEPTH, 2, 2, 8]); dm_r = P.res("dm")
        lamb = sb("lamb", [128, 256]); lamb_r = P.res("lamb")
        lsm = sb("lsm", [128, 16]); lsm_r = P.res("lsm")
        gfin_b = sb("gfin_b", [128, D]); gfin_r = P.res("gfin_b")
        ss_col = sb("ss_col", [128, 8]); ss_col_r = P.res("ss_col")

        banks = [st.enter_context(nc.psum_tensor("bank%d" % i, [128, 512], F32)) for i in range(8)]
        bank_r = [P.res("bank%d" % i, excl=True) for i in range(8)]

        rr = {"all": 0, "S": 0, "E": 0, "T": 0, "slot": 0, "xs": 0, "k": 0, "v": 0, "eng": 0}

        def nbank(pool="all"):
            if pool == "all":
                i = rr["all"] % 8
                rr["all"] += 1
            else:
                i = rr["S"] % 6
                rr["S"] += 1
            return i

        def nE():
            i = rr["E"] % NE
            rr["E"] += 1
            return i

        def nT():
            i = rr["T"] % NTMP
            rr["T"] += 1
            return i

        held_slots = set()

        def nslot():
            while True:
                i = rr["slot"] % NSLOT
                rr["slot"] += 1
                if i not in held_slots:
                    return i

        def alt():
            rr["eng"] += 1
            return "act" if rr["eng"] % 2 else "dve"

        def col(t, c):
            return t[:, c:c + 1]

        def load_w(src_ap, slot, cols=slice(0, 512), kk=slice(0, 8)):
            P.dma("pool", lambda h: h.dma_start(out=wslot[slot][:, kk, cols], in_=src_ap),
                  writes=[wslot_r[slot]])

        def wv(t3, l):
            return t3[l].rearrange("(k p) n -> p k n", p=128)

        def dump(name, src_ap, reads, q="sp"):
            if name in dbg_out and name not in dumped:
                dumped.add(name)
                P.dma(q, lambda h: h.dma_start(out=dbg_out[name], in_=src_ap), reads=reads, semres=OUT)
        dumped = set()

        P.dma("sp", lambda h: h.dma_start(out=ident[:], in_=c_ident), writes=[consts_r])
        P.dma("sp", lambda h: h.dma_start(out=cc[:], in_=c_cc), writes=[consts_r])
        P.dma("sp", lambda h: h.dma_start(out=vec_in[:], in_=vecs_d.rearrange("(a p) n -> p a n", p=128)), writes=[consts_r])
        P.dma("sp", lambda h: h.dma_start(out=cond_sb[:], in_=cond_d), writes=[cond_r])
        P.dma("sp", lambda h: h.dma_start(out=lamb[:], in_=AP(lamb_d.tensor, 0, [[0, 128], [1, 256]])), writes=[lamb_r])
        P.dma("sp", lambda h: h.dma_start(out=gfin_b[:], in_=AP(vecs_d.tensor, 128 * 128, [[0, 128], [1, D]])), writes=[gfin_r])
        constsp_r = P.res("constsp")
        P.dma("pool", lambda h: h.dma_start(out=sqb[:], in_=c_sq.rearrange("a p n -> p a n")), writes=[constsp_r])
        P.dma("pool", lambda h: h.dma_start(out=small[:], in_=c_small), writes=[constsp_r])
        for i in range(4):
            P.dma("pool", lambda h, i=i: h.dma_start(out=cmask[:, i * 400:(i + 1) * 400], in_=c_cmask[:, i * 400:(i + 1) * 400]),
                  writes=[constsp_r])
        if "s" in groups:
            P.dma("sp", lambda h: h.dma_start(out=ropes[:], in_=c_ropes.rearrange("a p n -> p a n")), writes=[ropes_r])
        ones_r = P.res("ones")
        P.op("dve", lambda h: h.memset(ones_b[:], 1.0), writes=[ones_r])
        identb = sqb[:, 0, :]
        permA = sqb[:, 1, :]
        permB = sqb[:, 2, :]
        blockones = sqb[:, 3, :]
        triA = sqb[:, 4, :]
        triB = sqb[:, 5, :]
        oh2 = small[0:2, 0:128]
        CR = [consts_r, constsp_r, ones_r]

        b = nbank()
        for a in range(2):
            P.op("pe", lambda h, a=a, b=b: h.transpose(banks[b][:, a * 128:(a + 1) * 128], vec_in[:, a, :], ident[:]),
                 reads=CR, writes=[bank_r[b]], signal=(a == 1))
        P.op("dve", lambda h, b=b: h.tensor_copy(out=vecT[:], in_=banks[b][:, 0:256]), reads=[bank_r[b]], writes=[vecT_r])
        V_BADA, V_G1, V_G2, V_GF, V_GQA, V_GKA, V_GSUB, V_SINK = 0, 96, 112, 128, 136, 138, 140, 142

        P.op("act", lambda h: h.activation(out=csil[:], in_=cond_sb[:], func=AF.Silu), reads=[cond_r], writes=[csil_r])
        b = nbank()
        P.op("pe", lambda h, b=b: h.transpose(banks[b][:, 0:16], csil[0:16, :], ident[0:16, 0:16]),
             reads=CR + [csil_r], writes=[bank_r[b]])
        P.op("dve", lambda h, b=b: h.tensor_copy(out=scT[:].rearrange("p k c -> p c k"), in_=banks[b][:, 0:16].rearrange("p (c k) -> p c k", c=2)),
             reads=[bank_r[b]], writes=[scT_r])

        ada_inflight = []

        def ada_issue(n):
            for _ in range(n):
                if not ada_pending or len(held_slots) >= NSLOT - 2 + (2 if not ada_started[0] else 0):
                    break
                l, cb = ada_pending.pop(0)
                s = nslot()
                held_slots.add(s)
                load_w(wv(w_ada, l)[:, :, cb * 512:(cb + 1) * 512], s)
                ada_inflight.append((l, cb, s))
            ada_started[0] = True

        def ada_compute():
            while ada_inflight:
                l, cb, s = ada_inflight.pop(0)
                ada_block(l, cb, s)
                held_slots.discard(s)

        def ada_block(l, cb, s):
            b = nbank()
            for ct in range(4):
                for k in range(8):
                    P.op("pe", lambda h, s=s, ct=ct, k=k, b=b: h.matmul(
                        banks[b][:, ct * 2:ct * 2 + 2], wslot[s][:, k, ct * 128:(ct + 1) * 128], scT[:, k, :],
                        start=(k == 0), stop=(k == 7)),
                        reads=[wslot_r[s], scT_r], writes=[bank_r[b]], signal=(k == 7 and ct == 3))
            P.op("dve", lambda h, l=l, b=b, cb=cb: h.tensor_tensor(
                out=mod[:, l, cb * 4:(cb + 1) * 4, :], in0=banks[b][:, 0:8].rearrange("p (j c) -> p j c", c=2),
                in1=AP(vecT, V_BADA + l * 48 + cb * 4, [[256, 128], [1, 4], [0, 2]]), op=ALU.add),
                reads=[bank_r[b], vecT_r], writes=[mod_r])
            if cb in (3, 9):
                which, j, gbase = (0, 1, V_G1) if cb == 3 else (1, 4, V_G2)
                for c in range(2):
                    P.op("dve", lambda h, l=l, c=c, which=which, j=j, gbase=gbase: h.scalar_tensor_tensor(
                        out=dm[:, l, c, which, :], in0=mod[:, l, j * 8:(j + 1) * 8, c], scalar=1.0,
                        in1=vecT[:, gbase + l * 8:gbase + (l + 1) * 8], op0=ALU.add, op1=ALU.mult),
                        reads=[mod_r, vecT_r], writes=[dm_r])

        ada_started = [False]
        ada_pending = [(l, cb) for l in range(n_layers if ada_layers is None else ada_layers) for cb in range(12)]

        def ada_some(n):
            ada_compute()
            ada_issue(n)

        def ada_flush():
            ada_compute()
            while ada_pending:
                ada_issue(2)
                ada_compute()


        if dbg:
            ada_flush()
        dump("mod", mod[:].rearrange("p l j c -> p (l j c)"), [mod_r])
        dump("vecT", vecT[:], [vecT_r])
        dump("dm", dm[:].rearrange("p l c w k -> p (l c w k)"), [dm_r])
        dump("scT", scT[:].rearrange("p k c -> p (k c)"), [scT_r], q="pool")

        def modcol(l, j, k, c):
            return mod[:, l, j * 8 + k, c:c + 1]

        for l in range(n_layers):
            lam_init = 0.8 - 0.6 * math.exp(-0.3 * l)
            for i in range(2):
                o = l * 128 + i * 64
                P.op("dve", lambda h, o=o, i=i: h.tensor_tensor(out=tmp[0][:, i * 32:(i + 1) * 32],
                                                              in0=lamb[:, o:o + 32], in1=lamb[:, o + 32:o + 64], op=ALU.mult),
                     reads=[lamb_r], writes=[tmp_r[0]])
                P.op("dve", lambda h, i=i: h.tensor_reduce(out=lsm[:, 8 + i:9 + i], in_=tmp[0][:, i * 32:(i + 1) * 32], axis=AX.X, op=ALU.add),
                     reads=[tmp_r[0]], writes=[lsm_r])
            P.op("act", lambda h: h.activation(out=lsm[:, 10:12], in_=lsm[:, 8:10], func=AF.Exp), reads=[lsm_r], writes=[lsm_r])
            P.op("dve", lambda h: h.tensor_tensor(out=lsm[:, 12:13], in0=lsm[:, 11:12], in1=lsm[:, 10:11], op=ALU.subtract),
                 reads=[lsm_r], writes=[lsm_r])
            P.op("dve", lambda h, l=l, li=lam_init: h.tensor_scalar(out=lsm[:, l:l + 1], in0=lsm[:, 12:13], scalar1=-li, scalar2=None, op0=ALU.add),
                 reads=[lsm_r], writes=[lsm_r])
            P.op("dve", lambda h, l=l, li=lam_init: h.tensor_scalar(out=lsm[:, 2 + l:3 + l], in0=vecT[:, V_GSUB + l:V_GSUB + l + 1],
                                                                   scalar1=(1.0 - li), scalar2=None, op0=ALU.mult),
                 reads=[vecT_r], writes=[lsm_r])
        P.op("act", lambda h: h.activation(out=lsm[:, 4:8], in_=vecT[:, V_SINK:V_SINK + 4], func=AF.Exp), reads=[vecT_r], writes=[lsm_r])

        def build_SB(l):
            for hh in range(4):
                for i in range(4):
                    P.dma("pool", lambda h, hh=hh, i=i: h.dma_start(out=SBh[hh][:, i * 400:(i + 1) * 400],
                                                                   in_=rpbx_d[l, hh][:, i * 400:(i + 1) * 400]), writes=[SBh_r[hh]])
            for hh in range(4):
                P.op("dve", lambda h, hh=hh: h.scalar_tensor_tensor(out=SBh[hh][:], in0=SBh[hh][:], scalar=8.0, in1=cmask[:],
                                                                  op0=ALU.mult, op1=ALU.add),
                     reads=[SBh_r[hh]] + CR, writes=[SBh_r[hh]])

        def rstd_from_bank(b, inv_n, eps_col):
            t1 = nT()
            out_t = nT()
            P.op("act", lambda h: h.activation(out=tmp[t1][:], in_=banks[b][:], func=AF.Ln, scale=inv_n, bias=col(cc, eps_col)),
                 reads=[bank_r[b]] + CR, writes=[tmp_r[t1]])
            P.op("act", lambda h: h.activation(out=tmp[out_t][:], in_=tmp[t1][:], func=AF.Exp, scale=-0.5),
                 reads=[tmp_r[t1]], writes=[tmp_r[out_t]])
            return out_t

        def norm_mod(l, c, which):
            jsh = 0 if which == 0 else 3
            for tb in range(2):
                ts = slice(tb * 512, (tb + 1) * 512)
                b = nbank()
                for k in range(8):
                    e = nE()
                    if k % 3 == 2:
                        P.op("act", lambda h, k=k, e=e: h.activation(out=Et[e][:], in_=xT[:, k, ts], func=AF.Square),
                             reads=[xT_r[k][tb]], writes=[Et_r[e]])
                    else:
                        P.op("pool" if k % 3 == 0 else "dve", lambda h, k=k, e=e: h.tensor_tensor(out=Et[e][:], in0=xT[:, k, ts], in1=xT[:, k, ts], op=ALU.mult),
                             reads=[xT_r[k][tb]], writes=[Et_r[e]])
                    P.op("pe", lambda h, k=k, e=e, b=b: h.matmul(banks[b][:], ones_b[:], Et[e][:], start=(k == 0), stop=(k == 7)),
                         reads=[Et_r[e]] + CR, writes=[bank_r[b]], signal=True)
                rs = rstd_from_bank(b, 1.0 / D, 2)
                for k in range(8):
                    t = nT()
                    P.op("dve", lambda h, k=k, t=t: h.tensor_tensor(out=tmp[t][:], in0=xT[:, k, ts], in1=tmp[rs][:], op=ALU.mult),
                         reads=[xT_r[k][tb], tmp_r[rs]], writes=[tmp_r[t]])
                    P.op("act", lambda h, k=k, t=t: h.activation(out=hT[:, k, ts], in_=tmp[t][:], func=AF.Identity,
                                                               scale=dm[:, l, c, which, k:k + 1], bias=modcol(l, jsh, k, c)),
                         reads=[tmp_r[t], dm_r, mod_r], writes=[hT_r[k][tb]])

        def QT(j, tb):
            return j * 2 + tb

        def KTtok(i, tb):
            return 4 + i * 2 + tb

        def KTctx_ap(i, p0, pn, c0, cn):
            return bufB[p0:p0 + pn, 12 + i // 2, (i % 2) * 256 + c0:(i % 2) * 256 + c0 + cn]

        def KTctx_r(i):
            return B_r[12 + i // 2]

        def V_ap(vt, c0, cn):
            return bufB[:, 14 + vt // 2, (vt % 2) * 256 + c0:(vt % 2) * 256 + c0 + cn]

        def V_r(vt):
            return B_r[14 + vt // 2]

        def post_qk(m, l, g, kind, j, tb, slot, s1, cbase):
            rope = (g == "s") and m in "abc"
            normed = (m == "a")
            nb, nt, ne, _ = qk_slot_plan(m, g)
            bl = [slot * nb + i for i in range(nb)]
            tl = [slot * nt + i for i in range(nt)]
            el = [slot * ne + i for i in range(ne)]
            b = bl.pop(0)
            b2 = bl.pop(0) if normed else None
            b3 = bl.pop(0) if rope else None
            b4 = bl.pop(0) if g == "p" else None
            t0 = tl.pop(0) if normed else None
            rs = tl.pop(0) if normed else None
            t1 = tl.pop(0) if (rope or normed) else None
            t2 = tl.pop(0) if rope else None
            t4 = tl.pop(0) if (g == "p" and not normed) else None
            e = el.pop(0) if normed else None
            e2 = el.pop(0) if rope else None
            for k in range(8):
                P.op("pe", lambda h, k=k: h.matmul(
                    banks[b][:], wslot[s1][:, k, cbase + j * 128:cbase + (j + 1) * 128], hT[:, k, tb * 512:(tb + 1) * 512],
                    start=(k == 0), stop=(k == 7)),
                    reads=[wslot_r[s1], hT_r[k][tb]], writes=[bank_r[b]], signal=(k == 7))
            yield
            ts = slice(tb * 512, (tb + 1) * 512)
            is_k = (kind == "k")
            need_f32 = is_k and g == "p"
            gcol = None
            if normed:
                gcol = col(vecT, (V_GKA if is_k else V_GQA) + l)
            if rope:
                ci, si, perm = (0, 1, permA) if m in "ac" else (2, 3, permB)
            if normed:
                P.op("act", lambda h: h.activation(out=Et[e][:], in_=banks[b][:], func=AF.Square), reads=[bank_r[b]], writes=[Et_r[e]])
            if rope:
                if normed:
                    P.op("act", lambda h: h.activation(out=Et[e2][:], in_=banks[b][:], func=AF.Copy, scale=gcol),
                         reads=[bank_r[b], vecT_r], writes=[Et_r[e2]])
                else:
                    P.op("act", lambda h: h.activation(out=Et[e2][:], in_=banks[b][:], func=AF.Copy),
                         reads=[bank_r[b]], writes=[Et_r[e2]])
            if normed or rope:
                yield
            if normed:
                P.op("pe", lambda h: h.matmul(banks[b2][:], blockones, Et[e][:], start=True, stop=True),
                     reads=[Et_r[e]] + CR, writes=[bank_r[b2]])
            if rope:
                P.op("pe", lambda h: h.matmul(banks[b3][:], perm, Et[e2][:], start=True, stop=True),
                     reads=[Et_r[e2]] + CR, writes=[bank_r[b3]])
                if normed:
                    P.op("dve", lambda h: h.scalar_tensor_tensor(out=tmp[t1][:], in0=banks[b][:], scalar=gcol, in1=ropes[:, ci, ts],
                                                               op0=ALU.mult, op1=ALU.mult),
                         reads=[bank_r[b], vecT_r, ropes_r], writes=[tmp_r[t1]])
                else:
                    P.op("dve", lambda h: h.tensor_tensor(out=tmp[t1][:], in0=banks[b][:], in1=ropes[:, ci, ts], op=ALU.mult),
                         reads=[bank_r[b], ropes_r], writes=[tmp_r[t1]])
            if normed or rope:
                yield
            if normed:
                P.op("act", lambda h: h.activation(out=tmp[t0][:], in_=banks[b2][:], func=AF.Ln, scale=1.0 / HD, bias=col(cc, 2)),
                     reads=[bank_r[b2]] + CR, writes=[tmp_r[t0]])
                P.op("act", lambda h: h.activation(out=tmp[rs][:], in_=tmp[t0][:], func=AF.Exp, scale=-0.5),
                     reads=[tmp_r[t0]], writes=[tmp_r[rs]])
            if rope:
                P.op("dve", lambda h: h.tensor_tensor(out=tmp[t2][:], in0=banks[b3][:], in1=ropes[:, si, ts], op=ALU.mult),
                     reads=[bank_r[b3], ropes_r], writes=[tmp_r[t2]])
                P.op("dve", lambda h: h.tensor_tensor(out=tmp[t1][:], in0=tmp[t1][:], in1=tmp[t2][:], op=ALU.add),
                     reads=[tmp_r[t1], tmp_r[t2]], writes=[tmp_r[t1]])
            if normed:
                yield
                if rope:
                    P.op("dve", lambda h: h.tensor_tensor(out=tmp[t1][:], in0=tmp[t1][:], in1=tmp[rs][:], op=ALU.mult),
                         reads=[tmp_r[t1], tmp_r[rs]], writes=[tmp_r[t1]])
                else:
                    P.op("dve", lambda h: h.scalar_tensor_tensor(out=tmp[t1][:], in0=banks[b][:], scalar=gcol, in1=tmp[rs][:],
                                                               op0=ALU.mult, op1=ALU.mult),
                         reads=[bank_r[b], vecT_r, tmp_r[rs]], writes=[tmp_r[t1]])
            if normed or rope:
                fin = (tmp[t1], tmp_r[t1])
            elif need_f32:
                P.op("act", lambda h: h.activation(out=tmp[t4][:], in_=banks[b][:], func=AF.Copy), reads=[bank_r[b]], writes=[tmp_r[t4]])
                fin = (tmp[t4], tmp_r[t4])
            else:
                fin = (banks[b], bank_r[b])
            yield
            ft, fr = fin
            if not is_k:
                d = QT(j, tb)
                eng = alt()
                if eng == "act":
                    P.op("act", lambda h: h.activation(out=bufB[:, d, :], in_=ft[:], func=AF.Copy), reads=[fr], writes=[B_r[d]])
                else:
                    P.op("dve", lambda h: h.tensor_copy(out=bufB[:, d, :], in_=ft[:]), reads=[fr], writes=[B_r[d]])
            elif m == "b":
                for mp in range(2):
                    d = KTtok(mp * 2 + j, tb)
                    eng = "act" if mp == 0 else "dve"
                    if eng == "act":
                        P.op("act", lambda h, mp=mp, d=d: h.activation(out=bufB[:, d, :], in_=ft[:], func=AF.Copy, scale=col(cc, mp)),
                             reads=[fr] + CR, writes=[B_r[d]])
                    else:
                        P.op("dve", lambda h, mp=mp, d=d: h.tensor_scalar(out=bufB[:, d, :], in0=ft[:], scalar1=col(cc, mp), scalar2=None,
                                                                        op0=ALU.mult),
                             reads=[fr] + CR, writes=[B_r[d]])
            else:
                d = KTtok(j, tb)
                eng = alt()
                if eng == "act":
                    P.op("act", lambda h: h.activation(out=bufB[:, d, :], in_=ft[:], func=AF.Copy), reads=[fr], writes=[B_r[d]])
                else:
                    P.op("dve", lambda h: h.tensor_copy(out=bufB[:, d, :], in_=ft[:]), reads=[fr], writes=[B_r[d]])
            if need_f32:
                dup = m in "ac"
                wd = 64 if dup else 128
                for i in range(4):
                    if dup:
                        P.op("pe", lambda h, i=i: h.transpose(banks[b4][:, i * 64:(i + 1) * 64], ft[0:64, i * 128:(i + 1) * 128], ident[0:64, 0:64]),
                             reads=[fr] + CR, writes=[bank_r[b4]], signal=(i == 3))
                    else:
                        P.op("pe", lambda h, i=i: h.transpose(banks[b4][:, i * 128:(i + 1) * 128], ft[:, i * 128:(i + 1) * 128], ident[:]),
                             reads=[fr] + CR, writes=[bank_r[b4]], signal=(i == 3))
                yield
                ks = rr["k"] % NKST
                rr["k"] += 1
                P.op("dve", lambda h: h.tensor_copy(out=kst[ks][:, 0:4 * wd], in_=banks[b4][:, 0:4 * wd]), reads=[bank_r[b4]], writes=[kst_r[ks]])
                for i in range(4):
                    tt = tb * 4 + i
                    bb, s0 = tt // 2, (tt % 2) * 128
                    if dup:
                        dst = nk_out[m][bb, l, j, s0:s0 + 128, :]
                        src = kst[ks][:, i * 64:(i + 1) * 64]
                    else:
                        dst = nk_out[m][bb, l, 2 * j:2 * j + 2, s0:s0 + 128, :].rearrange("h s d -> s h d")
                        src = kst[ks][:, i * 128:(i + 1) * 128].rearrange("p (h d) -> p h d", h=2)
                    P.dma("sp", lambda h, dst=dst, src=src: h.dma_start(out=dst, in_=src), reads=[kst_r[ks]], semres=kst_r[ks])

        def qk_slot_plan(m, g):
            rope = (g == "s") and m in "abc"
            normed = (m == "a")
            nb = 1 + (1 if normed else 0) + (1 if rope else 0) + (1 if g == "p" else 0)
            nt = (2 if normed else 0) + (2 if rope else (1 if normed else 0)) + (1 if (g == "p" and not normed) else 0)
            ne = (1 if normed else 0) + (1 if rope else 0)
            ns = min(4, 8 // nb)
            if nt:
                ns = min(ns, NTMP // nt)
            if ne:
                ns = min(ns, NE // ne)
            return nb, nt, ne, ns

        def run_pipelined(factories, depth=2):
            active = {}
            todo = list(factories)
            while todo or active:
                for slot in range(depth):
                    if slot not in active and todo:
                        active[slot] = todo.pop(0)(slot)
                for slot in sorted(active):
                    try:
                        next(active[slot])
                    except StopIteration:
                        del active[slot]

        def project_load(m, l):
            mx = MIX[m]
            dup = m in "ac"
            vw = mx["vh"] * 64
            wl = wv(w_in, l)
            s1 = nslot()
            load_w(wl[:, :, mx["q"]:mx["q"] + 256], s1, cols=slice(0, 256))
            if dup:
                for jj in range(2):
                    for r in range(2):
                        c0 = 256 + jj * 128 + r * 64
                        load_w(wl[:, :, mx["k"] + jj * 64:mx["k"] + (jj + 1) * 64], s1, cols=slice(c0, c0 + 64))
            else:
                load_w(wl[:, :, mx["k"]:mx["k"] + 256], s1, cols=slice(256, 512))
            s2 = nslot()
            load_w(wl[:, :, mx["v"]:mx["v"] + vw], s2, cols=slice(0, vw))
            return s1, s2

        def project(m, l, g, pre=None):
            mx = MIX[m]
            dup = m in "ac"
            vw = mx["vh"] * 64
            s1, s2 = pre if pre is not None else project_load(m, l)
            if g == "s":
                ck, cv = cache[m]
                H = mx["kvh"]
                for c in range(2):
                    P.dma("pool", lambda h, c=c: h.dma_start(
                        out=V_ap(c, 0, H * 64).rearrange("p (h d) -> p h d", h=H),
                        in_=cv[l, :, c * 128:(c + 1) * 128, :].rearrange("h s d -> s h d")), writes=[V_r(c)])
                for c in range(2):
                    if dup:
                        for hh in range(4):
                            P.dma("sp", lambda h, c=c, hh=hh: h.dma_start(out=cst[:, c, hh, :], in_=ck[l, hh // 2, c * 128:(c + 1) * 128, :]),
                                  writes=[cst_r])
                    else:
                        P.dma("sp", lambda h, c=c: h.dma_start(out=cst[:, c, :, :], in_=ck[l, :, c * 128:(c + 1) * 128, :].rearrange("h s d -> s h d")),
                              writes=[cst_r])
                for j in range(2):
                    b = nbank()
                    for c in range(2):
                        P.op("pe", lambda h, c=c, j=j, b=b: h.transpose(
                            banks[b][:, c * 128:(c + 1) * 128], cst[:, c, 2 * j:2 * j + 2, :].rearrange("p h d -> p (h d)"), ident[:]),
                            reads=[cst_r] + CR, writes=[bank_r[b]], signal=(c == 1))
                    if m == "b":
                        for mp in range(2):
                            i = mp * 2 + j
                            P.op("dve", lambda h, i=i, mp=mp, b=b: h.tensor_scalar(out=KTctx_ap(i, 0, 128, 0, 256), in0=banks[b][:, 0:256],
                                                                                 scalar1=col(cc, mp), scalar2=None, op0=ALU.mult),
                                 reads=[bank_r[b]] + CR, writes=[KTctx_r(i)])
                    else:
                        P.op("dve", lambda h, j=j, b=b: h.tensor_copy(out=KTctx_ap(j, 0, 128, 0, 256), in_=banks[b][:, 0:256]),
                             reads=[bank_r[b]], writes=[KTctx_r(j)])
            for tt in range(8):
                b = nbank()
                for k in range(8):
                    P.op("pe", lambda h, k=k, tt=tt, b=b: h.matmul(
                        banks[b][:, 0:vw], hT[:, k, tt * 128:(tt + 1) * 128], wslot[s2][:, k, 0:vw], start=(k == 0), stop=(k == 7)),
                        reads=[wslot_r[s2], hT_r[k][tt // 4]], writes=[bank_r[b]], signal=(k == 7))
                vt = 2 + tt
                P.op("dve", lambda h, vt=vt, b=b: h.tensor_copy(out=V_ap(vt, 0, vw), in_=banks[b][:, 0:vw]), reads=[bank_r[b]], writes=[V_r(vt)])
                if g == "p":
                    vs = rr["v"] % NVST
                    rr["v"] += 1
                    P.op("act", lambda h, vs=vs, b=b: h.activation(out=vst[vs][:, 0:vw], in_=banks[b][:, 0:vw], func=AF.Copy),
                         reads=[bank_r[b]], writes=[vst_r[vs]])
                    bb, s0 = tt // 2, (tt % 2) * 128
                    H = mx["vh"]
                    P.dma("sp", lambda h, vs=vs, bb=bb, s0=s0, H=H: h.dma_start(
                        out=nv_out[m][bb, l, :, s0:s0 + 128, :].rearrange("h s d -> s h d"),
                        in_=vst[vs][:, 0:H * 64].rearrange("p (h d) -> p h d", h=H)),
                        reads=[vst_r[vs]], semres=vst_r[vs])
            gens = []
            cnt_ = 0
            for kind, cbase in (("q", 0), ("k", 256)):
                for j in range(2):
                    for tb in range(2):
                        gens.append(lambda slot, kind=kind, j=j, tb=tb, cbase=cbase: post_qk(m, l, g, kind, j, tb, slot, s1, cbase))
            run_pipelined(gens, qk_slot_plan(m, g)[3])

        def recip_ap(d_ap, d_reads, nq, bias_ap, bias_reads):
            t1, t2 = nT(), nT()
            if nq <= 256:
                if bias_ap is None:
                    P.op("dve", lambda h: h.reciprocal(out=tmp[t2][:, 0:nq], in_=d_ap), reads=d_reads, writes=[tmp_r[t2]])
                else:
                    P.op("dve", lambda h: h.tensor_scalar(out=tmp[t1][:, 0:nq], in0=d_ap, scalar1=bias_ap, scalar2=None, op0=ALU.add),
                         reads=d_reads + bias_reads, writes=[tmp_r[t1]])
                    P.op("dve", lambda h: h.reciprocal(out=tmp[t2][:, 0:nq], in_=tmp[t1][:, 0:nq]), reads=[tmp_r[t1]], writes=[tmp_r[t2]])
                return t2
            if USE_APPROX_RECIP:
                if bias_ap is None:
                    P.op("dve", lambda h: h.tensor_copy(out=tmp[t1][:, 0:nq], in_=d_ap), reads=d_reads, writes=[tmp_r[t1]])
                else:
                    P.op("dve", lambda h: h.tensor_scalar(out=tmp[t1][:, 0:nq], in0=d_ap, scalar1=bias_ap, scalar2=None, op0=ALU.add),
                         reads=d_reads + bias_reads, writes=[tmp_r[t1]])
                P.op("dve", lambda h: h.reciprocal_approx_fast(out=tmp[t2][:, 0:nq], in_=tmp[t1][:, 0:nq]), reads=[tmp_r[t1]], writes=[tmp_r[t2]])
                return t2
            if bias_ap is None:
                P.op("act", lambda h: h.activation(out=tmp[t1][:, 0:nq], in_=d_ap, func=AF.Ln), reads=d_reads, writes=[tmp_r[t1]])
            else:
                P.op("act", lambda h: h.activation(out=tmp[t1][:, 0:nq], in_=d_ap, func=AF.Ln, bias=bias_ap),
                     reads=d_reads + bias_reads, writes=[tmp_r[t1]])
            P.op("act", lambda h: h.activation(out=tmp[t2][:, 0:nq], in_=tmp[t1][:, 0:nq], func=AF.Exp, scale=-1.0),
                 reads=[tmp_r[t1]], writes=[tmp_r[t2]])
            return t2

        def run_blocks(blocks, scale, prompt):
            stream = [(bi, ui) for bi, blk in enumerate(blocks) for ui in range(len(blk["units"]))]
            for bi, blk in enumerate(blocks):
                if prompt:
                    blk["ob"] = blk["db"] = 6 + bi % 2
                    blk["oc"], blk["dcol"] = 0, 256
                else:
                    blk["ob"], blk["db"] = 6, 7
                    blk["oc"], blk["dcol"] = 0, 0
            sbk = {}
            PD = 2

            def QK(idx):
                bi, ui = stream[idx]
                blk = blocks[bi]
                U = blk["units"][ui]
                qtile, qc0 = blk["qtile"], blk["qc0"]
                lo, hi = U["lo"], U["hi"]
                pair = []
                for hh in range(2):
                    b = nbank("S")
                    pair.append(b)
                    kap, kr = U["k%d" % hh]
                    ex = U.get("extra%d" % hh, [])
                    P.op("pe", lambda h, b=b, kap=kap, hh=hh: h.matmul(
                        banks[b][:, lo:hi], kap, bufB[hh * 64:(hh + 1) * 64, qtile, qc0 + lo:qc0 + hi],
                        start=True, stop=(len(ex) == 0), skip_group_check=True),
                        reads=[kr, B_r[qtile]], writes=[bank_r[b]], signal=(len(ex) == 0))
                    for xi, (lt, rh, c0, c1, rds) in enumerate(ex):
                        last = xi == len(ex) - 1
                        P.op("pe", lambda h, b=b, lt=lt, rh=rh, c0=c0, c1=c1, last=last: h.matmul(
                            banks[b][:, c0:c1], lt, rh, start=False, stop=last, skip_group_check=True),
                            reads=rds, writes=[bank_r[b]], signal=last)
                sbk[idx] = pair

            for idx in range(min(PD, len(stream))):
                QK(idx)
            pending_fin = []
            for idx, (bi, ui) in enumerate(stream):
                if idx + PD < len(stream):
                    QK(idx + PD)
                blk = blocks[bi]
                U = blk["units"][ui]
                n = len(blk["units"])
                lo, hi = U["lo"], U["hi"]
                ob, db, oc, dcol = blk["ob"], blk["db"], blk["oc"], blk["dcol"]
                es = []
                for hh in range(2):
                    e = nE()
                    es.append(e)
                    b = sbk[idx][hh]
                    P.op("act", lambda h, e=e, b=b: h.activation(out=Et[e][:, lo:hi], in_=banks[b][:, lo:hi], func=AF.Exp, scale=scale),
                         reads=[bank_r[b]], writes=[Et_r[e]])
                del sbk[idx]
                while pending_fin:
                    pending_fin.pop(0)()
                for hh in range(2):
                    e = es[hh]
                    vap, vr = U["v%d" % hh]
                    P.op("pe", lambda h, e=e, vap=vap, hh=hh: h.matmul(
                        banks[ob][hh * 64:(hh + 1) * 64, oc + lo:oc + hi], vap, Et[e][:, lo:hi], start=(ui == 0), stop=(ui == n - 1),
                        skip_group_check=True),
                        reads=[vr, Et_r[es[0]], Et_r[es[1]], bank_r[db]], writes=[bank_r[ob]], signal=False)
                dstart = (ui == 0) and (db != ob)
                for hh in range(2):
                    e = es[hh]
                    P.op("pe", lambda h, e=e, hh=hh: h.matmul(
                        banks[db][hh * 64:(hh + 1) * 64, dcol + lo:dcol + hi], ones_b[:, 0:64], Et[e][:, lo:hi], start=dstart, stop=(ui == n - 1),
                        skip_group_check=True),
                        reads=[Et_r[e]] + CR, writes=[bank_r[db]], signal=(hh == 1))
                if ui == n - 1:
                    nq = blk["nq"]
                    rds = [bank_r[ob]] if ob == db else [bank_r[ob], bank_r[db]]
                    pending_fin.append(lambda blk=blk, ob=ob, db=db, oc=oc, dcol=dcol, nq=nq: blk["fin"](
                        banks[ob][:, oc:oc + nq], banks[db][:, dcol:dcol + nq], [bank_r[ob]], [bank_r[db]], db))
            while pending_fin:
                pending_fin.pop(0)()

        def attention(m, l, g):
            mx = MIX[m]
            mi = mx["idx"]
            dup = m in "ac"
            scale = (32.0 ** -0.5) if m == "b" else (64.0 ** -0.5)
            if g == "s":
                qblocks = [(tb, 0, 512, tb * 512) for tb in range(2)]
            else:
                qblocks = [(bb // 2, (bb % 2) * 256, 256, bb * 256) for bb in range(4)]
            blocks = []
            for j in range(2):
                for qi, (tb, qc0, nq, tok0) in enumerate(qblocks):
                    qtile = QT(j, tb)
                    held = {}
                    for mp in range(2 if m == "b" else 1):
                        ki = (mp * 2 + j) if m == "b" else j
                        units = []

                        def vpair(vt):
                            if dup:
                                a0 = (V_ap(vt, j * 64, 64), V_r(vt))
                                return a0, a0
                            return (V_ap(vt, (2 * j) * 64, 64), V_r(vt)), (V_ap(vt, (2 * j + 1) * 64, 64), V_r(vt))

                        def ktok(tt):
                            d = KTtok(ki, tt // 4)
                            c = (tt % 4) * 128
                            return (bufB[0:64, d, c:c + 128], B_r[d]), (bufB[64:128, d, c:c + 128], B_r[d])

                        if g == "s":
                            for c in range(2):
                                v0, v1 = vpair(c)
                                units.append(dict(k0=(KTctx_ap(ki, 0, 64, c * 128, 128), KTctx_r(ki)),
                                                  k1=(KTctx_ap(ki, 64, 64, c * 128, 128), KTctx_r(ki)), v0=v0, v1=v1, lo=0, hi=512))
                            if m in "ab":
                                for tt in range(8):
                                    k0, k1 = ktok(tt)
                                    v0, v1 = vpair(2 + tt)
                                    units.append(dict(k0=k0, k1=k1, v0=v0, v1=v1, lo=0, hi=512))
                            elif m == "c":
                                for tt in range(8):
                                    blo, bhi = max(tt - 1, 4 * tb), min(tt + 1, 4 * tb + 3)
                                    if blo > bhi:
                                        continue
                                    k0, k1 = ktok(tt)
                                    v0, v1 = vpair(2 + tt)
                                    lo, hi = (blo - 4 * tb) * 128, (bhi + 1 - 4 * tb) * 128
                                    ex = []
                                    if blo <= tt - 1 <= bhi:
                                        c0 = (tt - 1 - 4 * tb) * 128
                                        ex.append((identb, triA, c0, c0 + 128, CR))
                                    if blo <= tt + 1 <= bhi:
                                        c0 = (tt + 1 - 4 * tb) * 128
                                        ex.append((identb, triB, c0, c0 + 128, CR))
                                    units.append(dict(k0=k0, k1=k1, v0=v0, v1=v1, lo=lo, hi=hi, extra0=ex, extra1=ex))
                            else:
                                qt = tb
                                for kt in (range(0, 6) if qt == 0 else range(2, 8)):
                                    k0, k1 = ktok(kt)
                                    v0, v1 = vpair(2 + kt)
                                    j0 = 2 * kt - 8 * qt + 12
                                    rmask = AP(small, 128 + kt * 16 + qt * 8, [[256, 2], [1, 8], [0, 64]])
                                    exs = []
                                    for hh in range(2):
                                        hd = 2 * j + hh
                                        bias = AP(SBh[hd], j0 * 64, [[1600, 128], [-64, 8], [1, 64]])
                                        exs.append([(identb, bias, 0, 512, CR + [SBh_r[hd]]), (oh2, rmask, 0, 512, CR)])
                                    units.append(dict(k0=k0, k1=k1, v0=v0, v1=v1, lo=0, hi=512, extra0=exs[0], extra1=exs[1]))
                        else:
                            bb = qi
                            for tt in (2 * bb, 2 * bb + 1):
                                k0, k1 = ktok(tt)
                                v0, v1 = vpair(2 + tt)
                                units.append(dict(k0=k0, k1=k1, v0=v0, v1=v1, lo=0, hi=nq))
                        chunk = mi * 2 + j
                        dst = bufA[:, chunk, tok0:tok0 + nq]
                        dres = A_r[chunk][tok0 // 512]

                        def fin(o_ap, d_ap, o_rd, d_rd, dbank, mp=mp, j=j, nq=nq, dst=dst, dres=dres, held=held):
                            if m != "b":
                                if m == "c":
                                    rt = recip_ap(d_ap, d_rd, nq, col(lsm, 4 + l * 2 + j), [lsm_r])
                                else:
                                    rt = recip_ap(d_ap, d_rd, nq, None, [])
                                P.op("dve", lambda h: h.tensor_tensor(out=dst, in0=o_ap, in1=tmp[rt][:, 0:nq], op=ALU.mult),
                                     reads=o_rd + [tmp_r[rt]], writes=[dres])
                                return
                            rt = recip_ap(d_ap, d_rd, nq, None, [])
                            if mp == 0:
                                t3 = nT()
                                P.op("dve", lambda h: h.tensor_tensor(out=tmp[t3][:, 0:nq], in0=o_ap, in1=tmp[rt][:, 0:nq], op=ALU.mult),
                                     reads=o_rd + [tmp_r[rt]], writes=[tmp_r[t3]])
                                held["t3"] = t3
                                return
                            t3 = held["t3"]
                            t4 = nT()
                            P.op("dve", lambda h: h.scalar_tensor_tensor(
                                out=tmp[t4][:, 0:nq], in0=o_ap, scalar=col(lsm, l), in1=tmp[rt][:, 0:nq], op0=ALU.mult, op1=ALU.mult),
                                reads=o_rd + [tmp_r[rt], lsm_r], writes=[tmp_r[t4]])
                            P.op("dve", lambda h: h.tensor_tensor(out=tmp[t3][:, 0:nq], in0=tmp[t3][:, 0:nq], in1=tmp[t4][:, 0:nq], op=ALU.add),
                                 reads=[tmp_r[t3], tmp_r[t4]], writes=[tmp_r[t3]])
                            e = nE()
                            P.op("act", lambda h: h.activation(out=Et[e][:, 0:nq], in_=tmp[t3][:, 0:nq], func=AF.Square),
                                 reads=[tmp_r[t3]], writes=[Et_r[e]])
                            b2 = dbank
                            P.op("pe", lambda h: h.matmul(banks[b2][:, 0:nq], blockones, Et[e][:, 0:nq], start=True, stop=True),
                                 reads=[Et_r[e]] + CR, writes=[bank_r[b2]])
                            t5, t6 = nT(), nT()
                            P.op("act", lambda h: h.activation(out=tmp[t5][:, 0:nq], in_=banks[b2][:, 0:nq], func=AF.Ln,
                                                               scale=1.0 / HD, bias=col(cc, 3)),
                                 reads=[bank_r[b2]] + CR, writes=[tmp_r[t5]])
                            P.op("act", lambda h: h.activation(out=tmp[t6][:, 0:nq], in_=tmp[t5][:, 0:nq], func=AF.Exp, scale=-0.5),
                                 reads=[tmp_r[t5]], writes=[tmp_r[t6]])
                            P.op("dve", lambda h: h.scalar_tensor_tensor(
                                out=dst, in0=tmp[t3][:, 0:nq], scalar=col(lsm, 2 + l), in1=tmp[t6][:, 0:nq], op0=ALU.mult, op1=ALU.mult),
                                reads=[tmp_r[t3], tmp_r[t6], lsm_r], writes=[dres])

                        blocks.append(dict(qtile=qtile, qc0=qc0, nq=nq, units=units, fin=fin))
            run_blocks(blocks, scale, g == "p")

        def merge_wout(l, g, c):
            wl = wv(w_in, l)
            wb = w_branch[l].rearrange("(k p) n -> p k n", p=128)
            sbr = [0, 1]
            for q in range(2):
                load_w(wb[:, :, q * 512:(q + 1) * 512], sbr[q])
            for dc in range(8):
                sg_ = 2 + dc % 2
                for br in range(4):
                    c0 = GATE0 + br * 1024 + dc * 128
                    load_w(wl[:, :, c0:c0 + 128], sg_, cols=slice(br * 128, (br + 1) * 128))
                for tb in range(2):
                    ts = slice(tb * 512, (tb + 1) * 512)
                    acc = nT()
                    for br in range(4):
                        bg = nbank()
                        for k in range(8):
                            P.op("pe", lambda h, k=k, br=br, bg=bg: h.matmul(banks[bg][:], wslot[sg_][:, k, br * 128:(br + 1) * 128], hT[:, k, ts],
                                                                            start=(k == 0), stop=(k == 7)),
                                 reads=[wslot_r[sg_], hT_r[k][tb]], writes=[bank_r[bg]], signal=(k == 7))
                        bp = nbank()
                        for kc in range(2):
                            ch = br * 2 + kc
                            P.op("pe", lambda h, kc=kc, ch=ch, bp=bp: h.matmul(
                                banks[bp][:], wslot[sbr[dc // 4]][:, ch, (dc % 4) * 128:(dc % 4 + 1) * 128], bufA[:, ch, ts],
                                start=(kc == 0), stop=(kc == 1)),
                                reads=[wslot_r[sbr[dc // 4]], A_r[ch][tb]], writes=[bank_r[bp]], signal=(kc == 1))
                        tg = nT()
                        P.op("act", lambda h, tg=tg, bg=bg: h.activation(out=tmp[tg][:], in_=banks[bg][:], func=AF.Sigmoid),
                             reads=[bank_r[bg]], writes=[tmp_r[tg]])
                        if br == 0:
                            P.op("dve", lambda h, tg=tg, bp=bp, acc=acc: h.tensor_tensor(out=tmp[acc][:], in0=banks[bp][:], in1=tmp[tg][:], op=ALU.mult),
                                 reads=[bank_r[bp], tmp_r[tg]], writes=[tmp_r[acc]])
                        else:
                            P.op("dve", lambda h, tg=tg, bp=bp: h.tensor_tensor(out=tmp[tg][:], in0=banks[bp][:], in1=tmp[tg][:], op=ALU.mult),
                                 reads=[bank_r[bp], tmp_r[tg]], writes=[tmp_r[tg]])
                            if br < 3:
                                P.op("dve", lambda h, tg=tg, acc=acc: h.tensor_tensor(out=tmp[acc][:], in0=tmp[acc][:], in1=tmp[tg][:], op=ALU.add),
                                     reads=[tmp_r[acc], tmp_r[tg]], writes=[tmp_r[acc]])
                            else:
                                d = dc * 2 + tb
                                P.op("dve", lambda h, tg=tg, acc=acc, d=d: h.tensor_tensor(out=bufB[:, d, :], in0=tmp[acc][:], in1=tmp[tg][:], op=ALU.add),
                                     reads=[tmp_r[acc], tmp_r[tg]], writes=[B_r[d]])
            wo = wv(w_out, l)
            for q in range(2):
                s = nslot()
                load_w(wo[:, :, q * 512:(q + 1) * 512], s)
                for dcl in range(4):
                    dc = q * 4 + dcl
                    for tb in range(2):
                        ts = slice(tb * 512, (tb + 1) * 512)
                        b = nbank()
                        for k in range(8):
                            P.op("pe", lambda h, k=k, b=b, s=s, dcl=dcl, tb=tb: h.matmul(
                                banks[b][:], wslot[s][:, k, dcl * 128:(dcl + 1) * 128], bufB[:, k * 2 + tb, :], start=(k == 0), stop=(k == 7)),
                                reads=[wslot_r[s], B_r[k * 2 + tb]], writes=[bank_r[b]], signal=(k == 7))
                        P.op("dve", lambda h, b=b, dc=dc, ts=ts: h.scalar_tensor_tensor(
                            out=xT[:, dc, ts], in0=banks[b][:], scalar=modcol(l, 2, dc, c), in1=xT[:, dc, ts], op0=ALU.mult, op1=ALU.add),
                            reads=[bank_r[b], mod_r, xT_r[dc][tb]], writes=[xT_r[dc][tb]])

        def ffn_load(l, jb, w_):
            wi = wv(w_ffn_in, l)
            sa, su = nslot(), nslot()
            load_w(wi[:, :, jb * 512:jb * 512 + w_], sa, cols=slice(0, w_))
            load_w(wi[:, :, DFF + jb * 512:DFF + jb * 512 + w_], su, cols=slice(0, w_))
            return sa, su

        def ffn(l, g, c, pre=None):
            wi = wv(w_ffn_in, l)
            wo = w_ffn_out[l].rearrange("(f p) n -> p f n", p=128)
            groups_ = [(0, [(0, 512), (1, 512)]), (8, [(2, 512), (3, 512)]), (16, [(4, 512), (5, 256)])]
            for (f0, blocks) in groups_:
                nF = sum(w_ // 128 for _, w_ in blocks)
                for (jb, w_) in blocks:
                    if jb == 0 and pre is not None:
                        sa, su = pre
                    else:
                        sa, su = ffn_load(l, jb, w_)
                    for fl in range(w_ // 128):
                        fi = jb * 4 + fl - f0
                        for tb in range(2):
                            ts = slice(tb * 512, (tb + 1) * 512)
                            ba, bu = nbank(), nbank()
                            for (bk, sl) in ((ba, sa), (bu, su)):
                                for k in range(8):
                                    P.op("pe", lambda h, k=k, bk=bk, sl=sl, fl=fl, ts=ts: h.matmul(
                                        banks[bk][:], wslot[sl][:, k, fl * 128:(fl + 1) * 128], hT[:, k, ts], start=(k == 0), stop=(k == 7)),
                                        reads=[wslot_r[sl], hT_r[k][tb]], writes=[bank_r[bk]], signal=(k == 7))
                            t = nT()
                            P.op("act", lambda h, t=t, ba=ba: h.activation(out=tmp[t][:], in_=banks[ba][:], func=AF.Silu),
                                 reads=[bank_r[ba]], writes=[tmp_r[t]])
                            P.op("dve", lambda h, t=t, bu=bu, fi=fi, ts=ts: h.tensor_tensor(out=bufA[:, fi, ts], in0=banks[bu][:], in1=tmp[t][:], op=ALU.mult),
                                 reads=[bank_r[bu], tmp_r[t]], writes=[A_r[fi][tb]])
                for q in range(2):
                    s = nslot()
                    load_w(wo[:, f0:f0 + nF, q * 512:(q + 1) * 512], s, kk=slice(0, nF))
                    for dcl in range(4):
                        dc = q * 4 + dcl
                        for tb in range(2):
                            ts = slice(tb * 512, (tb + 1) * 512)
                            b = nbank()
                            for fi in range(nF):
                                P.op("pe", lambda h, fi=fi, b=b, s=s, dcl=dcl, ts=ts: h.matmul(
                                    banks[b][:], wslot[s][:, fi, dcl * 128:(dcl + 1) * 128], bufA[:, fi, ts], start=(fi == 0), stop=(fi == nF - 1)),
                                    reads=[wslot_r[s], A_r[fi][tb]], writes=[bank_r[b]], signal=(fi == nF - 1))
                            P.op("dve", lambda h, b=b, dc=dc, ts=ts: h.scalar_tensor_tensor(
                                out=xT[:, dc, ts], in0=banks[b][:], scalar=modcol(l, 5, dc, c), in1=xT[:, dc, ts], op0=ALU.mult, op1=ALU.add),
                                reads=[bank_r[b], mod_r, xT_r[dc][tb]], writes=[xT_r[dc][tb]])

        for g in groups:
            c = 0 if g == "s" else 1
            if not ada_started[0]:
                ada_issue(4)
            for tt in range(8 if "load" not in skip else 0):
                xs = rr["xs"] % 2
                rr["xs"] += 1
                P.dma("sp", lambda h, xs=xs, tt=tt: h.dma_start(out=xstage[xs][:], in_=x_in[g][tt * 128:(tt + 1) * 128, :]), writes=[xstage_r[xs]])
                for kh in range(2):
                    b = nbank()
                    for i in range(4):
                        k = kh * 4 + i
                        P.op("pe", lambda h, xs=xs, k=k, i=i, b=b: h.transpose(banks[b][:, i * 128:(i + 1) * 128], xstage[xs][:, k * 128:(k + 1) * 128], ident[:]),
                             reads=[xstage_r[xs]] + CR, writes=[bank_r[b]], signal=(i == 3))
                    eng = alt()
                    wr = [xT_r[kh * 4 + i][tt // 4] for i in range(4)]
                    dst = xT[:, kh * 4:(kh + 1) * 4, tt * 128:(tt + 1) * 128]
                    src = banks[b][:].rearrange("p (i t) -> p i t", i=4)
                    if eng == "act":
                        P.op("act", lambda h, dst=dst, src=src: h.activation(out=dst, in_=src, func=AF.Copy), reads=[bank_r[b]], writes=wr)
                    else:
                        P.op("dve", lambda h, dst=dst, src=src: h.tensor_copy(out=dst, in_=src), reads=[bank_r[b]], writes=wr)
            ada_compute()
            dump("xT", xT[:].rearrange("p k t -> p (k t)"), [r_ for rr_ in xT_r for r_ in rr_])
            for l in range(n_layers):
                pre_a = project_load("a", l) if "a" not in skip else None
                if "norm" not in skip:
                    norm_mod(l, c, 0)
                dump("hT", hT[:].rearrange("p k t -> p (k t)"), [r_ for rr_ in hT_r for r_ in rr_], q="pool")
                for m in "abcd":
                    if m in skip:
                        continue
                    project(m, l, g, pre_a if m == "a" else None)
                    ada_some(2 if m != "d" else 0)
                    if m == "a" and g == "s" and "d" not in skip:
                        build_SB(l)
                    attention(m, l, g)
                    ada_some(2 if m != "d" else 0)
                if l == 0:
                    while any(b_[0] == 0 for b_ in ada_pending):
                        ada_issue(2)
                        ada_compute()
                dump("brT", bufA[:].rearrange("p k t -> p (k t)"), [r_ for rr_ in A_r for r_ in rr_], q="pool")
                if "merge" not in skip:
                    merge_wout(l, g, c)
                dump("merged", bufB[:, 0:16, :].rearrange("p k t -> p (k t)"), B_r[0:16], q="pool")
                dump("x1", xT[:].rearrange("p k t -> p (k t)"), [r_ for rr_ in xT_r for r_ in rr_])
                pre_f = ffn_load(l, 0, 512) if "ffn" not in skip else None
                if "norm" not in skip:
                    norm_mod(l, c, 1)
                if "ffn" not in skip:
                    ffn(l, g, c, pre_f)
            for tt in range(8 if "final" not in skip else 0):
                xs = rr["xs"] % 2
                rr["xs"] += 1
                bs = []
                for kh in range(2):
                    b = nbank()
                    bs.append(b)
                    for i in range(4):
                        k = kh * 4 + i
                        P.op("pe", lambda h, k=k, i=i, b=b, tt=tt: h.transpose(banks[b][:, i * 128:(i + 1) * 128], xT[:, k, tt * 128:(tt + 1) * 128], ident[:]),
                             reads=[xT_r[k][tt // 4]] + CR, writes=[bank_r[b]], signal=(i == 3))
                    t = nT()
                    P.op("act", lambda h, b=b, t=t, kh=kh: h.activation(out=tmp[t][:], in_=banks[b][:], func=AF.Square, accum_out=ss_col[:, kh:kh + 1]),
                         reads=[bank_r[b]], writes=[tmp_r[t], ss_col_r])
                P.op("dve", lambda h: h.tensor_tensor(out=ss_col[:, 2:3], in0=ss_col[:, 0:1], in1=ss_col[:, 1:2], op=ALU.add),
                     reads=[ss_col_r], writes=[ss_col_r])
                P.op("act", lambda h: h.activation(out=ss_col[:, 3:4], in_=ss_col[:, 2:3], func=AF.Ln, scale=1.0 / D, bias=col(cc, 2)),
                     reads=[ss_col_r] + CR, writes=[ss_col_r])
                P.op("act", lambda h: h.activation(out=ss_col[:, 4:5], in_=ss_col[:, 3:4], func=AF.Exp, scale=-0.5), reads=[ss_col_r], writes=[ss_col_r])
                for kh in range(2):
                    b = bs[kh]
                    P.op("dve", lambda h, b=b, kh=kh, xs=xs: h.scalar_tensor_tensor(
                        out=xstage[xs][:, kh * 512:(kh + 1) * 512], in0=banks[b][:], scalar=ss_col[:, 4:5], in1=gfin_b[:, kh * 512:(kh + 1) * 512],
                        op0=ALU.mult, op1=ALU.mult),
                        reads=[bank_r[b], ss_col_r, gfin_r], writes=[xstage_r[xs]])
                P.dma("sp", lambda h, xs=xs, tt=tt: h.dma_start(out=y_out[g][tt * 128:(tt + 1) * 128, :], in_=xstage[xs][:]),
                      reads=[xstage_r[xs]], semres=xstage_r[xs])

        dump("lsm", lsm[:], [lsm_r])
        all_res = out_res + xstage_r + kst_r + vst_r
        P.finish(all_res)
        P.emit(nc, st)
    return nc


_NC_CACHE = {}


def _get_program():
    if "nc" not in _NC_CACHE:
        _NC_CACHE["nc"] = build_program()
    return _NC_CACHE["nc"]


def make_in_maps(x_prompt, x_sample, cache_a_k, cache_a_v, cache_b_k, cache_b_v, cache_c_k, cache_c_v,
                 cache_d_k, cache_d_v, c, c_ctx, w_ada, b_ada, g_norm1, w_in, g_q_a, g_k_a, lam_b, g_subln_b,
                 sink_c, rpb_d, w_branch, w_out, g_norm2, w_ffn_in, w_ffn_out, g_final):
    f = lambda a: np.ascontiguousarray(np.asarray(a, dtype=np.float32))
    consts = _consts()
    vecs = np.zeros((256, 128), np.float32)
    b_ada_, g1, g2, gf = f(b_ada), f(g_norm1), f(g_norm2), f(g_final)
    vecs[0:48] = b_ada_[0].reshape(48, 128)
    vecs[48:96] = b_ada_[1].reshape(48, 128)
    vecs[96:104] = g1[0].reshape(8, 128)
    vecs[104:112] = g1[1].reshape(8, 128)
    vecs[112:120] = g2[0].reshape(8, 128)
    vecs[120:128] = g2[1].reshape(8, 128)
    vecs[128:136] = gf.reshape(8, 128)
    gq, gk, gs_, sk = f(g_q_a), f(g_k_a), f(g_subln_b), f(sink_c)
    for l in range(DEPTH):
        vecs[136 + l] = np.tile(gq[l], 2)
        vecs[138 + l] = np.tile(gk[l], 2)
        vecs[140 + l] = np.tile(gs_[l], 2)
        for j in range(2):
            vecs[142 + l * 2 + j] = np.repeat(sk[l, 2 * j:2 * j + 2], 64)
    lamb = f(lam_b).reshape(1, 256)
    rpbx = _rpb_expand(f(rpb_d))
    shared = dict(
        w_ada=f(w_ada), w_in=f(w_in), w_branch=f(w_branch).reshape(DEPTH, 1024, D), w_out=f(w_out),
        w_ffn_in=f(w_ffn_in), w_ffn_out=f(w_ffn_out), vecs=vecs, lamb=lamb, rpbx=rpbx, **consts)
    xp, xs, c_, cctx = f(x_prompt), f(x_sample), f(c), f(c_ctx)
    caches = dict(a=(f(cache_a_k), f(cache_a_v)), b=(f(cache_b_k), f(cache_b_v)), c=(f(cache_c_k), f(cache_c_v)), d=(f(cache_d_k), f(cache_d_v)))
    in_maps = []
    for i in range(N_CORES):
        mp = dict(shared)
        mp["xs"] = xs[i]
        mp["xp"] = np.ascontiguousarray(xp[4 * i:4 * i + 4].reshape(NT, D))
        mp["cond"] = np.ascontiguousarray(np.stack([c_[i], cctx]).reshape(16, 128))
        for m in "abcd":
            mp["ck_" + m] = np.ascontiguousarray(caches[m][0][i])
            mp["cv_" + m] = np.ascontiguousarray(caches[m][1][i])
        in_maps.append(mp)
    return in_maps


def kernel(**inputs):
    nc = _get_program()
    in_maps = make_in_maps(**inputs)
    res = run_bass_kernel_spmd(nc, in_maps, core_ids=list(range(N_CORES)))
    R = res.results
    y_prompt = np.concatenate([r["yp"].reshape(4, 256, D) for r in R], axis=0).astype(np.float32)
    y_sample = np.stack([r["ys"] for r in R], axis=0).astype(np.float32)
    outs = [y_prompt, y_sample]
    for m in "abcd":
        outs.append(np.concatenate([r["nk_" + m] for r in R], axis=0).astype(np.float32))
        outs.append(np.concatenate([r["nv_" + m] for r in R], axis=0).astype(np.float32))
    return tuple(outs)
```

```python
import os
import math
import types
from contextlib import ExitStack

import numpy as np
import concourse.bass as bass
import concourse.mybir as mybir
from concourse.bass_utils import run_bass_kernel_spmd

F32 = mybir.dt.float32
BF16 = mybir.dt.bfloat16
AF = mybir.ActivationFunctionType
ALU = mybir.AluOpType
AX = mybir.AxisListType
AP = bass.AP

N_CORES = 8
D = 1024
NT = 1024
DEPTH = 2
HD = 64
DFF = 2816
IN_COLS = 6656
NEG = -30000.0
SAME_ENGINE_SYNC = 'raw'
NSTEP = int(os.environ.get('NSTEP', '9'))
USE_APPROX_RECIP = False

MIX = {
    "a": dict(q=0, k=256, v=384, kvh=2, vh=2, idx=0),
    "b": dict(q=512, k=768, v=1024, kvh=4, vh=4, idx=1),
    "c": dict(q=1280, k=1536, v=1664, kvh=2, vh=2, idx=2),
    "d": dict(q=1792, k=2048, v=2304, kvh=4, vh=4, idx=3),
}
GATE0 = 2560


class Res:
    __slots__ = ("name", "writer", "readers", "overlaps", "dsem", "dcount", "excl")

    def __init__(self, name, excl=False):
        self.name = name
        self.excl = excl
        self.writer = None
        self.readers = []
        self.overlaps = []
        self.dsem = None
        self.dcount = 0


class Eng:
    def __init__(self, name, in_order_safe=False):
        self.name = name
        self.sem = "eng_" + name
        self.seq = 0
        self.waited = {}
        self.ops = []
        self.in_order_safe = in_order_safe


class Prog:
    def __init__(self):
        self.engs = {
            "pe": Eng("pe", in_order_safe=True),
            "act": Eng("act"),
            "dve": Eng("dve"),
            "pool": Eng("pool"),
            "sp": Eng("sp"),
        }
        self.sem_names = set(e.sem for e in self.engs.values())
        self.n_ops = 0
        self.log = None

    def res(self, name, excl=False):
        return Res(name, excl)

    def _hazards(self, reads, writes):
        hz = []
        for r in reads:
            if r.writer is not None:
                hz.append(r.writer + (True,))
            if r.excl:
                hz.extend(t + (False,) for t in r.readers)
            for o in r.overlaps:
                if o.writer is not None:
                    hz.append(o.writer + (True,))
        for w in writes:
            if w.writer is not None:
                hz.append(w.writer + (False,))
            hz.extend(t + (False,) for t in w.readers)
            for o in w.overlaps:
                if o.writer is not None:
                    hz.append(o.writer + (False,))
                hz.extend(t + (False,) for t in o.readers)
        return hz

    def _waits_for(self, eng, hz):
        need = {}
        if SAME_ENGINE_SYNC == 'raw' and not eng.in_order_safe:
            own = [(v, r) for (s_, v, r) in hz if s_ == eng.sem]
            if any(r and eng.waited.get(eng.sem, 0) < v for (v, r) in own):
                hz = [h_ for h_ in hz if h_[0] != eng.sem] + [(eng.sem, max(v for v, _ in own), True)]
        for (sem, val, is_raw) in hz:
            if sem == eng.sem:
                if eng.in_order_safe or not SAME_ENGINE_SYNC:
                    continue
                if SAME_ENGINE_SYNC == 'raw' and not is_raw:
                    continue
            if eng.waited.get(sem, 0) >= val:
                continue
            if need.get(sem, 0) < val:
                need[sem] = val
        for sem, val in need.items():
            eng.waited[sem] = val
        return list(need.items())

    def op(self, engname, fn, reads=(), writes=(), signal=True):
        fn = _snapshot(fn)
        eng = self.engs[engname]
        waits = self._waits_for(eng, self._hazards(reads, writes))
        if signal:
            eng.seq += 1
            tok = (eng.sem, eng.seq)
        else:
            tok = (eng.sem, eng.seq + 1)
        sem = eng.sem

        def emit(h, S, waits=waits, fn=fn, signal=signal, sem=sem):
            for (s, v) in waits:
                h.wait_ge(S[s], v)
            inst = fn(h)
            if signal:
                inst.then_inc(S[sem], 1)

        eng.ops.append(emit)
        if self.log is not None:
            self.log.append((engname, tok, [r.name for r in reads], [w.name for w in writes], list(waits)))
        for r in reads:
            r.readers.append(tok)
            if len(r.readers) > 64:
                r.readers = _compact(r.readers)
        for w in writes:
            w.writer = tok
            w.readers = []
        self.n_ops += 1

    def dma(self, queue, fn, reads=(), writes=(), semres=None):
        fn = _snapshot(fn)
        eng = self.engs[queue]
        waits = self._waits_for(eng, self._hazards(reads, writes))
        sr = semres if semres is not None else (writes[0] if writes else reads[0])
        if sr.dsem is None:
            sr.dsem = "dma%d" % len(self.sem_names)
            self.sem_names.add(sr.dsem)
        sr.dcount += 16
        tok = (sr.dsem, sr.dcount)
        dsem = sr.dsem

        def emit(h, S, waits=waits, fn=fn, dsem=dsem):
            for (s, v) in waits:
                h.wait_ge(S[s], v)
            fn(h).then_inc(S[dsem], 16)

        eng.ops.append(emit)
        for r in reads:
            r.readers.append(tok)
            if len(r.readers) > 64:
                r.readers = _compact(r.readers)
        for w in writes:
            w.writer = tok
            w.readers = []
        self.n_ops += 1

    def finish(self, final_res):
        eng = self.engs["sp"]
        hz = []
        for r in final_res:
            if r.writer is not None:
                hz.append(r.writer + (True,))
            hz.extend(t + (True,) for t in r.readers)
        waits = self._waits_for(eng, hz)

        def emit(h, S, waits=waits):
            for (s, v) in waits:
                h.wait_ge(S[s], v)

        eng.ops.append(emit)

    def emit(self, nc, stack):
        S = {}
        for name in sorted(self.sem_names):
            S[name] = stack.enter_context(nc.semaphore(name))
        block = stack.enter_context(nc.Block())
        engs = self.engs

        @block.tensor
        def _(h):
            for f in engs["pe"].ops:
                f(h, S)

        @block.scalar
        def _(h):
            for f in engs["act"].ops:
                f(h, S)

        @block.vector
        def _(h):
            for f in engs["dve"].ops:
                f(h, S)

        @block.gpsimd
        def _(h):
            for f in engs["pool"].ops:
                f(h, S)

        @block.sync
        def _(h):
            for f in engs["sp"].ops:
                f(h, S)


def _snapshot(fn):
    if fn.__closure__ is None:
        return fn
    cells = []
    for c in fn.__closure__:
        try:
            cells.append(types.CellType(c.cell_contents))
        except ValueError:
            cells.append(c)
    return types.FunctionType(fn.__code__, fn.__globals__, fn.__name__, fn.__defaults__, tuple(cells))


def _compact(toks):
    best = {}
    for s, v in toks:
        if best.get(s, 0) < v:
            best[s] = v
    return list(best.items())


def _rope_tables():
    theta = np.float32(10000.0)
    t = np.arange(NT)
    row = (t // 64).astype(np.float32)
    col = (t % 64).astype(np.float32)
    inv16 = np.power(theta, -np.arange(16, dtype=np.float32) / np.float32(16)).astype(np.float32)
    inv8 = np.power(theta, -np.arange(8, dtype=np.float32) / np.float32(8)).astype(np.float32)
    cA = np.zeros((128, NT), np.float32); sA = np.zeros((128, NT), np.float32)
    cB = np.zeros((128, NT), np.float32); sB = np.zeros((128, NT), np.float32)
    pA = np.zeros((128, 128), np.float32); pB = np.zeros((128, 128), np.float32)
    for p in range(128):
        d = p % 64
        e = d % 32
        pos = row if d < 32 else col
        first = e < 16
        ang = (pos * inv16[e % 16]).astype(np.float32)
        cA[p] = np.cos(ang)
        sA[p] = -np.sin(ang) if first else np.sin(ang)
        partner = p + 16 if first else p - 16
        pA[partner, p] = 1.0
        e32 = d % 32
        posb = row if e32 < 16 else col
        eb = e32 % 16
        firstb = eb < 8
        angb = (posb * inv8[eb % 8]).astype(np.float32)
        cB[p] = np.cos(angb)
        sB[p] = -np.sin(angb) if firstb else np.sin(angb)
        partnerb = p + 8 if firstb else p - 8
        pB[partnerb, p] = 1.0
    return cA, sA, cB, sB, pA, pB


def _consts():
    cA, sA, cB, sB, pA, pB = _rope_tables()
    ident = np.eye(128, dtype=np.float32)
    blockones = np.zeros((128, 128), np.float32)
    blockones[:64, :64] = 1.0
    blockones[64:, 64:] = 1.0
    kl = np.arange(128)[:, None]
    ql = np.arange(128)[None, :]
    triA = np.where(kl <= ql, 0.0, NEG).astype(np.float32)
    triB = np.where(kl >= ql, 0.0, NEG).astype(np.float32)
    sq = np.stack([ident, pA, pB, blockones, triA, triB]).astype(np.float32)
    kc = np.arange(64)[:, None]
    qc = np.arange(64)[None, :]
    start_c = np.clip(qc - 8, 0, 48)
    colvalid = (kc >= start_c) & (kc < start_c + 16)
    cm = np.where(colvalid, 0.0, NEG).astype(np.float32)
    cmask = np.zeros((128, 25, 64), np.float32)
    cmask[:64] = cm[:, None, :]
    cmask[64:] = cm[:, None, :]
    cmask = cmask.reshape(128, 1600)
    small = np.zeros((2, 256), np.float32)
    small[0, 0:64] = 1.0
    small[1, 64:128] = 1.0
    for krl in range(2):
        for kt in range(8):
            for qr in range(16):
                st_ = min(max(qr - 4, 0), 8)
                kr = 2 * kt + krl
                ok = (kr >= st_) and (kr < st_ + 8)
                small[krl, 128 + kt * 16 + qr] = 0.0 if ok else NEG
    cc = np.zeros((128, 8), np.float32)
    for p in range(128):
        cc[p, 0] = 1.0 if (p % 64) < 32 else 0.0
        cc[p, 1] = 0.0 if (p % 64) < 32 else 1.0
    cc[:, 2] = 1e-6
    cc[:, 3] = 1e-5
    ropes = np.stack([cA, sA, cB, sB]).astype(np.float32)
    return dict(c_ident=ident, c_sq=sq, c_cmask=cmask, c_small=small, c_cc=cc, c_ropes=ropes)


def _rpb_expand(rpb_d):
    L, H = rpb_d.shape[0], rpb_d.shape[1]
    kc = np.arange(64)[:, None]
    qc = np.arange(64)[None, :]
    dc = np.clip(kc - qc, -15, 15) + 15
    out = np.zeros((L, H, 2, 64, 25, 64), np.float32)
    for krl in range(2):
        for j in range(25):
            dr = j - 12 + krl
            if -7 <= dr <= 7:
                out[:, :, krl, :, j, :] = rpb_d[:, :, dr + 7][:, :, dc]
    return out.reshape(L, H, 128, 1600)


def build_program(groups=("s", "p"), n_layers=DEPTH, dbg=None, skip=(), ada_layers=None, oplog=None):
    nc = bass.Bass("TRN2", target_bir_lowering=False)
    P = Prog()
    P.log = oplog
    st = ExitStack()
    with st:
        def din(name, shape):
            return nc.dram_tensor(name, list(shape), F32, kind="ExternalInput").ap()

        def dout(name, shape):
            return nc.dram_tensor(name, list(shape), F32, kind="ExternalOutput").ap()

        def sb(name, shape, dt=F32):
            name = "s_" + name
            return st.enter_context(nc.sbuf_tensor(name, list(shape), dt))

        x_in = {"s": din("xs", [NT, D]), "p": din("xp", [NT, D])}
        cond_d = din("cond", [16, 128])
        cache = {}
        for m in "abcd":
            H = MIX[m]["kvh"]
            cache[m] = (din("ck_" + m, [DEPTH, H, 256, HD]), din("cv_" + m, [DEPTH, H, 256, HD]))
        w_ada = din("w_ada", [DEPTH, D, 6 * D])
        w_in = din("w_in", [DEPTH, D, IN_COLS])
        w_branch = din("w_branch", [DEPTH, 4 * 256, D])
        w_out = din("w_out", [DEPTH, D, D])
        w_ffn_in = din("w_ffn_in", [DEPTH, D, 2 * DFF])
        w_ffn_out = din("w_ffn_out", [DEPTH, DFF, D])
        vecs_d = din("vecs", [256, 128])
        lamb_d = din("lamb", [1, 256])
        rpbx_d = din("rpbx", [DEPTH, 4, 128, 1600])
        c_ident = din("c_ident", [128, 128])
        c_sq = din("c_sq", [6, 128, 128])
        c_cmask = din("c_cmask", [128, 1600])
        c_small = din("c_small", [2, 256])
        c_cc = din("c_cc", [128, 8])
        c_ropes = din("c_ropes", [4, 128, NT])

        y_out = {"s": dout("ys", [NT, D]), "p": dout("yp", [NT, D])}
        nk_out, nv_out = {}, {}
        for m in "abcd":
            H = MIX[m]["kvh"]
            nk_out[m] = dout("nk_" + m, [4, DEPTH, H, 256, HD])
            nv_out[m] = dout("nv_" + m, [4, DEPTH, H, 256, HD])
        out_res = [P.res("out_all")]
        OUT = out_res[0]
        dbg_out = {}
        if dbg:
            for name, shape in dbg.items():
                dbg_out[name] = dout("dbg_" + name, shape)

        xT = sb("xT", [128, 8, NT]); xT_r = [[P.res("xT%d_%d" % (k, tb)) for tb in range(2)] for k in range(8)]
        hT = sb("hT", [128, 8, NT], BF16); hT_r = [[P.res("hT%d_%d" % (k, tb)) for tb in range(2)] for k in range(8)]
        bufA = sb("bufA", [128, 8, NT], BF16); A_r = [[P.res("A%d_%d" % (k, tb)) for tb in range(2)] for k in range(8)]
        bufB = sb("bufB", [128, 19, 512], BF16); B_r = [P.res("B%d" % i) for i in range(19)]
        NSLOT = 4
        wslot = [sb("wslot%d" % i, [128, 8, 512], BF16) for i in range(NSLOT)]
        wslot_r = [P.res("wslot%d" % i) for i in range(NSLOT)]
        ropes = sb("ropes", [128, 4, NT]); ropes_r = P.res("ropes")
        cmask = sb("cmask", [128, 1600], BF16)
        SBh = [sb("SBh%d" % h, [128, 1600], BF16) for h in range(4)]; SBh_r = [P.res("SBh%d" % h) for h in range(4)]
        NE = 6
        Et = [sb("Et%d" % i, [128, 512], BF16) for i in range(NE)]; Et_r = [P.res("Et%d" % i) for i in range(NE)]
        NTMP = 10
        tmp = [sb("tmp%d" % i, [128, 512]) for i in range(NTMP)]; tmp_r = [P.res("tmp%d" % i) for i in range(NTMP)]
        xstage = [sb("xstage%d" % i, [128, D]) for i in range(2)]; xstage_r = [P.res("xstage%d" % i) for i in range(2)]
        NKST = 4
        kst = [sb("kst%d" % i, [128, 512]) for i in range(NKST)]; kst_r = [P.res("kst%d" % i) for i in range(NKST)]
        NVST = 4
        vst = [sb("vst%d" % i, [128, 256]) for i in range(NVST)]; vst_r = [P.res("vst%d" % i) for i in range(NVST)]
        cst = sb("cst", [128, 2, 4, 64]); cst_r = P.res("cst")
        ident = sb("ident", [128, 128])
        sqb = sb("sqb", [128, 6, 128], BF16)
        ones_b = sb("ones_b", [128, 128], BF16)
        small = sb("small", [2, 256], BF16)
        cc = sb("cc", [128, 8])
        consts_r = P.res("consts")
        vec_in = sb("vec_in", [128, 2, 128])
        vecT = sb("vecT", [128, 256]); vecT_r = P.res("vecT")
        cond_sb = sb("cond_sb", [16, 128]); cond_r = P.res("cond_sb")
        csil = sb("csil", [16, 128]); csil_r = P.res("csil")
        scT = sb("scT", [128, 8, 2], BF16); scT_r = P.res("scT")
        mod = sb("mod", [128, DEPTH, 48, 2]); mod_r = P.res("mod")
        dm = sb("dm", [128, DEPTH, 2, 2, 8]); dm_r = P.res("dm")
        lamb = sb("lamb", [128, 256]); lamb_r = P.res("lamb")
        lsm = sb("lsm", [128, 16]); lsm_r = P.res("lsm")
        gfin_b = sb("gfin_b", [128, D]); gfin_r = P.res("gfin_b")
        ss_col = sb("ss_col", [128, 8]); ss_col_r = P.res("ss_col")

        banks = [st.enter_context(nc.psum_tensor("bank%d" % i, [128, 512], F32)) for i in range(8)]
        bank_r = [P.res("bank%d" % i, excl=True) for i in range(8)]

        rr = {"all": 0, "S": 0, "E": 0, "T": 0, "slot": 0, "xs": 0, "k": 0, "v": 0, "eng": 0}

        def nbank(pool="all"):
            if pool == "all":
                i = rr["all"] % 8
                rr["all"] += 1
            else:
                i = rr["S"] % 6
                rr["S"] += 1
            return i

        def nE():
            i = rr["E"] % NE
            rr["E"] += 1
            return i

        def nT():
            i = rr["T"] % NTMP
            rr["T"] += 1
            return i

        held_slots = set()

        def nslot():
            while True:
                i = rr["slot"] % NSLOT
                rr["slot"] += 1
                if i not in held_slots:
                    return i

        def alt():
            rr["eng"] += 1
            return "act" if rr["eng"] % 2 else "dve"

        def col(t, c):
            return t[:, c:c + 1]

        def load_w(src_ap, slot, cols=slice(0, 512), kk=slice(0, 8)):
            P.dma("pool", lambda h: h.dma_start(out=wslot[slot][:, kk, cols], in_=src_ap),
                  writes=[wslot_r[slot]])

        def wv(t3, l):
            return t3[l].rearrange("(k p) n -> p k n", p=128)

        def dump(name, src_ap, reads, q="sp"):
            if name in dbg_out and name not in dumped:
                dumped.add(name)
                P.dma(q, lambda h: h.dma_start(out=dbg_out[name], in_=src_ap), reads=reads, semres=OUT)
        dumped = set()

        P.dma("sp", lambda h: h.dma_start(out=ident[:], in_=c_ident), writes=[consts_r])
        P.dma("sp", lambda h: h.dma_start(out=cc[:], in_=c_cc), writes=[consts_r])
        P.dma("sp", lambda h: h.dma_start(out=vec_in[:], in_=vecs_d.rearrange("(a p) n -> p a n", p=128)), writes=[consts_r])
        P.dma("sp", lambda h: h.dma_start(out=cond_sb[:], in_=cond_d), writes=[cond_r])
        P.dma("sp", lambda h: h.dma_start(out=lamb[:], in_=AP(lamb_d.tensor, 0, [[0, 128], [1, 256]])), writes=[lamb_r])
        P.dma("sp", lambda h: h.dma_start(out=gfin_b[:], in_=AP(vecs_d.tensor, 128 * 128, [[0, 128], [1, D]])), writes=[gfin_r])
        constsp_r = P.res("constsp")
        P.dma("pool", lambda h: h.dma_start(out=sqb[:], in_=c_sq.rearrange("a p n -> p a n")), writes=[constsp_r])
        P.dma("pool", lambda h: h.dma_start(out=small[:], in_=c_small), writes=[constsp_r])
        for i in range(4):
            P.dma("pool", lambda h, i=i: h.dma_start(out=cmask[:, i * 400:(i + 1) * 400], in_=c_cmask[:, i * 400:(i + 1) * 400]),
                  writes=[constsp_r])
        if "s" in groups:
            P.dma("sp", lambda h: h.dma_start(out=ropes[:], in_=c_ropes.rearrange("a p n -> p a n")), writes=[ropes_r])
        ones_r = P.res("ones")
        P.op("dve", lambda h: h.memset(ones_b[:], 1.0), writes=[ones_r])
        identb = sqb[:, 0, :]
        permA = sqb[:, 1, :]
        permB = sqb[:, 2, :]
        blockones = sqb[:, 3, :]
        triA = sqb[:, 4, :]
        triB = sqb[:, 5, :]
        oh2 = small[0:2, 0:128]
        CR = [consts_r, constsp_r, ones_r]

        b = nbank()
        for a in range(2):
            P.op("pe", lambda h, a=a, b=b: h.transpose(banks[b][:, a * 128:(a + 1) * 128], vec_in[:, a, :], ident[:]),
                 reads=CR, writes=[bank_r[b]], signal=(a == 1))
        P.op("dve", lambda h, b=b: h.tensor_copy(out=vecT[:], in_=banks[b][:, 0:256]), reads=[bank_r[b]], writes=[vecT_r])
        V_BADA, V_G1, V_G2, V_GF, V_GQA, V_GKA, V_GSUB, V_SINK = 0, 96, 112, 128, 136, 138, 140, 142

        P.op("act", lambda h: h.activation(out=csil[:], in_=cond_sb[:], func=AF.Silu), reads=[cond_r], writes=[csil_r])
        b = nbank()
        P.op("pe", lambda h, b=b: h.transpose(banks[b][:, 0:16], csil[0:16, :], ident[0:16, 0:16]),
             reads=CR + [csil_r], writes=[bank_r[b]])
        P.op("dve", lambda h, b=b: h.tensor_copy(out=scT[:].rearrange("p k c -> p c k"), in_=banks[b][:, 0:16].rearrange("p (c k) -> p c k", c=2)),
             reads=[bank_r[b]], writes=[scT_r])

        ada_inflight = []

        def ada_issue(n):
            for _ in range(n):
                if not ada_pending or len(held_slots) >= NSLOT - 2 + (2 if not ada_started[0] else 0):
                    break
                l, cb = ada_pending.pop(0)
                s = nslot()
                held_slots.add(s)
                load_w(wv(w_ada, l)[:, :, cb * 512:(cb + 1) * 512], s)
                ada_inflight.append((l, cb, s))
            ada_started[0] = True

        def ada_compute():
            while ada_inflight:
                l, cb, s = ada_inflight.pop(0)
                ada_block(l, cb, s)
                held_slots.discard(s)

        def ada_block(l, cb, s):
            b = nbank()
            for ct in range(4):
                for k in range(8):
                    P.op("pe", lambda h, s=s, ct=ct, k=k, b=b: h.matmul(
                        banks[b][:, ct * 2:ct * 2 + 2], wslot[s][:, k, ct * 128:(ct + 1) * 128], scT[:, k, :],
                        start=(k == 0), stop=(k == 7)),
                        reads=[wslot_r[s], scT_r], writes=[bank_r[b]], signal=(k == 7 and ct == 3))
            P.op("dve", lambda h, l=l, b=b, cb=cb: h.tensor_tensor(
                out=mod[:, l, cb * 4:(cb + 1) * 4, :], in0=banks[b][:, 0:8].rearrange("p (j c) -> p j c", c=2),
                in1=AP(vecT, V_BADA + l * 48 + cb * 4, [[256, 128], [1, 4], [0, 2]]), op=ALU.add),
                reads=[bank_r[b], vecT_r], writes=[mod_r])
            if cb in (3, 9):
                which, j, gbase = (0, 1, V_G1) if cb == 3 else (1, 4, V_G2)
                for c in range(2):
                    P.op("dve", lambda h, l=l, c=c, which=which, j=j, gbase=gbase: h.scalar_tensor_tensor(
                        out=dm[:, l, c, which, :], in0=mod[:, l, j * 8:(j + 1) * 8, c], scalar=1.0,
                        in1=vecT[:, gbase + l * 8:gbase + (l + 1) * 8], op0=ALU.add, op1=ALU.mult),
                        reads=[mod_r, vecT_r], writes=[dm_r])

        ada_started = [False]
        ada_pending = [(l, cb) for l in range(n_layers if ada_layers is None else ada_layers) for cb in range(12)]

        def ada_some(n):
            ada_compute()
            ada_issue(n)

        def ada_flush():
            ada_compute()
            while ada_pending:
                ada_issue(2)
                ada_compute()


        if dbg:
            ada_flush()
        dump("mod", mod[:].rearrange("p l j c -> p (l j c)"), [mod_r])
        dump("vecT", vecT[:], [vecT_r])
        dump("dm", dm[:].rearrange("p l c w k -> p (l c w k)"), [dm_r])
        dump("scT", scT[:].rearrange("p k c -> p (k c)"), [scT_r], q="pool")

        def modcol(l, j, k, c):
            return mod[:, l, j * 8 + k, c:c + 1]

        for l in range(n_layers):
            lam_init = 0.8 - 0.6 * math.exp(-0.3 * l)
            for i in range(2):
                o = l * 128 + i * 64
                P.op("dve", lambda h, o=o, i=i: h.tensor_tensor(out=tmp[0][:, i * 32:(i + 1) * 32],
                                                              in0=lamb[:, o:o + 32], in1=lamb[:, o + 32:o + 64], op=ALU.mult),
                     reads=[lamb_r], writes=[tmp_r[0]])
                P.op("dve", lambda h, i=i: h.tensor_reduce(out=lsm[:, 8 + i:9 + i], in_=tmp[0][:, i * 32:(i + 1) * 32], axis=AX.X, op=ALU.add),
                     reads=[tmp_r[0]], writes=[lsm_r])
            P.op("act", lambda h: h.activation(out=lsm[:, 10:12], in_=lsm[:, 8:10], func=AF.Exp), reads=[lsm_r], writes=[lsm_r])
            P.op("dve", lambda h: h.tensor_tensor(out=lsm[:, 12:13], in0=lsm[:, 11:12], in1=lsm[:, 10:11], op=ALU.subtract),
                 reads=[lsm_r], writes=[lsm_r])
            P.op("dve", lambda h, l=l, li=lam_init: h.tensor_scalar(out=lsm[:, l:l + 1], in0=lsm[:, 12:13], scalar1=-li, scalar2=None, op0=ALU.add),
                 reads=[lsm_r], writes=[lsm_r])
            P.op("dve", lambda h, l=l, li=lam_init: h.tensor_scalar(out=lsm[:, 2 + l:3 + l], in0=vecT[:, V_GSUB + l:V_GSUB + l + 1],
                                                                   scalar1=(1.0 - li), scalar2=None, op0=ALU.mult),
                 reads=[vecT_r], writes=[lsm_r])
        P.op("act", lambda h: h.activation(out=lsm[:, 4:8], in_=vecT[:, V_SINK:V_SINK + 4], func=AF.Exp), reads=[vecT_r], writes=[lsm_r])

        def build_SB(l):
            for hh in range(4):
                for i in range(4):
                    P.dma("pool", lambda h, hh=hh, i=i: h.dma_start(out=SBh[hh][:, i * 400:(i + 1) * 400],
                                                                   in_=rpbx_d[l, hh][:, i * 400:(i + 1) * 400]), writes=[SBh_r[hh]])
            for hh in range(4):
                P.op("dve", lambda h, hh=hh: h.scalar_tensor_tensor(out=SBh[hh][:], in0=SBh[hh][:], scalar=8.0, in1=cmask[:],
                                                                  op0=ALU.mult, op1=ALU.add),
                     reads=[SBh_r[hh]] + CR, writes=[SBh_r[hh]])

        def rstd_from_bank(b, inv_n, eps_col):
            t1 = nT()
            out_t = nT()
            P.op("act", lambda h: h.activation(out=tmp[t1][:], in_=banks[b][:], func=AF.Ln, scale=inv_n, bias=col(cc, eps_col)),
                 reads=[bank_r[b]] + CR, writes=[tmp_r[t1]])
            P.op("act", lambda h: h.activation(out=tmp[out_t][:], in_=tmp[t1][:], func=AF.Exp, scale=-0.5),
                 reads=[tmp_r[t1]], writes=[tmp_r[out_t]])
            return out_t

        def norm_mod(l, c, which, tbs=(0, 1)):
            jsh = 0 if which == 0 else 3
            for tb in tbs:
                ts = slice(tb * 512, (tb + 1) * 512)
                b = nbank()
                for k in range(8):
                    e = nE()
                    if k % 3 == 2:
                        P.op("act", lambda h, k=k, e=e: h.activation(out=Et[e][:], in_=xT[:, k, ts], func=AF.Square),
                             reads=[xT_r[k][tb]], writes=[Et_r[e]])
                    else:
                        P.op("pool" if k % 3 == 0 else "dve", lambda h, k=k, e=e: h.tensor_tensor(out=Et[e][:], in0=xT[:, k, ts], in1=xT[:, k, ts], op=ALU.mult),
                             reads=[xT_r[k][tb]], writes=[Et_r[e]])
                    P.op("pe", lambda h, k=k, e=e, b=b: h.matmul(banks[b][:], ones_b[:], Et[e][:], start=(k == 0), stop=(k == 7)),
                         reads=[Et_r[e]] + CR, writes=[bank_r[b]], signal=True)
                rs = rstd_from_bank(b, 1.0 / D, 2)
                for k in range(8):
                    t = nT()
                    P.op("dve", lambda h, k=k, t=t: h.tensor_tensor(out=tmp[t][:], in0=xT[:, k, ts], in1=tmp[rs][:], op=ALU.mult),
                         reads=[xT_r[k][tb], tmp_r[rs]], writes=[tmp_r[t]])
                    P.op("act", lambda h, k=k, t=t: h.activation(out=hT[:, k, ts], in_=tmp[t][:], func=AF.Identity,
                                                               scale=dm[:, l, c, which, k:k + 1], bias=modcol(l, jsh, k, c)),
                         reads=[tmp_r[t], dm_r, mod_r], writes=[hT_r[k][tb]])

        def QT(j, tb):
            return j * 2 + tb

        def KTtok(i, tb):
            return 4 + i * 2 + tb

        def KTctx_ap(i, p0, pn, c0, cn):
            return bufB[p0:p0 + pn, 12 + i // 2, (i % 2) * 256 + c0:(i % 2) * 256 + c0 + cn]

        def KTctx_r(i):
            return B_r[12 + i // 2]

        def V_ap(vt, c0, cn):
            return bufB[:, 14 + vt // 2, (vt % 2) * 256 + c0:(vt % 2) * 256 + c0 + cn]

        def V_r(vt):
            return B_r[14 + vt // 2]

        def post_qk(m, l, g, kind, j, tb, slot, s1, cbase):
            rope = (g == "s") and m in "abc"
            normed = (m == "a")
            nb, nt, ne, _ = qk_slot_plan(m, g)
            bl = [slot * nb + i for i in range(nb)]
            tl = [slot * nt + i for i in range(nt)]
            el = [slot * ne + i for i in range(ne)]
            b = bl.pop(0)
            b2 = bl.pop(0) if normed else None
            b3 = bl.pop(0) if rope else None
            b4 = bl.pop(0) if g == "p" else None
            t0 = tl.pop(0) if normed else None
            rs = tl.pop(0) if normed else None
            t1 = tl.pop(0) if (rope or normed) else None
            t2 = tl.pop(0) if rope else None
            t4 = tl.pop(0) if (g == "p" and not normed) else None
            e = el.pop(0) if normed else None
            e2 = el.pop(0) if rope else None
            for k in range(8):
                P.op("pe", lambda h, k=k: h.matmul(
                    banks[b][:], wslot[s1][:, k, cbase + j * 128:cbase + (j + 1) * 128], hT[:, k, tb * 512:(tb + 1) * 512],
                    start=(k == 0), stop=(k == 7)),
                    reads=[wslot_r[s1], hT_r[k][tb]], writes=[bank_r[b]], signal=(k == 7))
            yield
            ts = slice(tb * 512, (tb + 1) * 512)
            is_k = (kind == "k")
            need_f32 = is_k and g == "p"
            gcol = None
            if normed:
                gcol = col(vecT, (V_GKA if is_k else V_GQA) + l)
            if rope:
                ci, si, perm = (0, 1, permA) if m in "ac" else (2, 3, permB)
            if normed:
                P.op("act", lambda h: h.activation(out=Et[e][:], in_=banks[b][:], func=AF.Square), reads=[bank_r[b]], writes=[Et_r[e]])
            if rope:
                if normed:
                    P.op("act", lambda h: h.activation(out=Et[e2][:], in_=banks[b][:], func=AF.Copy, scale=gcol),
                         reads=[bank_r[b], vecT_r], writes=[Et_r[e2]])
                else:
                    P.op("act", lambda h: h.activation(out=Et[e2][:], in_=banks[b][:], func=AF.Copy),
                         reads=[bank_r[b]], writes=[Et_r[e2]])
            if normed or rope:
                yield
            if normed:
                P.op("pe", lambda h: h.matmul(banks[b2][:], blockones, Et[e][:], start=True, stop=True),
                     reads=[Et_r[e]] + CR, writes=[bank_r[b2]])
            if rope:
                P.op("pe", lambda h: h.matmul(banks[b3][:], perm, Et[e2][:], start=True, stop=True),
                     reads=[Et_r[e2]] + CR, writes=[bank_r[b3]])
                if normed:
                    P.op("dve", lambda h: h.scalar_tensor_tensor(out=tmp[t1][:], in0=banks[b][:], scalar=gcol, in1=ropes[:, ci, ts],
                                                               op0=ALU.mult, op1=ALU.mult),
                         reads=[bank_r[b], vecT_r, ropes_r], writes=[tmp_r[t1]])
                else:
                    P.op("dve", lambda h: h.tensor_tensor(out=tmp[t1][:], in0=banks[b][:], in1=ropes[:, ci, ts], op=ALU.mult),
                         reads=[bank_r[b], ropes_r], writes=[tmp_r[t1]])
            if normed or rope:
                yield
            if normed:
                P.op("act", lambda h: h.activation(out=tmp[t0][:], in_=banks[b2][:], func=AF.Ln, scale=1.0 / HD, bias=col(cc, 2)),
                     reads=[bank_r[b2]] + CR, writes=[tmp_r[t0]])
                P.op("act", lambda h: h.activation(out=tmp[rs][:], in_=tmp[t0][:], func=AF.Exp, scale=-0.5),
                     reads=[tmp_r[t0]], writes=[tmp_r[rs]])
            if rope:
                P.op("dve", lambda h: h.tensor_tensor(out=tmp[t2][:], in0=banks[b3][:], in1=ropes[:, si, ts], op=ALU.mult),
                     reads=[bank_r[b3], ropes_r], writes=[tmp_r[t2]])
                P.op("dve", lambda h: h.tensor_tensor(out=tmp[t1][:], in0=tmp[t1][:], in1=tmp[t2][:], op=ALU.add),
                     reads=[tmp_r[t1], tmp_r[t2]], writes=[tmp_r[t1]])
            if normed:
                yield
                if rope:
                    P.op("dve", lambda h: h.tensor_tensor(out=tmp[t1][:], in0=tmp[t1][:], in1=tmp[rs][:], op=ALU.mult),
                         reads=[tmp_r[t1], tmp_r[rs]], writes=[tmp_r[t1]])
                else:
                    P.op("dve", lambda h: h.scalar_tensor_tensor(out=tmp[t1][:], in0=banks[b][:], scalar=gcol, in1=tmp[rs][:],
                                                               op0=ALU.mult, op1=ALU.mult),
                         reads=[bank_r[b], vecT_r, tmp_r[rs]], writes=[tmp_r[t1]])
            if normed or rope:
                fin = (tmp[t1], tmp_r[t1])
            elif need_f32:
                P.op("act", lambda h: h.activation(out=tmp[t4][:], in_=banks[b][:], func=AF.Copy), reads=[bank_r[b]], writes=[tmp_r[t4]])
                fin = (tmp[t4], tmp_r[t4])
            else:
                fin = (banks[b], bank_r[b])
            yield
            ft, fr = fin
            if not is_k:
                d = QT(j, tb)
                eng = alt()
                if eng == "act":
                    P.op("act", lambda h: h.activation(out=bufB[:, d, :], in_=ft[:], func=AF.Copy), reads=[fr], writes=[B_r[d]])
                else:
                    P.op("dve", lambda h: h.tensor_copy(out=bufB[:, d, :], in_=ft[:]), reads=[fr], writes=[B_r[d]])
            elif m == "b":
                for mp in range(2):
                    d = KTtok(mp * 2 + j, tb)
                    eng = "act" if mp == 0 else "dve"
                    if eng == "act":
                        P.op("act", lambda h, mp=mp, d=d: h.activation(out=bufB[:, d, :], in_=ft[:], func=AF.Copy, scale=col(cc, mp)),
                             reads=[fr] + CR, writes=[B_r[d]])
                    else:
                        P.op("dve", lambda h, mp=mp, d=d: h.tensor_scalar(out=bufB[:, d, :], in0=ft[:], scalar1=col(cc, mp), scalar2=None,
                                                                        op0=ALU.mult),
                             reads=[fr] + CR, writes=[B_r[d]])
            else:
                d = KTtok(j, tb)
                eng = alt()
                if eng == "act":
                    P.op("act", lambda h: h.activation(out=bufB[:, d, :], in_=ft[:], func=AF.Copy), reads=[fr], writes=[B_r[d]])
                else:
                    P.op("dve", lambda h: h.tensor_copy(out=bufB[:, d, :], in_=ft[:]), reads=[fr], writes=[B_r[d]])
            if need_f32:
                dup = m in "ac"
                wd = 64 if dup else 128
                for i in range(4):
                    if dup:
                        P.op("pe", lambda h, i=i: h.transpose(banks[b4][:, i * 64:(i + 1) * 64], ft[0:64, i * 128:(i + 1) * 128], ident[0:64, 0:64]),
                             reads=[fr] + CR, writes=[bank_r[b4]], signal=(i == 3))
                    else:
                        P.op("pe", lambda h, i=i: h.transpose(banks[b4][:, i * 128:(i + 1) * 128], ft[:, i * 128:(i + 1) * 128], ident[:]),
                             reads=[fr] + CR, writes=[bank_r[b4]], signal=(i == 3))
                yield
                ks = rr["k"] % NKST
                rr["k"] += 1
                P.op("dve", lambda h: h.tensor_copy(out=kst[ks][:, 0:4 * wd], in_=banks[b4][:, 0:4 * wd]), reads=[bank_r[b4]], writes=[kst_r[ks]])
                for i in range(4):
                    tt = tb * 4 + i
                    bb, s0 = tt // 2, (tt % 2) * 128
                    if dup:
                        dst = nk_out[m][bb, l, j, s0:s0 + 128, :]
                        src = kst[ks][:, i * 64:(i + 1) * 64]
                    else:
                        dst = nk_out[m][bb, l, 2 * j:2 * j + 2, s0:s0 + 128, :].rearrange("h s d -> s h d")
                        src = kst[ks][:, i * 128:(i + 1) * 128].rearrange("p (h d) -> p h d", h=2)
                    P.dma("sp", lambda h, dst=dst, src=src: h.dma_start(out=dst, in_=src), reads=[kst_r[ks]], semres=kst_r[ks])

        def qk_slot_plan(m, g):
            rope = (g == "s") and m in "abc"
            normed = (m == "a")
            nb = 1 + (1 if normed else 0) + (1 if rope else 0) + (1 if g == "p" else 0)
            nt = (2 if normed else 0) + (2 if rope else (1 if normed else 0)) + (1 if (g == "p" and not normed) else 0)
            ne = (1 if normed else 0) + (1 if rope else 0)
            ns = min(4, 8 // nb)
            if nt:
                ns = min(ns, NTMP // nt)
            if ne:
                ns = min(ns, NE // ne)
            return nb, nt, ne, ns

        def run_pipelined(factories, depth=2):
            active = {}
            todo = list(factories)
            while todo or active:
                for slot in range(depth):
                    if slot not in active and todo:
                        active[slot] = todo.pop(0)(slot)
                for slot in sorted(active):
                    try:
                        next(active[slot])
                    except StopIteration:
                        del active[slot]

        def project_load(m, l):
            mx = MIX[m]
            dup = m in "ac"
            vw = mx["vh"] * 64
            wl = wv(w_in, l)
            s1 = nslot()
            load_w(wl[:, :, mx["q"]:mx["q"] + 256], s1, cols=slice(0, 256))
            if dup:
                for jj in range(2):
                    for r in range(2):
                        c0 = 256 + jj * 128 + r * 64
                        load_w(wl[:, :, mx["k"] + jj * 64:mx["k"] + (jj + 1) * 64], s1, cols=slice(c0, c0 + 64))
            else:
                load_w(wl[:, :, mx["k"]:mx["k"] + 256], s1, cols=slice(256, 512))
            s2 = nslot()
            load_w(wl[:, :, mx["v"]:mx["v"] + vw], s2, cols=slice(0, vw))
            return s1, s2

        def project(m, l, g, pre=None):
            mx = MIX[m]
            dup = m in "ac"
            vw = mx["vh"] * 64
            s1, s2 = pre if pre is not None else project_load(m, l)
            if g == "s":
                ck, cv = cache[m]
                H = mx["kvh"]
                for c in range(2):
                    P.dma("pool", lambda h, c=c: h.dma_start(
                        out=V_ap(c, 0, H * 64).rearrange("p (h d) -> p h d", h=H),
                        in_=cv[l, :, c * 128:(c + 1) * 128, :].rearrange("h s d -> s h d")), writes=[V_r(c)])
                for c in range(2):
                    if dup:
                        for hh in range(4):
                            P.dma("sp", lambda h, c=c, hh=hh: h.dma_start(out=cst[:, c, hh, :], in_=ck[l, hh // 2, c * 128:(c + 1) * 128, :]),
                                  writes=[cst_r])
                    else:
                        P.dma("sp", lambda h, c=c: h.dma_start(out=cst[:, c, :, :], in_=ck[l, :, c * 128:(c + 1) * 128, :].rearrange("h s d -> s h d")),
                              writes=[cst_r])
                for j in range(2):
                    b = nbank()
                    for c in range(2):
                        P.op("pe", lambda h, c=c, j=j, b=b: h.transpose(
                            banks[b][:, c * 128:(c + 1) * 128], cst[:, c, 2 * j:2 * j + 2, :].rearrange("p h d -> p (h d)"), ident[:]),
                            reads=[cst_r] + CR, writes=[bank_r[b]], signal=(c == 1))
                    if m == "b":
                        for mp in range(2):
                            i = mp * 2 + j
                            P.op("dve", lambda h, i=i, mp=mp, b=b: h.tensor_scalar(out=KTctx_ap(i, 0, 128, 0, 256), in0=banks[b][:, 0:256],
                                                                                 scalar1=col(cc, mp), scalar2=None, op0=ALU.mult),
                                 reads=[bank_r[b]] + CR, writes=[KTctx_r(i)])
                    else:
                        P.op("dve", lambda h, j=j, b=b: h.tensor_copy(out=KTctx_ap(j, 0, 128, 0, 256), in_=banks[b][:, 0:256]),
                             reads=[bank_r[b]], writes=[KTctx_r(j)])
            for tt in range(8):
                b = nbank()
                for k in range(8):
                    P.op("pe", lambda h, k=k, tt=tt, b=b: h.matmul(
                        banks[b][:, 0:vw], hT[:, k, tt * 128:(tt + 1) * 128], wslot[s2][:, k, 0:vw], start=(k == 0), stop=(k == 7)),
                        reads=[wslot_r[s2], hT_r[k][tt // 4]], writes=[bank_r[b]], signal=(k == 7))
                vt = 2 + tt
                P.op("dve", lambda h, vt=vt, b=b: h.tensor_copy(out=V_ap(vt, 0, vw), in_=banks[b][:, 0:vw]), reads=[bank_r[b]], writes=[V_r(vt)])
                if g == "p":
                    vs = rr["v"] % NVST
                    rr["v"] += 1
                    P.op("act", lambda h, vs=vs, b=b: h.activation(out=vst[vs][:, 0:vw], in_=banks[b][:, 0:vw], func=AF.Copy),
                         reads=[bank_r[b]], writes=[vst_r[vs]])
                    bb, s0 = tt // 2, (tt % 2) * 128
                    H = mx["vh"]
                    P.dma("sp", lambda h, vs=vs, bb=bb, s0=s0, H=H: h.dma_start(
                        out=nv_out[m][bb, l, :, s0:s0 + 128, :].rearrange("h s d -> s h d"),
                        in_=vst[vs][:, 0:H * 64].rearrange("p (h d) -> p h d", h=H)),
                        reads=[vst_r[vs]], semres=vst_r[vs])
            gens = []
            cnt_ = 0
            for kind, cbase in (("q", 0), ("k", 256)):
                for j in range(2):
                    for tb in range(2):
                        gens.append(lambda slot, kind=kind, j=j, tb=tb, cbase=cbase: post_qk(m, l, g, kind, j, tb, slot, s1, cbase))
            run_pipelined(gens, qk_slot_plan(m, g)[3])

        def recip_ap(d_ap, d_reads, nq, bias_ap, bias_reads):
            t1, t2 = nT(), nT()
            if nq <= 256:
                if bias_ap is None:
                    P.op("dve", lambda h: h.reciprocal(out=tmp[t2][:, 0:nq], in_=d_ap), reads=d_reads, writes=[tmp_r[t2]])
                else:
                    P.op("dve", lambda h: h.tensor_scalar(out=tmp[t1][:, 0:nq], in0=d_ap, scalar1=bias_ap, scalar2=None, op0=ALU.add),
                         reads=d_reads + bias_reads, writes=[tmp_r[t1]])
                    P.op("dve", lambda h: h.reciprocal(out=tmp[t2][:, 0:nq], in_=tmp[t1][:, 0:nq]), reads=[tmp_r[t1]], writes=[tmp_r[t2]])
                return t2
            if USE_APPROX_RECIP:
                if bias_ap is None:
                    P.op("dve", lambda h: h.tensor_copy(out=tmp[t1][:, 0:nq], in_=d_ap), reads=d_reads, writes=[tmp_r[t1]])
                else:
                    P.op("dve", lambda h: h.tensor_scalar(out=tmp[t1][:, 0:nq], in0=d_ap, scalar1=bias_ap, scalar2=None, op0=ALU.add),
                         reads=d_reads + bias_reads, writes=[tmp_r[t1]])
                P.op("dve", lambda h: h.reciprocal_approx_fast(out=tmp[t2][:, 0:nq], in_=tmp[t1][:, 0:nq]), reads=[tmp_r[t1]], writes=[tmp_r[t2]])
                return t2
            if bias_ap is None:
                P.op("act", lambda h: h.activation(out=tmp[t1][:, 0:nq], in_=d_ap, func=AF.Ln), reads=d_reads, writes=[tmp_r[t1]])
            else:
                P.op("act", lambda h: h.activation(out=tmp[t1][:, 0:nq], in_=d_ap, func=AF.Ln, bias=bias_ap),
                     reads=d_reads + bias_reads, writes=[tmp_r[t1]])
            P.op("act", lambda h: h.activation(out=tmp[t2][:, 0:nq], in_=tmp[t1][:, 0:nq], func=AF.Exp, scale=-1.0),
                 reads=[tmp_r[t1]], writes=[tmp_r[t2]])
            return t2

        def run_blocks(blocks, scale, prompt):
            stream = [(bi, ui) for bi, blk in enumerate(blocks) for ui in range(len(blk["units"]))]
            for bi, blk in enumerate(blocks):
                if prompt:
                    blk["ob"] = blk["db"] = 6 + bi % 2
                    blk["oc"], blk["dcol"] = 0, 256
                else:
                    blk["ob"], blk["db"] = 6, 7
                    blk["oc"], blk["dcol"] = 0, 0
            sbk = {}
            PD = 2

            def QK(idx):
                bi, ui = stream[idx]
                blk = blocks[bi]
                U = blk["units"][ui]
                qtile, qc0 = blk["qtile"], blk["qc0"]
                lo, hi = U["lo"], U["hi"]
                pair = []
                for hh in range(2):
                    b = nbank("S")
                    pair.append(b)
                    kap, kr = U["k%d" % hh]
                    ex = U.get("extra%d" % hh, [])
                    P.op("pe", lambda h, b=b, kap=kap, hh=hh: h.matmul(
                        banks[b][:, lo:hi], kap, bufB[hh * 64:(hh + 1) * 64, qtile, qc0 + lo:qc0 + hi],
                        start=True, stop=(len(ex) == 0), skip_group_check=True),
                        reads=[kr, B_r[qtile]], writes=[bank_r[b]], signal=(len(ex) == 0))
                    for xi, (lt, rh, c0, c1, rds) in enumerate(ex):
                        last = xi == len(ex) - 1
                        P.op("pe", lambda h, b=b, lt=lt, rh=rh, c0=c0, c1=c1, last=last: h.matmul(
                            banks[b][:, c0:c1], lt, rh, start=False, stop=last, skip_group_check=True),
                            reads=rds, writes=[bank_r[b]], signal=last)
                sbk[idx] = pair

            for idx in range(min(PD, len(stream))):
                QK(idx)
            pending_fin = []
            for idx, (bi, ui) in enumerate(stream):
                if idx + PD < len(stream):
                    QK(idx + PD)
                blk = blocks[bi]
                U = blk["units"][ui]
                n = len(blk["units"])
                lo, hi = U["lo"], U["hi"]
                ob, db, oc, dcol = blk["ob"], blk["db"], blk["oc"], blk["dcol"]
                es = []
                for hh in range(2):
                    e = nE()
                    es.append(e)
                    b = sbk[idx][hh]
                    P.op("act", lambda h, e=e, b=b: h.activation(out=Et[e][:, lo:hi], in_=banks[b][:, lo:hi], func=AF.Exp, scale=scale),
                         reads=[bank_r[b]], writes=[Et_r[e]])
                del sbk[idx]
                while pending_fin:
                    pending_fin.pop(0)()
                for hh in range(2):
                    e = es[hh]
                    vap, vr = U["v%d" % hh]
                    P.op("pe", lambda h, e=e, vap=vap, hh=hh: h.matmul(
                        banks[ob][hh * 64:(hh + 1) * 64, oc + lo:oc + hi], vap, Et[e][:, lo:hi], start=(ui == 0), stop=(ui == n - 1),
                        skip_group_check=True),
                        reads=[vr, Et_r[es[0]], Et_r[es[1]], bank_r[db]], writes=[bank_r[ob]], signal=False)
                dstart = (ui == 0) and (db != ob)
                for hh in range(2):
                    e = es[hh]
                    P.op("pe", lambda h, e=e, hh=hh: h.matmul(
                        banks[db][hh * 64:(hh + 1) * 64, dcol + lo:dcol + hi], ones_b[:, 0:64], Et[e][:, lo:hi], start=dstart, stop=(ui == n - 1),
                        skip_group_check=True),
                        reads=[Et_r[e]] + CR, writes=[bank_r[db]], signal=(hh == 1))
                if ui == n - 1:
                    nq = blk["nq"]
                    rds = [bank_r[ob]] if ob == db else [bank_r[ob], bank_r[db]]
                    pending_fin.append(lambda blk=blk, ob=ob, db=db, oc=oc, dcol=dcol, nq=nq: blk["fin"](
                        banks[ob][:, oc:oc + nq], banks[db][:, dcol:dcol + nq], [bank_r[ob]], [bank_r[db]], db))
            while pending_fin:
                pending_fin.pop(0)()

        def attention(m, l, g):
            mx = MIX[m]
            mi = mx["idx"]
            dup = m in "ac"
            scale = (32.0 ** -0.5) if m == "b" else (64.0 ** -0.5)
            if g == "s":
                qblocks = [(tb, 0, 512, tb * 512) for tb in range(2)]
            else:
                qblocks = [(bb // 2, (bb % 2) * 256, 256, bb * 256) for bb in range(4)]
            blocks = []
            for j in range(2):
                for qi, (tb, qc0, nq, tok0) in enumerate(qblocks):
                    qtile = QT(j, tb)
                    held = {}
                    for mp in range(2 if m == "b" else 1):
                        ki = (mp * 2 + j) if m == "b" else j
                        units = []

                        def vpair(vt):
                            if dup:
                                a0 = (V_ap(vt, j * 64, 64), V_r(vt))
                                return a0, a0
                            return (V_ap(vt, (2 * j) * 64, 64), V_r(vt)), (V_ap(vt, (2 * j + 1) * 64, 64), V_r(vt))

                        def ktok(tt):
                            d = KTtok(ki, tt // 4)
                            c = (tt % 4) * 128
                            return (bufB[0:64, d, c:c + 128], B_r[d]), (bufB[64:128, d, c:c + 128], B_r[d])

                        if g == "s":
                            for c in range(2):
                                v0, v1 = vpair(c)
                                units.append(dict(k0=(KTctx_ap(ki, 0, 64, c * 128, 128), KTctx_r(ki)),
                                                  k1=(KTctx_ap(ki, 64, 64, c * 128, 128), KTctx_r(ki)), v0=v0, v1=v1, lo=0, hi=512))
                            if m in "ab":
                                for tt in range(8):
                                    k0, k1 = ktok(tt)
                                    v0, v1 = vpair(2 + tt)
                                    units.append(dict(k0=k0, k1=k1, v0=v0, v1=v1, lo=0, hi=512))
                            elif m == "c":
                                for tt in range(8):
                                    blo, bhi = max(tt - 1, 4 * tb), min(tt + 1, 4 * tb + 3)
                                    if blo > bhi:
                                        continue
                                    k0, k1 = ktok(tt)
                                    v0, v1 = vpair(2 + tt)
                                    lo, hi = (blo - 4 * tb) * 128, (bhi + 1 - 4 * tb) * 128
                                    ex = []
                                    if blo <= tt - 1 <= bhi:
                                        c0 = (tt - 1 - 4 * tb) * 128
                                        ex.append((identb, triA, c0, c0 + 128, CR))
                                    if blo <= tt + 1 <= bhi:
                                        c0 = (tt + 1 - 4 * tb) * 128
                                        ex.append((identb, triB, c0, c0 + 128, CR))
                                    units.append(dict(k0=k0, k1=k1, v0=v0, v1=v1, lo=lo, hi=hi, extra0=ex, extra1=ex))
                            else:
                                qt = tb
                                for kt in (range(0, 6) if qt == 0 else range(2, 8)):
                                    k0, k1 = ktok(kt)
                                    v0, v1 = vpair(2 + kt)
                                    j0 = 2 * kt - 8 * qt + 12
                                    rmask = AP(small, 128 + kt * 16 + qt * 8, [[256, 2], [1, 8], [0, 64]])
                                    exs = []
                                    for hh in range(2):
                                        hd = 2 * j + hh
                                        bias = AP(SBh[hd], j0 * 64, [[1600, 128], [-64, 8], [1, 64]])
                                        exs.append([(identb, bias, 0, 512, CR + [SBh_r[hd]]), (oh2, rmask, 0, 512, CR)])
                                    units.append(dict(k0=k0, k1=k1, v0=v0, v1=v1, lo=0, hi=512, extra0=exs[0], extra1=exs[1]))
                        else:
                            bb = qi
                            for tt in (2 * bb, 2 * bb + 1):
                                k0, k1 = ktok(tt)
                                v0, v1 = vpair(2 + tt)
                                units.append(dict(k0=k0, k1=k1, v0=v0, v1=v1, lo=0, hi=nq))
                        chunk = mi * 2 + j
                        dst = bufA[:, chunk, tok0:tok0 + nq]
                        dres = A_r[chunk][tok0 // 512]

                        def fin(o_ap, d_ap, o_rd, d_rd, dbank, mp=mp, j=j, nq=nq, dst=dst, dres=dres, held=held):
                            if m != "b":
                                if m == "c":
                                    rt = recip_ap(d_ap, d_rd, nq, col(lsm, 4 + l * 2 + j), [lsm_r])
                                else:
                                    rt = recip_ap(d_ap, d_rd, nq, None, [])
                                P.op("dve", lambda h: h.tensor_tensor(out=dst, in0=o_ap, in1=tmp[rt][:, 0:nq], op=ALU.mult),
                                     reads=o_rd + [tmp_r[rt]], writes=[dres])
                                return
                            rt = recip_ap(d_ap, d_rd, nq, None, [])
                            if mp == 0:
                                t3 = nT()
                                P.op("dve", lambda h: h.tensor_tensor(out=tmp[t3][:, 0:nq], in0=o_ap, in1=tmp[rt][:, 0:nq], op=ALU.mult),
                                     reads=o_rd + [tmp_r[rt]], writes=[tmp_r[t3]])
                                held["t3"] = t3
                                return
                            t3 = held["t3"]
                            t4 = nT()
                            P.op("dve", lambda h: h.scalar_tensor_tensor(
                                out=tmp[t4][:, 0:nq], in0=o_ap, scalar=col(lsm, l), in1=tmp[rt][:, 0:nq], op0=ALU.mult, op1=ALU.mult),
                                reads=o_rd + [tmp_r[rt], lsm_r], writes=[tmp_r[t4]])
                            P.op("dve", lambda h: h.tensor_tensor(out=tmp[t3][:, 0:nq], in0=tmp[t3][:, 0:nq], in1=tmp[t4][:, 0:nq], op=ALU.add),
                                 reads=[tmp_r[t3], tmp_r[t4]], writes=[tmp_r[t3]])
                            e = nE()
                            P.op("act", lambda h: h.activation(out=Et[e][:, 0:nq], in_=tmp[t3][:, 0:nq], func=AF.Square),
                                 reads=[tmp_r[t3]], writes=[Et_r[e]])
                            b2 = dbank
                            P.op("pe", lambda h: h.matmul(banks[b2][:, 0:nq], blockones, Et[e][:, 0:nq], start=True, stop=True),
                                 reads=[Et_r[e]] + CR, writes=[bank_r[b2]])
                            t5, t6 = nT(), nT()
                            P.op("act", lambda h: h.activation(out=tmp[t5][:, 0:nq], in_=banks[b2][:, 0:nq], func=AF.Ln,
                                                               scale=1.0 / HD, bias=col(cc, 3)),
                                 reads=[bank_r[b2]] + CR, writes=[tmp_r[t5]])
                            P.op("act", lambda h: h.activation(out=tmp[t6][:, 0:nq], in_=tmp[t5][:, 0:nq], func=AF.Exp, scale=-0.5),
                                 reads=[tmp_r[t5]], writes=[tmp_r[t6]])
                            P.op("dve", lambda h: h.scalar_tensor_tensor(
                                out=dst, in0=tmp[t3][:, 0:nq], scalar=col(lsm, 2 + l), in1=tmp[t6][:, 0:nq], op0=ALU.mult, op1=ALU.mult),
                                reads=[tmp_r[t3], tmp_r[t6], lsm_r], writes=[dres])

                        blocks.append(dict(qtile=qtile, qc0=qc0, nq=nq, units=units, fin=fin))
            run_blocks(blocks, scale, g == "p")

        def merge_wout(l, g, c, after_tb=None, pre_hook=None):
            wl = wv(w_in, l)
            wb = w_branch[l].rearrange("(k p) n -> p k n", p=128)
            sbr = [0, 1]
            for q in range(2):
                load_w(wb[:, :, q * 512:(q + 1) * 512], sbr[q])
            for dc in range(8):
                sg_ = 2 + dc % 2
                for br in range(4):
                    c0 = GATE0 + br * 1024 + dc * 128
                    load_w(wl[:, :, c0:c0 + 128], sg_, cols=slice(br * 128, (br + 1) * 128))
                for tb in range(2):
                    ts = slice(tb * 512, (tb + 1) * 512)
                    acc = nT()
                    for br in range(4):
                        bg = nbank()
                        for k in range(8):
                            P.op("pe", lambda h, k=k, br=br, bg=bg: h.matmul(banks[bg][:], wslot[sg_][:, k, br * 128:(br + 1) * 128], hT[:, k, ts],
                                                                            start=(k == 0), stop=(k == 7)),
                                 reads=[wslot_r[sg_], hT_r[k][tb]], writes=[bank_r[bg]], signal=(k == 7))
                        bp = nbank()
                        for kc in range(2):
                            ch = br * 2 + kc
                            P.op("pe", lambda h, kc=kc, ch=ch, bp=bp: h.matmul(
                                banks[bp][:], wslot[sbr[dc // 4]][:, ch, (dc % 4) * 128:(dc % 4 + 1) * 128], bufA[:, ch, ts],
                                start=(kc == 0), stop=(kc == 1)),
                                reads=[wslot_r[sbr[dc // 4]], A_r[ch][tb]], writes=[bank_r[bp]], signal=(kc == 1))
                        tg = nT()
                        P.op("act", lambda h, tg=tg, bg=bg: h.activation(out=tmp[tg][:], in_=banks[bg][:], func=AF.Sigmoid),
                             reads=[bank_r[bg]], writes=[tmp_r[tg]])
                        if br == 0:
                            P.op("dve", lambda h, tg=tg, bp=bp, acc=acc: h.tensor_tensor(out=tmp[acc][:], in0=banks[bp][:], in1=tmp[tg][:], op=ALU.mult),
                                 reads=[bank_r[bp], tmp_r[tg]], writes=[tmp_r[acc]])
                        else:
                            P.op("dve", lambda h, tg=tg, bp=bp: h.tensor_tensor(out=tmp[tg][:], in0=banks[bp][:], in1=tmp[tg][:], op=ALU.mult),
                                 reads=[bank_r[bp], tmp_r[tg]], writes=[tmp_r[tg]])
                            if br < 3:
                                P.op("dve", lambda h, tg=tg, acc=acc: h.tensor_tensor(out=tmp[acc][:], in0=tmp[acc][:], in1=tmp[tg][:], op=ALU.add),
                                     reads=[tmp_r[acc], tmp_r[tg]], writes=[tmp_r[acc]])
                            else:
                                d = dc * 2 + tb
                                P.op("dve", lambda h, tg=tg, acc=acc, d=d: h.tensor_tensor(out=bufB[:, d, :], in0=tmp[acc][:], in1=tmp[tg][:], op=ALU.add),
                                     reads=[tmp_r[acc], tmp_r[tg]], writes=[B_r[d]])
            wo = wv(w_out, l)
            wslots = []
            for q in range(2):
                s = nslot()
                load_w(wo[:, :, q * 512:(q + 1) * 512], s)
                wslots.append(s)
            if pre_hook is not None:
                pre_hook()
            for tb in range(2):
                for q in range(2):
                    s = wslots[q]
                    for dcl in range(4):
                        dc = q * 4 + dcl
                        ts = slice(tb * 512, (tb + 1) * 512)
                        b = nbank()
                        for k in range(8):
                            P.op("pe", lambda h, k=k, b=b, s=s, dcl=dcl, tb=tb: h.matmul(
                                banks[b][:], wslot[s][:, k, dcl * 128:(dcl + 1) * 128], bufB[:, k * 2 + tb, :], start=(k == 0), stop=(k == 7)),
                                reads=[wslot_r[s], B_r[k * 2 + tb]], writes=[bank_r[b]], signal=(k == 7))
                        P.op("dve", lambda h, b=b, dc=dc, ts=ts: h.scalar_tensor_tensor(
                            out=xT[:, dc, ts], in0=banks[b][:], scalar=modcol(l, 2, dc, c), in1=xT[:, dc, ts], op0=ALU.mult, op1=ALU.add),
                            reads=[bank_r[b], mod_r, xT_r[dc][tb]], writes=[xT_r[dc][tb]])
                if after_tb is not None:
                    after_tb(tb)

        def ffn_load(l, jb, w_):
            wi = wv(w_ffn_in, l)
            sa, su = nslot(), nslot()
            load_w(wi[:, :, jb * 512:jb * 512 + w_], sa, cols=slice(0, w_))
            load_w(wi[:, :, DFF + jb * 512:DFF + jb * 512 + w_], su, cols=slice(0, w_))
            return sa, su

        def ffn(l, g, c, pre=None):
            wi = wv(w_ffn_in, l)
            wo = w_ffn_out[l].rearrange("(f p) n -> p f n", p=128)
            groups_ = [(0, [(0, 512), (1, 512)]), (8, [(2, 512), (3, 512)]), (16, [(4, 512), (5, 256)])]
            for (f0, blocks) in groups_:
                nF = sum(w_ // 128 for _, w_ in blocks)
                for (jb, w_) in blocks:
                    if jb == 0 and pre is not None:
                        sa, su = pre
                    else:
                        sa, su = ffn_load(l, jb, w_)
                    for fl in range(w_ // 128):
                        fi = jb * 4 + fl - f0
                        for tb in range(2):
                            ts = slice(tb * 512, (tb + 1) * 512)
                            ba, bu = nbank(), nbank()
                            for (bk, sl) in ((ba, sa), (bu, su)):
                                for k in range(8):
                                    P.op("pe", lambda h, k=k, bk=bk, sl=sl, fl=fl, ts=ts: h.matmul(
                                        banks[bk][:], wslot[sl][:, k, fl * 128:(fl + 1) * 128], hT[:, k, ts], start=(k == 0), stop=(k == 7)),
                                        reads=[wslot_r[sl], hT_r[k][tb]], writes=[bank_r[bk]], signal=(k == 7))
                            t = nT()
                            P.op("act", lambda h, t=t, ba=ba: h.activation(out=tmp[t][:], in_=banks[ba][:], func=AF.Silu),
                                 reads=[bank_r[ba]], writes=[tmp_r[t]])
                            P.op("dve", lambda h, t=t, bu=bu, fi=fi, ts=ts: h.tensor_tensor(out=bufA[:, fi, ts], in0=banks[bu][:], in1=tmp[t][:], op=ALU.mult),
                                 reads=[bank_r[bu], tmp_r[t]], writes=[A_r[fi][tb]])
                for q in range(2):
                    s = nslot()
                    load_w(wo[:, f0:f0 + nF, q * 512:(q + 1) * 512], s, kk=slice(0, nF))
                    for dcl in range(4):
                        dc = q * 4 + dcl
                        for tb in range(2):
                            ts = slice(tb * 512, (tb + 1) * 512)
                            b = nbank()
                            for fi in range(nF):
                                P.op("pe", lambda h, fi=fi, b=b, s=s, dcl=dcl, ts=ts: h.matmul(
                                    banks[b][:], wslot[s][:, fi, dcl * 128:(dcl + 1) * 128], bufA[:, fi, ts], start=(fi == 0), stop=(fi == nF - 1)),
                                    reads=[wslot_r[s], A_r[fi][tb]], writes=[bank_r[b]], signal=(fi == nF - 1))
                            P.op("dve", lambda h, b=b, dc=dc, ts=ts: h.scalar_tensor_tensor(
                                out=xT[:, dc, ts], in0=banks[b][:], scalar=modcol(l, 5, dc, c), in1=xT[:, dc, ts], op0=ALU.mult, op1=ALU.add),
                                reads=[bank_r[b], mod_r, xT_r[dc][tb]], writes=[xT_r[dc][tb]])

        for g in groups:
            c = 0 if g == "s" else 1
            if not ada_started[0]:
                ada_issue(4)
            for tt in range(8 if "load" not in skip else 0):
                xs = rr["xs"] % 2
                rr["xs"] += 1
                P.dma("sp", lambda h, xs=xs, tt=tt: h.dma_start(out=xstage[xs][:], in_=x_in[g][tt * 128:(tt + 1) * 128, :]), writes=[xstage_r[xs]])
                for kh in range(2):
                    b = nbank()
                    for i in range(4):
                        k = kh * 4 + i
                        P.op("pe", lambda h, xs=xs, k=k, i=i, b=b: h.transpose(banks[b][:, i * 128:(i + 1) * 128], xstage[xs][:, k * 128:(k + 1) * 128], ident[:]),
                             reads=[xstage_r[xs]] + CR, writes=[bank_r[b]], signal=(i == 3))
                    eng = alt()
                    wr = [xT_r[kh * 4 + i][tt // 4] for i in range(4)]
                    dst = xT[:, kh * 4:(kh + 1) * 4, tt * 128:(tt + 1) * 128]
                    src = banks[b][:].rearrange("p (i t) -> p i t", i=4)
                    if eng == "act":
                        P.op("act", lambda h, dst=dst, src=src: h.activation(out=dst, in_=src, func=AF.Copy), reads=[bank_r[b]], writes=wr)
                    else:
                        P.op("dve", lambda h, dst=dst, src=src: h.tensor_copy(out=dst, in_=src), reads=[bank_r[b]], writes=wr)
            ada_compute()
            dump("xT", xT[:].rearrange("p k t -> p (k t)"), [r_ for rr_ in xT_r for r_ in rr_])
            for l in range(n_layers):
                pre_a = project_load("a", l) if "a" not in skip else None
                if "norm" not in skip:
                    norm_mod(l, c, 0)
                dump("hT", hT[:].rearrange("p k t -> p (k t)"), [r_ for rr_ in hT_r for r_ in rr_], q="pool")
                for m in "abcd":
                    if m in skip:
                        continue
                    project(m, l, g, pre_a if m == "a" else None)
                    ada_some(2 if m != "d" else 0)
                    if m == "a" and g == "s" and "d" not in skip:
                        build_SB(l)
                    attention(m, l, g)
                    ada_some(2 if m != "d" else 0)
                if l == 0:
                    while any(b_[0] == 0 for b_ in ada_pending):
                        ada_issue(2)
                        ada_compute()
                dump("brT", bufA[:].rearrange("p k t -> p (k t)"), [r_ for rr_ in A_r for r_ in rr_], q="pool")
                pre = {}
                fused_norm2 = ("merge" not in skip) and ("norm" not in skip) and not dbg
                if "merge" not in skip:
                    if fused_norm2:
                        merge_wout(l, g, c, after_tb=lambda tb: norm_mod(l, c, 1, tbs=(tb,)),
                                   pre_hook=(lambda: pre.__setitem__("f", ffn_load(l, 0, 512))) if "ffn" not in skip else None)
                    else:
                        merge_wout(l, g, c)
                dump("merged", bufB[:, 0:16, :].rearrange("p k t -> p (k t)"), B_r[0:16], q="pool")
                dump("x1", xT[:].rearrange("p k t -> p (k t)"), [r_ for rr_ in xT_r for r_ in rr_])
                if fused_norm2:
                    pre_f = pre.get("f")
                else:
                    pre_f = ffn_load(l, 0, 512) if "ffn" not in skip else None
                    if "norm" not in skip:
                        norm_mod(l, c, 1)
                if "ffn" not in skip:
                    ffn(l, g, c, pre_f)
            for tt in range(8 if "final" not in skip else 0):
                xs = rr["xs"] % 2
                rr["xs"] += 1
                bs = []
                for kh in range(2):
                    b = nbank()
                    bs.append(b)
                    for i in range(4):
                        k = kh * 4 + i
                        P.op("pe", lambda h, k=k, i=i, b=b, tt=tt: h.transpose(banks[b][:, i * 128:(i + 1) * 128], xT[:, k, tt * 128:(tt + 1) * 128], ident[:]),
                             reads=[xT_r[k][tt // 4]] + CR, writes=[bank_r[b]], signal=(i == 3))
                    t = nT()
                    P.op("act", lambda h, b=b, t=t, kh=kh: h.activation(out=tmp[t][:], in_=banks[b][:], func=AF.Square, accum_out=ss_col[:, kh:kh + 1]),
                         reads=[bank_r[b]], writes=[tmp_r[t], ss_col_r])
                P.op("dve", lambda h: h.tensor_tensor(out=ss_col[:, 2:3], in0=ss_col[:, 0:1], in1=ss_col[:, 1:2], op=ALU.add),
                     reads=[ss_col_r], writes=[ss_col_r])
                P.op("act", lambda h: h.activation(out=ss_col[:, 3:4], in_=ss_col[:, 2:3], func=AF.Ln, scale=1.0 / D, bias=col(cc, 2)),
                     reads=[ss_col_r] + CR, writes=[ss_col_r])
                P.op("act", lambda h: h.activation(out=ss_col[:, 4:5], in_=ss_col[:, 3:4], func=AF.Exp, scale=-0.5), reads=[ss_col_r], writes=[ss_col_r])
                for kh in range(2):
                    b = bs[kh]
                    P.op("dve", lambda h, b=b, kh=kh, xs=xs: h.scalar_tensor_tensor(
                        out=xstage[xs][:, kh * 512:(kh + 1) * 512], in0=banks[b][:], scalar=ss_col[:, 4:5], in1=gfin_b[:, kh * 512:(kh + 1) * 512],
                        op0=ALU.mult, op1=ALU.mult),
                        reads=[bank_r[b], ss_col_r, gfin_r], writes=[xstage_r[xs]])
                P.dma("sp", lambda h, xs=xs, tt=tt: h.dma_start(out=y_out[g][tt * 128:(tt + 1) * 128, :], in_=xstage[xs][:]),
                      reads=[xstage_r[xs]], semres=xstage_r[xs])

        dump("lsm", lsm[:], [lsm_r])
        all_res = out_res + xstage_r + kst_r + vst_r
        P.finish(all_res)
        P.emit(nc, st)
    return nc


_NC_CACHE = {}


def _get_program():
    if "nc" not in _NC_CACHE:
        _NC_CACHE["nc"] = build_program()
    return _NC_CACHE["nc"]


def make_in_maps(x_prompt, x_sample, cache_a_k, cache_a_v, cache_b_k, cache_b_v, cache_c_k, cache_c_v,
                 cache_d_k, cache_d_v, c, c_ctx, w_ada, b_ada, g_norm1, w_in, g_q_a, g_k_a, lam_b, g_subln_b,
                 sink_c, rpb_d, w_branch, w_out, g_norm2, w_ffn_in, w_ffn_out, g_final):
    f = lambda a: np.ascontiguousarray(np.asarray(a, dtype=np.float32))
    consts = _consts()
    vecs = np.zeros((256, 128), np.float32)
    b_ada_, g1, g2, gf = f(b_ada), f(g_norm1), f(g_norm2), f(g_final)
    vecs[0:48] = b_ada_[0].reshape(48, 128)
    vecs[48:96] = b_ada_[1].reshape(48, 128)
    vecs[96:104] = g1[0].reshape(8, 128)
    vecs[104:112] = g1[1].reshape(8, 128)
    vecs[112:120] = g2[0].reshape(8, 128)
    vecs[120:128] = g2[1].reshape(8, 128)
    vecs[128:136] = gf.reshape(8, 128)
    gq, gk, gs_, sk = f(g_q_a), f(g_k_a), f(g_subln_b), f(sink_c)
    for l in range(DEPTH):
        vecs[136 + l] = np.tile(gq[l], 2)
        vecs[138 + l] = np.tile(gk[l], 2)
        vecs[140 + l] = np.tile(gs_[l], 2)
        for j in range(2):
            vecs[142 + l * 2 + j] = np.repeat(sk[l, 2 * j:2 * j + 2], 64)
    lamb = f(lam_b).reshape(1, 256)
    rpbx = _rpb_expand(f(rpb_d))
    shared = dict(
        w_ada=f(w_ada), w_in=f(w_in), w_branch=f(w_branch).reshape(DEPTH, 1024, D), w_out=f(w_out),
        w_ffn_in=f(w_ffn_in), w_ffn_out=f(w_ffn_out), vecs=vecs, lamb=lamb, rpbx=rpbx, **consts)
    xp, xs, c_, cctx = f(x_prompt), f(x_sample), f(c), f(c_ctx)
    caches = dict(a=(f(cache_a_k), f(cache_a_v)), b=(f(cache_b_k), f(cache_b_v)), c=(f(cache_c_k), f(cache_c_v)), d=(f(cache_d_k), f(cache_d_v)))
    in_maps = []
    for i in range(N_CORES):
        mp = dict(shared)
        mp["xs"] = xs[i]
        mp["xp"] = np.ascontiguousarray(xp[4 * i:4 * i + 4].reshape(NT, D))
        mp["cond"] = np.ascontiguousarray(np.stack([c_[i], cctx]).reshape(16, 128))
        for m in "abcd":
            mp["ck_" + m] = np.ascontiguousarray(caches[m][0][i])
            mp["cv_" + m] = np.ascontiguousarray(caches[m][1][i])
        in_maps.append(mp)
    return in_maps


def kernel(**inputs):
    nc = _get_program()
    in_maps = make_in_maps(**inputs)
    res = run_bass_kernel_spmd(nc, in_maps, core_ids=list(range(N_CORES)))
    R = res.results
    y_prompt = np.concatenate([r["yp"].reshape(4, 256, D) for r in R], axis=0).astype(np.float32)
    y_sample = np.stack([r["ys"] for r in R], axis=0).astype(np.float32)
    outs = [y_prompt, y_sample]
    for m in "abcd":
        outs.append(np.concatenate([r["nk_" + m] for r in R], axis=0).astype(np.float32))
        outs.append(np.concatenate([r["nv_" + m] for r in R], axis=0).astype(np.float32))
    return tuple(outs)
```
